# Optimizing a Trainium2 kernel written in Bass

```python
import jax, jax.numpy as jnp
from jax import lax
import numpy as np

D_MODEL = 1024
BATCH = 4
SEQ = 4096
DEPTH = 2
DEC_BATCH = 128
DEC_SEQ = 4
PAST_LEN = 2048
PAGE_SIZE = 128

BRANCH_W = D_MODEL // 2
N_BRANCH = 4
N_PARTS = 13
IN_W = N_PARTS * BRANCH_W
CHUNK = 128
A_GROUPS = 4
A_GW = BRANCH_W // A_GROUPS
CONV_W = 3
C_HEADS = 8
C_HD = BRANCH_W // C_HEADS
ROPE_DIM = C_HD // 4
ROPE_THETA = 500000.0
MOBA_BLOCK = 256
MOBA_TOPK = 3
MOBA_Q_BLOCK = 64
N_MEM = 256
M_HEADS = 4
M_HD = BRANCH_W // M_HEADS
EPS = 1e-6

kernel_name = 'hybrid_gated_branch_decoder_step'


def _rmsnorm(x, g):
    xf = x.astype(jnp.float32)
    r = lax.rsqrt(jnp.mean(xf * xf, axis=-1, keepdims=True) + EPS)
    return (xf * r).astype(x.dtype) * g


def _layernorm(x, g):
    xf = x.astype(jnp.float32)
    mu = jnp.mean(xf, axis=-1, keepdims=True)
    xc = xf - mu
    r = lax.rsqrt(jnp.mean(xc * xc, axis=-1, keepdims=True) + EPS)
    return (xc * r).astype(x.dtype) * g


def _rope(x, pos):
    half = ROPE_DIM // 2
    inv = jnp.power(jnp.float32(ROPE_THETA), -jnp.arange(half, dtype=jnp.float32) * (2.0 / ROPE_DIM))
    ang = pos.astype(jnp.float32)[:, None] * inv[None, :]
    cos = jnp.cos(ang)[None, :, None, :]
    sin = jnp.sin(ang)[None, :, None, :]
    x1 = x[..., :half].astype(jnp.float32)
    x2 = x[..., half:ROPE_DIM].astype(jnp.float32)
    rot = jnp.concatenate([x1 * cos - x2 * sin, x2 * cos + x1 * sin], axis=-1).astype(x.dtype)
    return jnp.concatenate([rot, x[..., ROPE_DIM:]], axis=-1)


def _moba(q, k, v, q_pos):
    B, L, H, d = k.shape
    T = q.shape[1]
    nb = -(-L // MOBA_BLOCK)
    pad = nb * MOBA_BLOCK - L
    kb = jnp.pad(k, ((0, 0), (0, pad), (0, 0), (0, 0))).reshape(B, nb, MOBA_BLOCK, H, d)
    vb = jnp.pad(v, ((0, 0), (0, pad), (0, 0), (0, 0))).reshape(B, nb, MOBA_BLOCK, H, d)
    kmean = jnp.mean(kb.astype(jnp.float32), axis=2)
    kb = kb.transpose(0, 3, 1, 2, 4)
    vb = vb.transpose(0, 3, 1, 2, 4)
    n_sel = min(MOBA_TOPK, nb)
    qb = min(T, MOBA_Q_BLOCK)
    n_qb = T // qb
    q_blocks = q.reshape(B, n_qb, qb, H, d).transpose(1, 0, 2, 3, 4)
    pos_blocks = q_pos.reshape(n_qb, qb)
    b_ix = jnp.arange(B)[:, None, None, None]
    h_ix = jnp.arange(H)[None, None, :, None]
    offs = jnp.arange(MOBA_BLOCK, dtype=jnp.int32)
    blk_ids = jnp.arange(nb, dtype=jnp.int32)
    scale = d ** -0.5

    def one_block(args):
        qq, pp = args
        own = pp // MOBA_BLOCK
        gate = jnp.einsum('bqhd,bnhd->bqhn', qq.astype(jnp.float32), kmean)
        fully_past = blk_ids[None, :] < own[:, None]
        gate = jnp.where(fully_past[None, :, None, :], gate, -jnp.inf)
        _, top_i = lax.top_k(gate, n_sel)
        sel_ok = top_i < own[None, :, None, None]
        own_b = jnp.broadcast_to(own[None, :, None, None], (B, qb, H, 1))
        blocks = jnp.concatenate([top_i, own_b], axis=-1)
        valid = jnp.concatenate([sel_ok, jnp.ones((B, qb, H, 1), bool)], axis=-1)
        kg = kb[b_ix, h_ix, blocks]
        vg = vb[b_ix, h_ix, blocks]
        logits = jnp.einsum('bqhd,bqhnpd->bqhnp', qq, kg).astype(jnp.float32) * scale
        kpos = blocks[..., None] * MOBA_BLOCK + offs
        mask = valid[..., None] & (kpos <= pp[None, :, None, None, None])
        logits = jnp.where(mask, logits, -jnp.inf)
        probs = jax.nn.softmax(logits.reshape(B, qb, H, -1), axis=-1).reshape(logits.shape)
        return jnp.einsum('bqhnp,bqhnpd->bqhd', probs.astype(v.dtype), vg)

    out = lax.map(one_block, (q_blocks, pos_blocks))
    return out.transpose(1, 0, 2, 3, 4).reshape(B, T, H, d)


def _mem_attend(q, mk, mv):
    logits = jnp.einsum('bthd,bmhd->bhtm', q, mk).astype(jnp.float32) * (M_HD ** -0.5)
    p = jax.nn.softmax(logits, axis=-1).astype(mv.dtype)
    return jnp.einsum('bhtm,bmhd->bthd', p, mv)


def _mem_kv(mem, g_mem, w_mem_kv):
    B = mem.shape[0]
    mk, mv = jnp.split(_rmsnorm(mem, g_mem) @ w_mem_kv, 2, axis=-1)
    return mk.reshape(B, -1, M_HEADS, M_HD), mv.reshape(B, -1, M_HEADS, M_HD)


def _layer(x, pos, conv_prev, k_past, v_past, mk, mv,
           g_pre, g_post, w_in, ln_v_gain, w_spatial, b_spatial, conv_w,
           w_merge, b_merge, w_branch, w_out):
    B, T, _ = x.shape
    h = _rmsnorm(x, g_pre)
    (a_u, a_v, a_g, b_b, b_c, b_x, b_g,
     c_q, c_k, c_v, c_g, m_q, m_g) = jnp.split(h @ w_in, N_PARTS, axis=-1)
    lc = min(T, CHUNK)
    nc = T // lc
    vn = _layernorm(a_v, ln_v_gain)
    ws = jnp.where(jnp.tril(jnp.ones((lc, lc), bool))[None], w_spatial[:, :lc, :lc], 0.0)
    vc = vn.reshape(B, nc, lc, A_GROUPS, A_GW)
    sp = jnp.einsum('gts,bcsge->bctge', ws, vc) + b_spatial[:, :lc].T[None, None, :, :, None]
    out_a = a_u * sp.reshape(B, T, BRANCH_W)
    z = b_c * b_x
    zp = jnp.concatenate([conv_prev.astype(z.dtype), z], axis=1)
    conv = conv_w[0] * zp[:, 0:T]
    for j in range(1, CONV_W):
        conv = conv + conv_w[j] * zp[:, j:j + T]
    out_b = b_b * conv
    new_conv = zp[:, T:]
    q = _rope(c_q.reshape(B, T, C_HEADS, C_HD), pos)
    k = _rope(c_k.reshape(B, T, C_HEADS, C_HD), pos)
    v = c_v.reshape(B, T, C_HEADS, C_HD)
    if k_past is None:
        k_all, v_all = k, v
    else:
        k_all = jnp.concatenate([k_past.astype(k.dtype), k], axis=1)
        v_all = jnp.concatenate([v_past.astype(v.dtype), v], axis=1)
    out_c = _moba(q, k_all, v_all, pos).reshape(B, T, BRANCH_W)
    out_m = _mem_attend(m_q.reshape(B, T, M_HEADS, M_HD), mk, mv).reshape(B, T, BRANCH_W)
    branches = jnp.stack([jax.nn.silu(a_g) * out_a, jax.nn.silu(b_g) * out_b,
                          jax.nn.silu(c_g) * out_c, jax.nn.silu(m_g) * out_m], axis=2)
    gates = jax.nn.sigmoid(h @ w_merge + b_merge).reshape(B, T, N_BRANCH, D_MODEL)
    proj = jnp.einsum('btnw,nwd->btnd', branches, w_branch)
    y = jnp.einsum('btnd,btnd->btd', gates, proj) @ w_out
    return x + _rmsnorm(y, g_post), k, v, new_conv, vn


def setup_inputs(seed: int = 0) -> dict:
    key = jax.random.key(seed)
    ks = jax.random.split(key, 24)
    n_pages = PAST_LEN // PAGE_SIZE
    n_pool = (DEC_BATCH * n_pages * 5 + 3) // 4
    f32 = jnp.float32

    def nrm(k, shape, scale):
        return jax.random.normal(k, shape, f32) * scale

    page_table = jax.random.permutation(ks[9], n_pool)[:DEC_BATCH * n_pages]
    page_table = page_table.reshape(DEC_BATCH, n_pages).astype(jnp.int32)
    return {
        'x_prompt': nrm(ks[0], (BATCH, SEQ, D_MODEL), 1.0),
        'x_sample': nrm(ks[1], (DEC_BATCH, DEC_SEQ, D_MODEL), 1.0),
        'cache_k': nrm(ks[2], (DEPTH, n_pool, PAGE_SIZE, C_HEADS, C_HD), 1.0),
        'cache_v': nrm(ks[3], (DEPTH, n_pool, PAGE_SIZE, C_HEADS, C_HD), 1.0),
        'cache_mem_k': nrm(ks[4], (DEPTH, DEC_BATCH, N_MEM, M_HEADS, M_HD), 1.0),
        'cache_mem_v': nrm(ks[5], (DEPTH, DEC_BATCH, N_MEM, M_HEADS, M_HD), 1.0),
        'state_conv': nrm(ks[6], (DEPTH, DEC_BATCH, CONV_W - 1, BRANCH_W), 1.0),
        'page_table': page_table,
        'mem_prompt': nrm(ks[7], (BATCH, N_MEM, D_MODEL), 1.0),
        'g_pre': 1.0 + nrm(ks[10], (DEPTH, D_MODEL), 0.05),
        'g_post': 1.0 + nrm(ks[11], (DEPTH, D_MODEL), 0.05),
        'w_in': nrm(ks[12], (DEPTH, D_MODEL, IN_W), D_MODEL ** -0.5),
        'ln_v_gain': 1.0 + nrm(ks[13], (DEPTH, BRANCH_W), 0.05),
        'w_spatial': nrm(ks[14], (DEPTH, A_GROUPS, CHUNK, CHUNK), CHUNK ** -0.5),
        'b_spatial': 1.0 + nrm(ks[15], (DEPTH, A_GROUPS, CHUNK), 0.1),
        'conv_w': nrm(ks[16], (DEPTH, CONV_W, BRANCH_W), CONV_W ** -0.5),
        'g_mem': 1.0 + nrm(ks[17], (DEPTH, D_MODEL), 0.05),
        'w_mem_kv': nrm(ks[18], (DEPTH, D_MODEL, 2 * BRANCH_W), D_MODEL ** -0.5),
        'w_merge': nrm(ks[19], (DEPTH, D_MODEL, N_BRANCH * D_MODEL), D_MODEL ** -0.5),
        'b_merge': nrm(ks[20], (DEPTH, N_BRANCH * D_MODEL), 0.1),
        'w_branch': nrm(ks[21], (DEPTH, N_BRANCH, BRANCH_W, D_MODEL), BRANCH_W ** -0.5),
        'w_out': nrm(ks[22], (DEPTH, D_MODEL, D_MODEL), D_MODEL ** -0.5),
    }


def reference(x_prompt, x_sample, cache_k, cache_v, cache_mem_k, cache_mem_v, state_conv, page_table,
              mem_prompt, g_pre, g_post, w_in, ln_v_gain, w_spatial, b_spatial, conv_w, g_mem, w_mem_kv,
              w_merge, b_merge, w_branch, w_out):
    bp, tp = x_prompt.shape[0], x_prompt.shape[1]
    bs, ts = x_sample.shape[0], x_sample.shape[1]
    past_len = page_table.shape[1] * cache_k.shape[2]
    pos_p = jnp.arange(tp, dtype=jnp.int32)
    pos_s = past_len + jnp.arange(ts, dtype=jnp.int32)
    hp, hs = x_prompt, x_sample
    kp_l, vp_l, cp_l, mkp_l, mvp_l = [], [], [], [], []
    ks_l, vs_l, cs_l, vns_l = [], [], [], []
    for l in range(DEPTH):
        lw = (g_pre[l], g_post[l], w_in[l], ln_v_gain[l], w_spatial[l], b_spatial[l], conv_w[l],
              w_merge[l], b_merge[l], w_branch[l], w_out[l])
        mk_p, mv_p = _mem_kv(mem_prompt, g_mem[l], w_mem_kv[l])
        conv0 = jnp.zeros((bp, CONV_W - 1, BRANCH_W), hp.dtype)
        hp, kp, vp, cp, _ = _layer(hp, pos_p, conv0, None, None, mk_p, mv_p, *lw)
        k_past = cache_k[l, page_table].reshape(bs, past_len, C_HEADS, C_HD)
        v_past = cache_v[l, page_table].reshape(bs, past_len, C_HEADS, C_HD)
        hs, kn, vn_, cs, vns = _layer(hs, pos_s, state_conv[l], k_past, v_past,
                                      cache_mem_k[l], cache_mem_v[l], *lw)
        kp_l.append(kp); vp_l.append(vp); cp_l.append(cp); mkp_l.append(mk_p); mvp_l.append(mv_p)
        ks_l.append(kn); vs_l.append(vn_); cs_l.append(cs); vns_l.append(vns)
    return (hp, hs,
            jnp.stack(kp_l), jnp.stack(vp_l), jnp.stack(cp_l), jnp.stack(mkp_l), jnp.stack(mvp_l),
            jnp.stack(ks_l), jnp.stack(vs_l), jnp.stack(cs_l), jnp.stack(vns_l))
```

```python
import contextlib
import os
STAGE = int(os.environ.get('KSTAGE', '9'))
SUB = int(os.environ.get('KSUB', '99'))
FIN = int(os.environ.get('KFIN', '99'))
KC = int(os.environ.get('KC', '99'))
KR = int(os.environ.get('KR', '99'))
PSRR = int(os.environ.get('PSRR', '1'))
NDB = int(os.environ.get('NDB', '4'))
XQ = os.environ.get('XQ', 'sp')
import numpy as np
import concourse.bass as bass
import concourse.mybir as mybir
from concourse.bass_utils import run_bass_kernel_spmd

F32 = mybir.dt.float32
BF16 = mybir.dt.bfloat16
I32 = mybir.dt.int32
ALU = mybir.AluOpType
AF = mybir.ActivationFunctionType
AX = mybir.AxisListType

ENGS = ["pe", "act", "dve", "pool", "sp"]
NEG = -30000.0
EPS = 1e-6


class Op:
    __slots__ = ("eng", "fn", "waits", "needs_inc", "val", "dma_slot", "dma_val")

    def __init__(self, eng, fn, dma_slot=None):
        self.eng = eng
        self.fn = fn
        self.waits = []
        self.needs_inc = False
        self.val = None
        self.dma_slot = dma_slot
        self.dma_val = None


import types


def _freeze(fn):
    if fn.__closure__ is None:
        return fn
    cells = []
    for c in fn.__closure__:
        try:
            cells.append(types.CellType(c.cell_contents))
        except ValueError:
            cells.append(c)
    return types.FunctionType(fn.__code__, fn.__globals__, fn.__name__, fn.__defaults__, tuple(cells))


class Prog:
    def __init__(self, nc, same_engine_sync=True):
        self.nc = nc
        self.ops = {e: [] for e in ENGS}
        self.last_w = {}
        self.readers = {}
        self.same_engine_sync = same_engine_sync
        self.dma_slots = {}

    def op(self, eng, fn, reads=(), writes=(), dma=None):
        o = Op(eng, _freeze(fn), dma_slot=dma)
        deps = []
        for r in reads:
            w = self.last_w.get(r)
            if w is not None:
                deps.append((w, "raw"))
            if PSRR and isinstance(r, str) and r.startswith("ps") and r[2:].isdigit():
                lastrd = {}
                for rd in self.readers.get(r, ()):
                    if rd.eng != eng:
                        lastrd[rd.eng] = rd
                for rd in lastrd.values():
                    deps.append((rd, "rar"))
        for wkey in writes:
            w = self.last_w.get(wkey)
            if w is not None:
                deps.append((w, "waw"))
            lastrd = {}
            for rd in self.readers.get(wkey, ()):
                if rd.dma_slot is not None:
                    deps.append((rd, "war"))
                else:
                    lastrd[rd.eng] = rd
            for rd in lastrd.values():
                deps.append((rd, "war"))
        seen = set()
        for d, kind in deps:
            if d is o or id(d) in seen:
                continue
            seen.add(id(d))
            if d.eng == o.eng and d.dma_slot is None and o.dma_slot is None:
                if o.eng == "pe" or not self.same_engine_sync:
                    continue
            o.waits.append(d)
            if d.dma_slot is None:
                d.needs_inc = True
        if dma is not None:
            st = self.dma_slots.setdefault(dma, [0, None])
            if st[1] is not None:
                o.waits.append(st[1])
            st[0] += 16
            o.dma_val = st[0]
            st[1] = o
        for r in reads:
            self.readers.setdefault(r, []).append(o)
        for wkey in writes:
            self.last_w[wkey] = o
            self.readers[wkey] = []
        self.ops[eng].append(o)
        return o

    def pe(self, fn, reads=(), writes=()):
        return self.op("pe", fn, reads, writes)

    def act(self, fn, reads=(), writes=()):
        return self.op("act", fn, reads, writes)

    def dve(self, fn, reads=(), writes=()):
        return self.op("dve", fn, reads, writes)

    def pool(self, fn, reads=(), writes=()):
        return self.op("pool", fn, reads, writes)

    def dma(self, eng, slot, fn, reads=(), writes=()):
        return self.op(eng, fn, reads, writes, dma=slot)

    def emit(self):
        nc = self.nc
        for e in ENGS:
            c = 0
            for o in self.ops[e]:
                if o.dma_slot is None and o.needs_inc:
                    c += 1
                    o.val = c
        slots = sorted(self.dma_slots.keys(), key=str)
        with contextlib.ExitStack() as es:
            esem = {e: es.enter_context(nc.semaphore("s_" + e)) for e in ENGS}
            dsem = {s: es.enter_context(nc.semaphore("d_%d" % i)) for i, s in enumerate(slots)}
            es.enter_context(nc.allow_non_contiguous_dma(reason="small strided parameter / layout DMAs"))
            block = es.enter_context(nc.Block())

            def run(ename, eng):
                waited = {}
                for o in self.ops[ename]:
                    for d in o.waits:
                        if d.dma_slot is not None:
                            sem, v = dsem[d.dma_slot], d.dma_val
                        else:
                            sem, v = esem[d.eng], d.val
                        k = id(sem)
                        if waited.get(k, 0) >= v:
                            continue
                        waited[k] = v
                        eng.wait_ge(sem, v)
                    ins = o.fn(eng)
                    if o.dma_slot is not None:
                        ins.then_inc(dsem[o.dma_slot], 16)
                    elif o.needs_inc:
                        ins.then_inc(esem[ename], 1)
                if ename == "sp":
                    for s in slots:
                        st = self.dma_slots[s]
                        if waited.get(id(dsem[s]), 0) < st[0]:
                            eng.wait_ge(dsem[s], st[0])

            @block.tensor
            def _(eng):
                run("pe", eng)

            @block.scalar
            def _(eng):
                run("act", eng)

            @block.vector
            def _(eng):
                run("dve", eng)

            @block.gpsimd
            def _(eng):
                run("pool", eng)

            @block.sync
            def _(eng):
                run("sp", eng)


D = 1024
SEQ = 4096
DEPTH = 2
NSB = 16
NS = 64
W = 512
NPARTS = 13
NPOOL = 2560
PAGE = 128
NPAGES = 16


def host_consts():
    c = {}
    c["ident"] = np.eye(128, dtype=np.float32)
    k = np.arange(128)[:, None, None]
    o = np.arange(4)[None, :, None]
    q = np.arange(512)[None, None, :]
    c["tri"] = np.where(q >= o * 128 + k, 0.0, NEG).astype(np.float32)
    n = np.arange(16)[:, None]
    key = np.arange(SEQ)[None, :]
    c["kind"] = (key // 256 == n).astype(np.float32)
    half = 8
    inv = np.power(np.float32(500000.0), -np.arange(half, dtype=np.float32) * np.float32(2.0 / 16)).astype(np.float32)
    pos = np.arange(SEQ, dtype=np.float32)
    ang = (pos[:, None] * inv[None, :]).astype(np.float32)
    cs = np.cos(ang).astype(np.float32).reshape(32, 128, 8).transpose(1, 0, 2)
    sn = np.sin(ang).astype(np.float32).reshape(32, 128, 8).transpose(1, 0, 2)
    c["rope_p"] = np.ascontiguousarray(np.stack([cs, sn], axis=2))
    pos_s = (2048 + np.arange(4, dtype=np.float32))
    ang_s = (pos_s[:, None] * inv[None, :]).astype(np.float32)
    cs_s = np.repeat(np.cos(ang_s).astype(np.float32), 16, axis=0)
    sn_s = np.repeat(np.sin(ang_s).astype(np.float32), 16, axis=0)
    rs = np.zeros((128, 2, 8), np.float32)
    rs[:64, 0] = cs_s
    rs[:64, 1] = sn_s
    c["rope_s"] = rs
    p = np.arange(64)
    dm = np.zeros((128, 4, 16), np.float32)
    for t in range(4):
        for blp in range(16):
            dm[:64, t, blp] = ((p % 16) == blp) & ((p // 16) <= t)
    c["dmask"] = dm
    r = np.arange(32)
    md = np.zeros((128, 512), np.float32)
    md[:32] = ((r // 4)[:, None] == (np.arange(512) // 64)[None, :])
    c["mdiag"] = md
    rsel = np.zeros((128, 16, 64), np.float32)
    for bl in range(16):
        for rr in range(32):
            rsel[rr, bl, (rr % 4) * 16 + bl] = 1.0
    c["rsel"] = rsel
    cown = np.zeros((128, 4), np.float32)
    cown[:32] = np.where(np.arange(4)[None, :] <= (r % 4)[:, None], 0.0, NEG)
    c["cown"] = cown
    selT = np.zeros((128, 16, 4), np.float32)
    for bl in range(16):
        for t in range(4):
            selT[t * 16 + bl, bl, t] = 1.0
    c["selT"] = selT
    r16 = np.arange(16)
    mdm = np.zeros((128, 512), np.float32)
    mdm[:16] = ((r16 // 4)[:, None] == (np.arange(512) // 128)[None, :])
    c["mdiagm"] = mdm
    rselm = np.zeros((128, 16, 64), np.float32)
    for bl in range(16):
        for rr in range(16):
            rselm[rr, bl, (rr % 4) * 16 + bl] = 1.0
    c["rselm"] = rselm
    iot = np.zeros((128, 1), np.float32)
    iot[:, 0] = np.arange(128)
    c["iota"] = iot
    pp = np.arange(128)
    qm = np.zeros((128, 4, 8), np.float32)
    for cc in range(4):
        for h in range(8):
            qm[:, cc, h] = (h == 2 * cc + pp // 64)
    c["qmask"] = qm
    mm = np.zeros((128, 4, 4), np.float32)
    for cc in range(4):
        mm[:, cc, cc] = 1.0
    c["mmask"] = mm
    bi = np.zeros((128, 16, 8), np.float32)
    for j in range(16):
        bi[:, j, j // 2] = 1.0 / 256
    c["blkind"] = bi
    ob = np.full((128, 16, 64), NEG, np.float32)
    for rr in range(32):
        t = rr % 4
        for bl in range(16):
            for t2 in range(t + 1):
                ob[rr, bl, t2 * 16 + bl] = 0.0
    c["ownb"] = ob
    return c


def build(do_prompt=True, do_sample=True, n_groups=8, depth=DEPTH):
    nc = bass.Bass("TRN2", target_bir_lowering=False)
    P = Prog(nc, same_engine_sync=bool(int(os.environ.get('KSES', '1'))))

    def din(name, shape, dt=F32):
        return nc.dram_tensor(name, list(shape), dt, kind="ExternalInput").ap()

    def dout(name, shape, dt=F32):
        return nc.dram_tensor(name, list(shape), dt, kind="ExternalOutput").ap()

    def dscr(name, shape, dt):
        return nc.dram_tensor(name, list(shape), dt).ap()

    xp = din("xp", [SEQ, D])
    xs = din("xs", [NS, D])
    cache_k = din("cache_k", [DEPTH, NPOOL * PAGE, W])
    cache_v = din("cache_v", [DEPTH, NPOOL * PAGE, W])
    cmk = din("cmk", [DEPTH, NSB, 256, W])
    cmv = din("cmv", [DEPTH, NSB, 256, W])
    sconv = din("sconv", [DEPTH, NSB, 2, W])
    ptab = din("ptab", [1, NSB * NPAGES], I32)
    memp = din("memp", [256, D])
    g_pre = din("g_pre", [DEPTH, D])
    g_post = din("g_post", [DEPTH, D])
    w_in = din("w_in", [DEPTH, D, NPARTS * W])
    ln_v_gain = din("ln_v_gain", [DEPTH, W])
    w_spatial = din("w_spatial", [DEPTH, 4, 128, 128])
    b_spatial = din("b_spatial", [DEPTH, 4, 128])
    conv_w = din("conv_w", [DEPTH, 3, W])
    g_mem = din("g_mem", [DEPTH, D])
    w_mem_kv = din("w_mem_kv", [DEPTH, D, D])
    w_merge = din("w_merge", [DEPTH, D, 4 * D])
    b_merge = din("b_merge", [DEPTH, 4 * D])
    w_branch = din("w_branch", [DEPTH, 4 * W, D])
    w_out = din("w_out", [DEPTH, D, D])
    hc = host_consts()
    cin = {k: din("c_" + k, v.shape) for k, v in hc.items()}

    yp = dout("yp", [SEQ, D])
    ys = dout("ys", [NS, D])
    nkp = dout("nkp", [DEPTH, SEQ, W])
    nvp = dout("nvp", [DEPTH, SEQ, W])
    ncp = dout("ncp", [DEPTH, 2, W])
    nmk = dout("nmk", [DEPTH, 256, W])
    nmv = dout("nmv", [DEPTH, 256, W])
    nks = dout("nks", [DEPTH, NS, W])
    nvs = dout("nvs", [DEPTH, NS, W])
    ncs = dout("ncs", [DEPTH, NSB, 2, W])
    nvn = dout("nvn", [DEPTH, NS, W])

    w_in_b = dscr("w_in_b", [DEPTH, NPARTS, 128, 8 * W], BF16)
    w_merge_b = dscr("w_merge_b", [DEPTH, 8, 128, 8 * W], BF16)
    w_branch_b = dscr("w_branch_b", [DEPTH, 4, 128, 4 * D], BF16)
    w_out_b = dscr("w_out_b", [DEPTH, 2, 128, 8 * W], BF16)
    w_mem_b = dscr("w_mem_b", [DEPTH, 2, 128, 8 * W], BF16)
    x1p = dscr("x1p", [SEQ, D], F32)
    x1s = dscr("x1s", [NS, D], F32)
    KT_d = dscr("KT_d", [8, 80, SEQ], BF16)
    VA_d = dscr("VA_d", [SEQ, 4 * 192], BF16)

    es = contextlib.ExitStack()
    with es:
        def T(name, shape, dt):
            return es.enter_context(nc.sbuf_tensor(name, list(shape), dt))

        ps = [es.enter_context(nc.psum_tensor("ps%d" % i, [128, 512], F32)) for i in range(8)]
        PSK = ["ps%d" % i for i in range(8)]

        ident = T("ident", [128, 128], F32)
        identb = T("identb", [128, 128], BF16)
        trib = T("trib", [128, 4, 512], BF16)
        ones_f = T("ones_f", [128, 128], F32)
        ones_b = T("ones_b", [128, 128], BF16)
        rope_p = T("rope_p", [128, 32, 2, 8], F32)
        rope_s = T("rope_s", [128, 2, 8], F32)
        dmask = T("dmask", [128, 4, 16], F32)
        mdiag = T("mdiag", [128, 512], F32)
        rsel = T("rsel", [128, 16, 64], F32)
        cown = T("cown", [128, 4], F32)
        selT = T("selT", [128, 16, 4], F32)
        mdiagm = T("mdiagm", [128, 512], F32)
        rselm = T("rselm", [128, 16, 64], F32)
        iota = T("iota", [128, 1], F32)
        epsc = T("epsc", [128, 1], F32)
        qmask = T("qmask", [128, 4, 8], F32)
        mmask = T("mmask", [128, 4, 4], F32)
        blkind = T("blkind", [128, 16, 8], F32)
        ownb = T("ownb", [128, 16, 64], F32)

        cq = [0]

        def cload(dst, src, key):
            cq[0] += 1
            P.dma("pool", "c%d" % (cq[0] % 4), lambda e: e.dma_start(out=dst, in_=src), writes=[key])

        cload(ident[:], cin["ident"], "ident")
        cload(rope_p[:], cin["rope_p"], "rope_p")
        cload(rope_s[:], cin["rope_s"], "rope_s")
        cload(dmask[:], cin["dmask"], "dmask")
        cload(mdiag[:], cin["mdiag"], "mdiag")
        cload(rsel[:], cin["rsel"], "rsel")
        cload(cown[:], cin["cown"], "cown")
        cload(selT[:], cin["selT"], "selT")
        cload(mdiagm[:], cin["mdiagm"], "mdiagm")
        cload(rselm[:], cin["rselm"], "rselm")
        cload(iota[:], cin["iota"], "iota")
        cload(qmask[:], cin["qmask"], "qmask")
        cload(mmask[:], cin["mmask"], "mmask")
        cload(blkind[:], cin["blkind"], "blkind")
        cload(ownb[:], cin["ownb"], "ownb")
        P.dma("pool", "c0", lambda e: e.dma_start(out=trib[:], in_=cin["tri"]), writes=["trib"])
        P.dve(lambda e: e.tensor_copy(out=identb[:], in_=ident[:]), reads=["ident"], writes=["identb"])
        P.dve(lambda e: e.memset(ones_f[:], 1.0), writes=["ones_f"])
        P.dve(lambda e: e.memset(ones_b[:], 1.0), writes=["ones_b"])
        P.dve(lambda e: e.memset(epsc[:], EPS), writes=["epsc"])
        for h in range(8):
            for q4 in range(4):
                P.dma("pool", "c1", (lambda h, q4: lambda e: e.dma_start(out=KT_d[h, 64:80, q4 * 1024:(q4 + 1) * 1024], in_=cin["kind"][:, q4 * 1024:(q4 + 1) * 1024]))(h, q4), writes=["KTind"])
        def _va_ones():
          P.dve(lambda e: e.memset(vab[0][:], 1.0), writes=["vab0"])
          for c in range(4):
            P.dma("pool", "c2", (lambda c: lambda e: e.dma_start(
                out=VA_d.rearrange("(j p) f -> p j f", p=128)[:, :, c * 192 + 64:c * 192 + 128], in_=vab[0][:, :, 0:64]))(c),
                reads=["vab0"], writes=["VAones"])

        wq = [0]

        def wconv(dst, src, key):
            wq[0] += 1
            P.dma("pool", "wc%d" % (wq[0] % 4), lambda e: e.dma_start(out=dst, in_=src), writes=[key])

        def kpc(ap2d):
            return ap2d.rearrange("(k p) c -> p k c", p=128)

        for l in range(depth):
            for j in range(NPARTS):
                wconv(w_in_b[l, j].rearrange("p (k c) -> p k c", k=8), kpc(w_in[l, :, j * W:(j + 1) * W]), ("w_in_b", l, j))
            for n in range(4):
                for hf in range(2):
                    wconv(w_merge_b[l, n * 2 + hf].rearrange("p (k c) -> p k c", k=8), kpc(w_merge[l, :, n * D + hf * W:n * D + (hf + 1) * W]), ("w_merge_b", l, n, hf))
                wconv(w_branch_b[l, n].rearrange("p (k c) -> p k c", k=4), kpc(w_branch[l, n * W:(n + 1) * W, :]), ("w_branch_b", l, n))
            for hf in range(2):
                wconv(w_out_b[l, hf].rearrange("p (k c) -> p k c", k=8), kpc(w_out[l, :, hf * W:(hf + 1) * W]), ("w_out_b", l, hf))
                wconv(w_mem_b[l, hf].rearrange("p (k c) -> p k c", k=8), kpc(w_mem_kv[l, :, hf * W:(hf + 1) * W]), ("w_mem_b", l, hf))

        NR = 3
        ring = [T("wr%d" % i, [128, 8, 512], BF16) for i in range(NR)]
        wcount = [0]

        def wload(src_ap_fn, srckey):
            i = wcount[0] % NR
            wcount[0] += 1
            key = "wr%d" % i
            P.dma("sp", "w%d" % i, lambda e: e.dma_start(out=src_ap_fn[0](ring[i]), in_=src_ap_fn[1]), reads=[srckey], writes=[key])
            return ring[i], key

        flat = (lambda r: r[:].rearrange("p k c -> p (k c)"))

        def unit_in(l, j):
            return (flat, w_in_b[l, j]), ("w_in_b", l, j)

        def unit_merge(l, n, hf):
            return (flat, w_merge_b[l, n * 2 + hf]), ("w_merge_b", l, n, hf)

        def unit_branch(l, n):
            return (flat, w_branch_b[l, n]), ("w_branch_b", l, n)

        def unit_out(l, hf):
            return (flat, w_out_b[l, hf]), ("w_out_b", l, hf)

        def unit_mem(l, hf):
            return (flat, w_mem_b[l, hf]), ("w_mem_b", l, hf)

        xt = T("xt", [128, D], F32)
        junk = T("junk", [128, D], BF16)
        st = T("st", [128, 8], F32)
        gpre_bc = T("gpre_bc", [128, D], F32)
        gpost_bc = T("gpost_bc", [128, D], F32)
        gln_bc = T("gln_bc", [128, W], F32)
        bm_col = T("bm_col", [128, 32], F32)
        cw_col = T("cw_col", [128, 4, 3], F32)
        hT = T("hT", [128, 8, 512], BF16)
        yacc = T("yacc", [128, 8, 512], BF16)
        fA = T("fA", [128, 4, 512], BF16)
        fB = T("fB", [128, 4, 512], BF16)
        fC = T("fC", [128, 4, 512], BF16)
        fT = T("fT", [128, 512], BF16)
        gT = T("gT", [128, 512], BF16)
        tmpf = T("tmpf", [128, 512], F32)
        zp = T("zp", [128, 4, 32 + 512], F32)
        cacc = T("cacc", [128, 512], F32)
        vn_b = T("vn_b", [128, 4, 512], BF16)
        vn_f = T("vn_f", [128, 512], F32)
        bnst = T("bnst", [128, 8], F32)
        wsT = T("wsT", [128, 4, 128], BF16)
        ws_nat = T("ws_nat", [128, 4, 128], F32)
        bsp_row = T("bsp_row", [1, 4, 128], BF16)
        bsp_f = T("bsp_f", [1, 4, 128], F32)
        ws64 = T("ws64", [128, 4, 64], BF16)
        w4bc = T("w4bc", [128, 4, 4], F32)
        bsp64 = T("bsp64", [1, 4, 64], BF16)
        qa = T("qa", [128, 4, 8, 80], F32)
        kst = T("kst", [128, 512], F32)
        vst = T("vst", [128, 512], F32)
        lqq = T("lqq", [128, 4, 8], F32)
        ktst = T("ktst", [64, 8, 128], BF16)
        vaug = T("vaug", [128, 4, 192], BF16)
        KM = T("KM", [128, 4, 128], BF16)
        qT4 = T("qT4", [128, 4, 128], BF16)
        gm = T("gm", [128, 8, 16], F32)
        m8 = T("m8", [128, 8, 8], F32)
        selm = T("selm", [128, 8, 16], F32)
        qTa = T("qTa", [80, 8, 512], BF16)
        ktb = [T("ktb0", [128, 2, SEQ], BF16)] * 2
        vab = [T("vab0", [128, 32, 192], BF16)] * 2
        pT = [T("pT%d" % i, [128, 512], BF16) for i in range(3)]
        rs = T("rs", [128, 512], F32)
        oc = fB
        memT = fA[:].rearrange("p c (a n) -> p (c a) n", a=2)
        mkT = T("mkT", [128, 4, 256], BF16)
        mv_b = T("mv_b", [128, 2, 512], BF16)
        pm = T("pm", [128, 256], F32)
        pmT = T("pmT", [128, 2, 128], BF16)
        small = T("small", [128, 16], F32)
        ncst = T("ncst", [32, 512], F32)
        ksum = T("ksum", [128, 4], F32)
        kring = T("kring", [128, 2, 512], F32)
        vring = T("vring", [128, 2, 512], F32)
        ocs = T("ocs", [64, 512], F32)
        pt_i = T("pt_i", [128, NSB * NPAGES], I32)
        pt_f = T("pt_f", [128, NSB * NPAGES], F32)
        idx_i = T("idx_i", [128, NSB * NPAGES], I32)
        qTs = T("qTs", [128, 4, 64], F32)
        kTn = T("kTn", [128, 4, 64], BF16)
        Qblk_f = T("Qblk_f", [128, 4, 32], F32)
        Qblk = T("Qblk", [128, 4, 32], BF16)
        kmT = T("kmT", [128, 4, 8], F32)
        gms = T("gms", [32, 8], F32)
        m8s = T("m8s", [32, 8], F32)
        selb = T("selb", [32, 8], F32)
        den = T("den", [32, 16], F32)
        PTo = T("PTo", [64, 32], F32)
        Po = T("Po", [32, 64], F32)
        KTs = ktb[0][:].rearrange("p a (b k) -> p (a b) k", b=2)
        Pm = [rs, cacc]
        PT = vn_f[:].rearrange("p (j r) -> p j r", j=16)
        On = ncst
        P.pool(lambda e: e.memset(small[:], 0.0), writes=["small"])
        _va_ones()

        def rstd_from_sumsq(col_in, col_out, np_, n_elem, keys_r, keys_w):
            P.act(lambda e: e.activation(out=st[:np_, col_out:col_out + 1], in_=st[:np_, col_in:col_in + 1], func=AF.Ln,
                                         scale=1.0 / n_elem, bias=epsc[:np_, 0:1]), reads=keys_r + ["epsc"], writes=keys_w)
            P.act(lambda e: e.activation(out=st[:np_, col_out:col_out + 1], in_=st[:np_, col_out:col_out + 1], func=AF.Exp,
                                         scale=-0.5), reads=keys_w, writes=keys_w)

        def norm_rows_to_T(src_rows_ap, np_, gbc, gkey, dstT, dst_key, col0, xkeys=()):
            P.dma(XQ, "xin", lambda e: e.dma_start(out=xt[:np_, :], in_=src_rows_ap), reads=list(xkeys), writes=["xt"])
            P.act(lambda e: e.activation(out=junk[:np_, :], in_=xt[:np_, :], func=AF.Square, accum_out=st[:np_, 0:1]),
                  reads=["xt"], writes=["junk", "st0"])
            rstd_from_sumsq(0, 1, np_, D, ["st0"], ["st1"])
            P.dve(lambda e: e.scalar_tensor_tensor(out=xt[:np_, :], in0=xt[:np_, :], scalar=st[:np_, 1:2], in1=gbc[:np_, :],
                                                   op0=ALU.mult, op1=ALU.mult), reads=["xt", "st1", gkey], writes=["xt"])
            for half in range(2):
                pb = 6 + half
                for i in range(4):
                    k = half * 4 + i
                    P.pe((lambda k, i, pb: lambda e: e.transpose(out=ps[pb][:, i * 128:i * 128 + np_], in_=xt[:np_, k * 128:(k + 1) * 128],
                                                                 identity=ident[:np_, :np_]))(k, i, pb),
                         reads=["xt", "ident"], writes=[PSK[pb]])
                src = ps[pb][:].rearrange("p (a b) -> p a b", a=4)[:, :, 0:np_]
                dst = dstT[:, half * 4:(half + 1) * 4, col0:col0 + np_]
                if half == 0:
                    P.act((lambda src, dst: lambda e: e.copy(out=dst, in_=src))(src, dst), reads=[PSK[pb]], writes=[dst_key])
                else:
                    P.dve((lambda src, dst: lambda e: e.tensor_copy(out=dst, in_=src))(src, dst), reads=[PSK[pb]], writes=[dst_key])

        dps = [0]

        def dense_bank():
            dps[0] = (dps[0] + 1) % NDB
            return dps[0]

        def proj_F(wt, wkey, c, N, src=None, srckey="hT", nk=8, cols=None):
            b = dense_bank()
            s = hT if src is None else src
            for k in range(nk):
                lw = wt[:, k, c * 128:(c + 1) * 128] if cols is None else cols(k)
                P.pe((lambda k, lw: lambda e: e.matmul(ps[b][:, 0:N], lhsT=lw, rhs=s[:, k, 0:N], start=(k == 0), stop=(k == nk - 1)))(k, lw),
                     reads=[wkey, srckey], writes=[PSK[b]])
            return b

        def proj_T(wt, wkey, t, tp, src=None, srckey="hT"):
            b = dense_bank()
            s = hT if src is None else src
            for k in range(8):
                P.pe((lambda k: lambda e: e.matmul(ps[b][:tp, 0:512], lhsT=s[:, k, t * tp:(t + 1) * tp], rhs=wt[:, k, :],
                                                   start=(k == 0), stop=(k == 7)))(k),
                     reads=[wkey, srckey], writes=[PSK[b]])
            return b

        def rope(b, tp, tab, jt, dst3, scale, keys_w):
            src = ps[b][:tp, :].rearrange("p (h d) -> p h d", h=8)
            if jt is None:
                cs = tab[:tp, 0, :].unsqueeze(1).to_broadcast([tp, 8, 8])
                sn = tab[:tp, 1, :].unsqueeze(1).to_broadcast([tp, 8, 8])
            else:
                cs = tab[:tp, jt, 0, :].unsqueeze(1).to_broadcast([tp, 8, 8])
                sn = tab[:tp, jt, 1, :].unsqueeze(1).to_broadcast([tp, 8, 8])
            t1 = tmpf[:tp, 0:64].rearrange("p (h d) -> p h d", h=8)
            t2 = tmpf[:tp, 64:128].rearrange("p (h d) -> p h d", h=8)
            x1 = src[:, :, 0:8]
            x2 = src[:, :, 8:16]
            rk = [PSK[b], "rope"]
            if KR < 1:
                return
            P.act(lambda e: e.mul(out=dst3[:, :, 16:64], in_=src[:, :, 16:64], mul=scale), reads=[PSK[b]], writes=keys_w)
            if KR < 2:
                return
            P.dve(lambda e: e.tensor_tensor(out=t1, in0=x1, in1=cs, op=ALU.mult), reads=rk, writes=["tmpf"])
            P.dve(lambda e: e.tensor_tensor(out=t2, in0=x2, in1=sn, op=ALU.mult), reads=rk, writes=["tmpf"])
            P.dve(lambda e: e.tensor_tensor(out=dst3[:, :, 0:8], in0=t1, in1=t2, op=ALU.subtract), reads=["tmpf"], writes=keys_w)
            if KR < 3:
                return
            P.dve(lambda e: e.tensor_tensor(out=t1, in0=x2, in1=cs, op=ALU.mult), reads=rk, writes=["tmpf"])
            P.dve(lambda e: e.tensor_tensor(out=t2, in0=x1, in1=sn, op=ALU.mult), reads=rk, writes=["tmpf"])
            P.dve(lambda e: e.tensor_tensor(out=dst3[:, :, 8:16], in0=t1, in1=t2, op=ALU.add), reads=["tmpf"], writes=keys_w)
            if scale != 1.0:
                P.dve(lambda e: e.tensor_scalar(out=dst3[:, :, 0:16], in0=dst3[:, :, 0:16], scalar1=scale, scalar2=None, op0=ALU.mult),
                      reads=keys_w, writes=keys_w)

        def silu_mul(b, N, dst, dkey, other, okey):
            P.act(lambda e: e.activation(out=fT[:, 0:N], in_=ps[b][:, 0:N], func=AF.Silu), reads=[PSK[b]], writes=["fT"])
            P.pool(lambda e: e.tensor_tensor(out=dst, in0=fT[:, 0:N], in1=other, op=ALU.mult), reads=["fT", okey], writes=[dkey])

        def branch_proj(l, n, brT, brkey, N):
            wm0, km0 = wload(*unit_merge(l, n, 0))
            wm1, km1 = wload(*unit_merge(l, n, 1))
            wb, kb = wload(*unit_branch(l, n))
            wbv = wb[:].rearrange("p (a b) c -> p a (b c)", a=4)
            for ocn in range(8):
                wm, km = (wm0, km0) if ocn < 4 else (wm1, km1)
                bg = proj_F(wm, km, ocn % 4, N)
                P.act((lambda bg, ocn: lambda e: e.activation(out=gT[:, 0:N], in_=ps[bg][:, 0:N], func=AF.Sigmoid,
                                                              bias=bm_col[:, n * 8 + ocn:n * 8 + ocn + 1]))(bg, ocn),
                      reads=[PSK[bg], "bm_col"], writes=["gT"])
                bp = proj_F(wb, kb, ocn, N, src=brT, srckey=brkey, nk=4, cols=(lambda ocn: lambda k: wbv[:, k, ocn * 128:(ocn + 1) * 128])(ocn))
                if n == 0:
                    P.dve((lambda bp, ocn: lambda e: e.tensor_tensor(out=yacc[:, ocn, 0:N], in0=ps[bp][:, 0:N], in1=gT[:, 0:N], op=ALU.mult))(bp, ocn),
                          reads=[PSK[bp], "gT"], writes=[("yacc", ocn)])
                else:
                    P.dve((lambda bp: lambda e: e.tensor_tensor(out=fT[:, 0:N], in0=ps[bp][:, 0:N], in1=gT[:, 0:N], op=ALU.mult))(bp),
                          reads=[PSK[bp], "gT"], writes=["fT"])
                    P.pool((lambda ocn: lambda e: e.tensor_tensor(out=yacc[:, ocn, 0:N], in0=yacc[:, ocn, 0:N], in1=fT[:, 0:N], op=ALU.add))(ocn),
                           reads=["fT", ("yacc", ocn)], writes=[("yacc", ocn)])

        def layer_consts(l):
            P.dma("pool", "lc0", lambda e: e.dma_start(out=gpre_bc[:], in_=g_pre[l:l + 1, :].partition_broadcast(128)), writes=["gpre_bc"])
            P.dma("pool", "lc1", lambda e: e.dma_start(out=gpost_bc[:], in_=g_post[l:l + 1, :].partition_broadcast(128)), writes=["gpost_bc"])
            P.dma("pool", "lc2", lambda e: e.dma_start(out=gln_bc[:], in_=ln_v_gain[l:l + 1, :].partition_broadcast(128)), writes=["gln_bc"])
            with nc.allow_non_contiguous_dma(reason="small per-layer vectors"):
                P.dma("pool", "lc3", lambda e: e.dma_start(out=bm_col[:], in_=b_merge[l].rearrange("(a p) -> p a", p=128)), writes=["bm_col"])
                for j3 in range(3):
                    P.dma("pool", "lc0", (lambda j3: lambda e: e.dma_start(out=cw_col[:, :, j3], in_=conv_w[l, j3].rearrange("(c p) -> p c", p=128)))(j3), writes=["cw_col"])
            P.dma("pool", "lc1", lambda e: e.dma_start(out=ws_nat[:], in_=w_spatial[l].rearrange("g t s -> t g s")), writes=["ws_nat"])
            for g4 in range(4):
                P.pe((lambda g4: lambda e: e.transpose(out=ps[6][:, g4 * 128:(g4 + 1) * 128], in_=ws_nat[:, g4, :], identity=ident[:]))(g4),
                     reads=["ws_nat", "ident"], writes=[PSK[6]])
            P.act(lambda e: e.copy(out=tmpf[:, :], in_=ps[6][:, :]), reads=[PSK[6]], writes=["tmpf"])
            P.pool(lambda e: e.affine_select(out=tmpf[:].rearrange("p (g t) -> p g t", g=4), in_=tmpf[:].rearrange("p (g t) -> p g t", g=4),
                                             pattern=[[0, 4], [1, 128]], compare_op=ALU.is_ge, fill=0.0, base=0, channel_multiplier=-1),
                   reads=["tmpf"], writes=["tmpf"])
            P.dve(lambda e: e.tensor_copy(out=wsT[:].rearrange("p g t -> p (g t)"), in_=tmpf[:, :]), reads=["tmpf"], writes=["wsT"])
            P.dma("pool", "lc2", lambda e: e.dma_start(out=bsp_f[:], in_=b_spatial[l:l + 1, :, :]), writes=["bsp_f"])
            P.dve(lambda e: e.tensor_copy(out=bsp_row[:], in_=bsp_f[:]), reads=["bsp_f"], writes=["bsp_row"])
            with nc.allow_non_contiguous_dma(reason="tiny 4x4 spatial block"):
                for s in range(4):
                    for g4 in range(4):
                        P.dma("pool", "lc3", (lambda s, g4: lambda e: e.dma_start(
                            out=w4bc[s * 16:(s + 1) * 16, g4, :],
                            in_=w_spatial[l, g4, 0:4, s:s + 1].rearrange("t o -> o t").partition_broadcast(16)))(s, g4), writes=["w4bc"])
            for g4 in range(4):
                for t in range(4):
                    P.dve((lambda g4, t: lambda e: e.tensor_scalar(out=ws64[:64, g4, t * 16:(t + 1) * 16], in0=dmask[:64, t, :],
                                                                    scalar1=w4bc[:64, g4, t:t + 1], scalar2=None, op0=ALU.mult))(g4, t),
                          reads=["w4bc", "dmask"], writes=["ws64"])
            P.dve(lambda e: e.tensor_copy(out=bsp64[:].rearrange("o g (t b) -> o g t b", b=16),
                                          in_=bsp_f[0:1, :, 0:4].unsqueeze(3).to_broadcast([1, 4, 4, 16])), reads=["bsp_f"], writes=["bsp64"])

        def dense_front(l, N, TT, tp, xsrc, is_sample, g, xkeys=()):
            for t in range(TT):
                norm_rows_to_T(xsrc(t), tp, gpre_bc, "gpre_bc", hT, "hT", t * tp, xkeys)
            if SUB < 1:
                return
            w0, k0 = wload(*unit_in(l, 0))
            if os.environ.get('KW1'):
                return
            w1, k1 = wload(*unit_in(l, 1))
            w2, k2 = wload(*unit_in(l, 2))
            for c in range(4):
                b = proj_F(w0, k0, c, N)
                P.act((lambda b, c: lambda e: e.copy(out=fA[:, c, 0:N], in_=ps[b][:, 0:N]))(b, c), reads=[PSK[b]], writes=[("fA", c)])
            if FIN < 1:
                return
            for t in range(TT):
                b = proj_T(w1, k1, t, tp)
                if FIN < 2:
                    continue
                P.act((lambda b: lambda e: e.activation(out=junk[:tp, 0:512], in_=ps[b][:tp, :], func=AF.Identity, accum_out=st[:tp, 2:3]))(b),
                      reads=[PSK[b]], writes=["junk", "st2"])
                P.dve(lambda e: e.tensor_scalar(out=st[:tp, 3:4], in0=st[:tp, 2:3], scalar1=-1.0 / 512, scalar2=None, op0=ALU.mult), reads=["st2"], writes=["st3"])
                P.act((lambda b: lambda e: e.activation(out=junk[:tp, 0:512], in_=ps[b][:tp, :], func=AF.Square, bias=st[:tp, 3:4], accum_out=st[:tp, 4:5]))(b),
                      reads=[PSK[b], "st3"], writes=["junk", "st4"])
                P.act(lambda e: e.activation(out=st[:tp, 4:5], in_=st[:tp, 4:5], func=AF.Ln, scale=1.0 / 512, bias=epsc[:tp, 0:1]), reads=["st4", "epsc"], writes=["st4"])
                P.act(lambda e: e.activation(out=st[:tp, 4:5], in_=st[:tp, 4:5], func=AF.Exp, scale=-0.5), reads=["st4"], writes=["st4"])
                P.dve(lambda e: e.tensor_tensor(out=st[:tp, 2:3], in0=st[:tp, 3:4], in1=st[:tp, 4:5], op=ALU.mult), reads=["st3", "st4"], writes=["st2"])
                P.act((lambda b: lambda e: e.activation(out=vn_f[:tp, :], in_=ps[b][:tp, :], func=AF.Identity, scale=st[:tp, 4:5], bias=st[:tp, 2:3]))(b),
                      reads=[PSK[b], "st2", "st4"], writes=["vn_f"])
                P.dve(lambda e: e.tensor_tensor(out=vn_f[:tp, :], in0=vn_f[:tp, :], in1=gln_bc[:tp, :], op=ALU.mult), reads=["vn_f", "gln_bc"], writes=["vn_f"])
                P.act((lambda t: lambda e: e.copy(out=vn_b[:tp, t, :], in_=vn_f[:tp, :]))(t), reads=["vn_f"], writes=["vn_b"])
                if is_sample:
                    P.dma(XQ, "o_vn", lambda e: e.dma_start(out=nvn[l], in_=vn_f[:tp, :]), reads=["vn_f"])
            if SUB < 2:
                return
            for g4 in range(4):
                b = dense_bank()
                for t in range(TT):
                    rhs_w = ws64[:tp, g4, :] if is_sample else wsT[:, g4, :]
                    rhs_b = bsp64[0:1, g4, :] if is_sample else bsp_row[0:1, g4, :]
                    P.pe((lambda t, g4, rhs_w: lambda e: e.matmul(ps[b][:, t * tp:(t + 1) * tp], lhsT=vn_b[:tp, t, g4 * 128:(g4 + 1) * 128], rhs=rhs_w,
                                                                  start=True, stop=False))(t, g4, rhs_w),
                         reads=["vn_b", "wsT", "ws64"], writes=[PSK[b]])
                    P.pe((lambda t, rhs_b: lambda e: e.matmul(ps[b][:, t * tp:(t + 1) * tp], lhsT=ones_b[0:1, :], rhs=rhs_b, start=False, stop=True))(t, rhs_b),
                         reads=["ones_b", "bsp_row", "bsp64"], writes=[PSK[b]])
                P.dve((lambda b, g4: lambda e: e.tensor_tensor(out=fA[:, g4, 0:N], in0=ps[b][:, 0:N], in1=fA[:, g4, 0:N], op=ALU.mult))(b, g4),
                      reads=[PSK[b], ("fA", g4)], writes=[("fA", g4)])
            if SUB < 3:
                return
            for c in range(4):
                b = proj_F(w2, k2, c, N)
                silu_mul(b, N, fA[:, c, 0:N], ("fA", c), fA[:, c, 0:N], ("fA", c))
            fAk = [("fA", c) for c in range(4)]
            P.pool(lambda e: e.tensor_copy(out=small[:, 0:1], in_=small[:, 0:1]), reads=fAk, writes=["fA"])
            if SUB < 4:
                return
            branch_proj(l, 0, fA, "fA", N)
            if SUB < 5:
                return
            S0 = 32 if is_sample else 2
            sh = 16 if is_sample else 1
            w3, k3 = wload(*unit_in(l, 3))
            for c in range(4):
                b = proj_F(w3, k3, c, N)
                P.act((lambda b, c: lambda e: e.copy(out=fB[:, c, 0:N], in_=ps[b][:, 0:N]))(b, c), reads=[PSK[b]], writes=[("fB", c)])
            w4, k4 = wload(*unit_in(l, 4))
            for c in range(4):
                b = proj_F(w4, k4, c, N)
                P.act((lambda b, c: lambda e: e.copy(out=fC[:, c, 0:N], in_=ps[b][:, 0:N]))(b, c), reads=[PSK[b]], writes=[("fC", c)])
            w5, k5 = wload(*unit_in(l, 5))
            for c in range(4):
                b = proj_F(w5, k5, c, N)
                P.dve((lambda b, c: lambda e: e.tensor_tensor(out=zp[:, c, S0:S0 + N], in0=ps[b][:, 0:N], in1=fC[:, c, 0:N], op=ALU.mult))(b, c),
                      reads=[PSK[b], ("fC", c)], writes=[("zp", c)])
                P.dve((lambda c: lambda e: e.tensor_scalar(out=cacc[:, 0:N], in0=zp[:, c, 0:N], scalar1=cw_col[:, c, 0:1], scalar2=None, op0=ALU.mult))(c),
                      reads=[("zp", c), "cw_col"], writes=["cacc"])
                P.dve((lambda c: lambda e: e.scalar_tensor_tensor(out=cacc[:, 0:N], in0=zp[:, c, sh:sh + N], scalar=cw_col[:, c, 1:2], in1=cacc[:, 0:N],
                                                                   op0=ALU.mult, op1=ALU.add))(c), reads=[("zp", c), "cw_col", "cacc"], writes=["cacc"])
                P.dve((lambda c: lambda e: e.scalar_tensor_tensor(out=cacc[:, 0:N], in0=zp[:, c, 2 * sh:2 * sh + N], scalar=cw_col[:, c, 2:3], in1=cacc[:, 0:N],
                                                                   op0=ALU.mult, op1=ALU.add))(c), reads=[("zp", c), "cw_col", "cacc"], writes=["cacc"])
                P.dve((lambda c: lambda e: e.tensor_tensor(out=fB[:, c, 0:N], in0=cacc[:, 0:N], in1=fB[:, c, 0:N], op=ALU.mult))(c),
                      reads=["cacc", ("fB", c)], writes=[("fB", c)])
            if SUB < 6:
                return
            last = is_sample or (g == n_groups - 1)
            if last:
                ncol = 32 if is_sample else 2
                for c in range(4):
                    P.pe((lambda c: lambda e: e.transpose(out=ps[7][:ncol, c * 128:(c + 1) * 128], in_=zp[:, c, S0 + N - ncol:S0 + N], identity=ident[:]))(c),
                         reads=[("zp", c), "ident"], writes=[PSK[7]])
                P.act(lambda e: e.copy(out=ncst[:ncol, :], in_=ps[7][:ncol, :]), reads=[PSK[7]], writes=["ncst"])
                if is_sample:
                    for r in range(2):
                        P.dma(XQ, "o_nc", (lambda r: lambda e: e.dma_start(out=ncs[l, :, r, :], in_=ncst[r * 16:(r + 1) * 16, :]))(r), reads=["ncst"])
                else:
                    P.dma(XQ, "o_nc", lambda e: e.dma_start(out=ncp[l], in_=ncst[0:2, :]), reads=["ncst"])
            if not is_sample:
                for c in range(4):
                    P.act((lambda c: lambda e: e.copy(out=zp[:, c, 0:2], in_=zp[:, c, N:N + 2]))(c), reads=[("zp", c)], writes=[("zp", c)])
            w6, k6 = wload(*unit_in(l, 6))
            for c in range(4):
                b = proj_F(w6, k6, c, N)
                silu_mul(b, N, fB[:, c, 0:N], ("fB", c), fB[:, c, 0:N], ("fB", c))
            P.pool(lambda e: e.tensor_copy(out=small[:, 0:1], in_=small[:, 0:1]), reads=[("fB", c) for c in range(4)], writes=["fB"])
            branch_proj(l, 1, fB, "fB", N)

        def dense_back(l, N, TT, tp, xsrc, dst_rows, dslot, xkeys=(), okey="xout"):
            wo0, ko0 = wload(*unit_out(l, 0))
            wo1, ko1 = wload(*unit_out(l, 1))
            ykeys = [("yacc", i) for i in range(8)]
            for t in range(TT):
                bs = []
                for hf, (wo, ko) in enumerate(((wo0, ko0), (wo1, ko1))):
                    b = 2 + hf
                    for k in range(8):
                        P.pe((lambda k, b, wo: lambda e: e.matmul(ps[b][:tp, :], lhsT=yacc[:, k, t * tp:(t + 1) * tp], rhs=wo[:, k, :],
                                                                  start=(k == 0), stop=(k == 7)))(k, b, wo),
                             reads=[ko] + ykeys, writes=[PSK[b]])
                    P.act((lambda b, hf: lambda e: e.activation(out=junk[:tp, 0:512], in_=ps[b][:tp, :], func=AF.Square,
                                                                accum_out=st[:tp, 5 + hf:6 + hf]))(b, hf),
                          reads=[PSK[b]], writes=["junk", "st56"])
                    bs.append(b)
                P.dve(lambda e: e.tensor_tensor(out=st[:tp, 5:6], in0=st[:tp, 5:6], in1=st[:tp, 6:7], op=ALU.add), reads=["st56"], writes=["st56"])
                rstd_from_sumsq(5, 7, tp, D, ["st56"], ["st7"])
                P.dma(XQ, "xin", (lambda t: lambda e: e.dma_start(out=xt[:tp, :], in_=xsrc(t)))(t), reads=list(xkeys), writes=["xt"])
                for hf in range(2):
                    b = bs[hf]
                    P.dve((lambda b, hf: lambda e: e.scalar_tensor_tensor(out=tmpf[:tp, :], in0=ps[b][:tp, :], scalar=st[:tp, 7:8],
                                                                          in1=gpost_bc[:tp, hf * 512:(hf + 1) * 512], op0=ALU.mult, op1=ALU.mult))(b, hf),
                          reads=[PSK[b], "st7", "gpost_bc"], writes=["tmpf"])
                    P.pool((lambda hf: lambda e: e.tensor_tensor(out=xt[:tp, hf * 512:(hf + 1) * 512], in0=xt[:tp, hf * 512:(hf + 1) * 512],
                                                                  in1=tmpf[:tp, :], op=ALU.add))(hf), reads=["tmpf", "xt"], writes=["xt"])
                P.dma(XQ, dslot, (lambda t: lambda e: e.dma_start(out=dst_rows(t), in_=xt[:tp, :]))(t), reads=["xt"], writes=[(okey, l)])

        def mem_kv_prompt(l):
            P.dma("pool", "lc0", lambda e: e.dma_start(out=gpost_bc[:], in_=g_mem[l:l + 1, :].partition_broadcast(128)), writes=["gpost_bc"])
            for t in range(2):
                norm_rows_to_T(memp[t * 128:(t + 1) * 128, :], 128, gpost_bc, "gpost_bc", memT, "fA", t * 128)
            wk, kk = wload(*unit_mem(l, 0))
            wv, kv = wload(*unit_mem(l, 1))
            for t in range(2):
                b = proj_T(wk, kk, t, 128, src=memT, srckey="fA")
                P.act((lambda b: lambda e: e.copy(out=kst[:, :], in_=ps[b][:, :]))(b), reads=[PSK[b]], writes=["kst"])
                P.dma(XQ, "o_mk", (lambda t: lambda e: e.dma_start(out=nmk[l, t * 128:(t + 1) * 128, :], in_=kst[:, :]))(t), reads=["kst"])
                b = proj_T(wv, kv, t, 128, src=memT, srckey="fA")
                P.act((lambda b: lambda e: e.copy(out=vst[:, :], in_=ps[b][:, :]))(b), reads=[PSK[b]], writes=["vst"])
                P.dve((lambda t: lambda e: e.tensor_copy(out=mv_b[:, t, :], in_=vst[:, :]))(t), reads=["vst"], writes=["mv_b"])
                P.dma(XQ, "o_mv", (lambda t: lambda e: e.dma_start(out=nmv[l, t * 128:(t + 1) * 128, :], in_=vst[:, :]))(t), reads=["vst"])
            for h4 in range(4):
                b = proj_F(wk, kk, h4, 256, src=memT, srckey="fA")
                P.act((lambda b, h4: lambda e: e.copy(out=mkT[:, h4, :], in_=ps[b][:, 0:256]))(b, h4), reads=[PSK[b]], writes=["mkT"])

        def prompt_group(l, g):
            N, TT, tp = 512, 4, 128
            src = xp if l == 0 else x1p
            dst = x1p if l < depth - 1 else yp

            def xsrc(t):
                return src[(g * 4 + t) * 128:(g * 4 + t + 1) * 128, :]

            def xdst(t):
                return dst[(g * 4 + t) * 128:(g * 4 + t + 1) * 128, :]

            if STAGE < 1:
                return
            xkeys = [("xout", l - 1)] if l > 0 else []
            dense_front(l, N, TT, tp, xsrc, False, g, xkeys)
            if STAGE < 2:
                return
            w7, k7 = wload(*unit_in(l, 7))
            for t in range(TT):
                b = proj_T(w7, k7, t, tp)
                rope(b, tp, rope_p, g * 4 + t, qa[:, t, :, :], 0.125, [("qa", t)])
            if KC < 1:
                return
            w8, k8 = wload(*unit_in(l, 8))
            for t in range(TT):
                jt = g * 4 + t
                b = proj_T(w8, k8, t, tp)
                rope(b, tp, rope_p, jt, kst[:, :].rearrange("p (h d) -> p h d", h=8), 1.0, ["kst"])
                P.dma(XQ, "o_k", (lambda jt: lambda e: e.dma_start(out=nkp[l, jt * 128:(jt + 1) * 128, :], in_=kst[:, :]))(jt), reads=["kst"])
                if KC < 2:
                    continue
                P.dve((lambda t: lambda e: e.tensor_tensor(out=tmpf[:, :].rearrange("p (h d) -> p h d", h=8), in0=qa[:, t, :, 0:64],
                                                          in1=kst[:, :].rearrange("p (h d) -> p h d", h=8), op=ALU.mult))(t),
                      reads=[("qa", t), "kst"], writes=["tmpf"])
                P.dve((lambda t: lambda e: e.tensor_reduce(out=lqq[:, t, :], in_=tmpf[:, :].rearrange("p (h d) -> p h d", h=8), axis=AX.X, op=ALU.add))(t),
                      reads=["tmpf"], writes=["lqq"])
                if KC < 3:
                    continue
                for half in range(2):
                    pb = 6 + half
                    for i in range(4):
                        h = half * 4 + i
                        P.pe((lambda h, i, pb: lambda e: e.transpose(out=ps[pb][0:64, i * 128:(i + 1) * 128], in_=kst[:, h * 64:(h + 1) * 64], identity=ident[:]))(h, i, pb),
                             reads=["kst", "ident"], writes=[PSK[pb]])
                    srcp = ps[pb][0:64, :].rearrange("p (a b) -> p a b", a=4)
                    if half == 0:
                        P.act((lambda srcp: lambda e: e.copy(out=ktst[:, 0:4, :], in_=srcp))(srcp), reads=[PSK[pb]], writes=["ktst"])
                    else:
                        P.dve((lambda srcp: lambda e: e.tensor_copy(out=ktst[:, 4:8, :], in_=srcp))(srcp), reads=[PSK[pb]], writes=["ktst"])
                P.dma(XQ, "kt_w", (lambda jt: lambda e: e.dma_start(out=KT_d[:, 0:64, jt * 128:(jt + 1) * 128].rearrange("h p k -> p h k"), in_=ktst[:, :, :]))(jt),
                      reads=["ktst", "KTind"], writes=[("KT_d", g)])
                if KC < 4:
                    continue
                for c in range(4):
                    P.pe((lambda c: lambda e: e.matmul(ps[5][:, c:c + 1], lhsT=kst[:, c * 128:(c + 1) * 128], rhs=ones_f[:, 0:1],
                                                      start=True, stop=True))(c),
                         reads=["kst", "ones_f"], writes=[PSK[5]])
                if jt % 2 == 0:
                    P.dve(lambda e: e.tensor_scalar(out=ksum[:, 0:4], in0=ps[5][:, 0:4], scalar1=1.0 / 256, scalar2=None, op0=ALU.mult),
                          reads=[PSK[5]], writes=["ksum"])
                else:
                    nb = jt // 2
                    KMf = KM[:].rearrange("p c n -> p (c n)")
                    for c in range(4):
                        P.dve((lambda c, nb: lambda e: e.scalar_tensor_tensor(out=KMf[0:64, c * 128 + (2 * c) * 16 + nb:c * 128 + (2 * c) * 16 + nb + 1], in0=ps[5][0:64, c:c + 1],
                                                                               scalar=1.0 / 256, in1=ksum[0:64, c:c + 1], op0=ALU.mult, op1=ALU.add))(c, nb),
                              reads=[PSK[5], "ksum"], writes=["KM"])
                        P.dve((lambda c, nb: lambda e: e.scalar_tensor_tensor(out=KMf[64:128, c * 128 + (2 * c + 1) * 16 + nb:c * 128 + (2 * c + 1) * 16 + nb + 1],
                                                                               in0=ps[5][64:128, c:c + 1], scalar=1.0 / 256, in1=ksum[64:128, c:c + 1], op0=ALU.mult, op1=ALU.add))(c, nb),
                              reads=[PSK[5], "ksum"], writes=["KM"])
            if KC < 5:
                return
            w9, k9 = wload(*unit_in(l, 9))
            for t in range(TT):
                jt = g * 4 + t
                b = proj_T(w9, k9, t, tp)
                P.act((lambda b: lambda e: e.copy(out=vst[:, :], in_=ps[b][:, :]))(b), reads=[PSK[b]], writes=["vst"])
                if KC < 6:
                    continue
                P.dma(XQ, "o_v", (lambda jt: lambda e: e.dma_start(out=nvp[l, jt * 128:(jt + 1) * 128, :], in_=vst[:, :]))(jt), reads=["vst"])
                P.dve((lambda b: lambda e: e.tensor_copy(out=vaug[:].rearrange("p c (s d) -> p c s d", s=3)[:, :, 0:3:2, :],
                                                        in_=ps[b][:, :].rearrange("p (c s d) -> p c s d", c=4, s=2)))(b), reads=[PSK[b]], writes=["vaug"])
                P.dma(XQ, "va_w", (lambda jt: lambda e: e.dma_start(out=VA_d[jt * 128:(jt + 1) * 128, :], in_=vaug[:].rearrange("p c f -> p (c f)")))(jt),
                      reads=["vaug", "VAones"], writes=[("VA_d", g)])
            if STAGE < 3:
                return
            for t in range(TT):
                jt = g * 4 + t
                own = jt // 2
                P.act((lambda t: lambda e: e.copy(out=vn_f[:, :].rearrange("p (h d) -> p h d", h=8), in_=qa[:, t, :, 0:64]))(t), reads=[("qa", t)], writes=["vn_f"])
                for c in range(4):
                    P.pe((lambda c, t: lambda e: e.transpose(out=ps[6][:, c * 128:(c + 1) * 128], in_=vn_f[:, c * 128:(c + 1) * 128], identity=ident[:]))(c, t),
                         reads=["vn_f", "ident"], writes=[PSK[6]])
                P.act(lambda e: e.copy(out=qT4[:].rearrange("p c q -> p (c q)"), in_=ps[6][:, :]), reads=[PSK[6]], writes=["qT4"])
                for c in range(4):
                    P.pe((lambda c: lambda e: e.matmul(ps[7][:, 0:128], lhsT=qT4[:, c, :], rhs=KM[:, c, :], start=(c == 0), stop=(c == 3)))(c),
                         reads=["qT4", "KM"], writes=[PSK[7]])
                P.dve(lambda e: e.tensor_copy(out=gm[:].rearrange("p h n -> p (h n)"), in_=ps[7][:, 0:128]), reads=[PSK[7]], writes=["gm"])
                if own < 16:
                    P.dve((lambda own: lambda e: e.memset(gm[:, :, own:16], -1e30))(own), reads=["gm"], writes=["gm"])
                for h in range(8):
                    P.dve((lambda h: lambda e: e.max(out=m8[:, h, :], in_=gm[:, h, :]))(h), reads=["gm"], writes=["m8"])
                P.dve(lambda e: e.tensor_tensor(out=selm[:], in0=gm[:], in1=m8[:, :, 2:3].to_broadcast([128, 8, 16]), op=ALU.is_ge), reads=["gm", "m8"], writes=["selm"])
                P.dve(lambda e: e.tensor_scalar(out=selm[:], in0=selm[:], scalar1=-1.0, scalar2=-NEG, op0=ALU.add, op1=ALU.mult), reads=["selm"], writes=["selm"])
                P.dve((lambda t: lambda e: e.tensor_tensor(out=qa[:, t, :, 64:80], in0=selm[:], in1=lqq[:, t, :].unsqueeze(2).to_broadcast([128, 8, 16]),
                                                          op=ALU.subtract))(t), reads=["selm", "lqq", ("qa", t)], writes=[("qa", t)])
                P.dve((lambda t, own: lambda e: e.tensor_scalar(out=qa[:, t, :, 64 + own:65 + own], in0=lqq[:, t, :].unsqueeze(2), scalar1=-1.0, scalar2=None,
                                                               op0=ALU.mult))(t, own), reads=["lqq", ("qa", t)], writes=[("qa", t)])
                for half in range(2):
                    pb = 6 + half
                    for i in range(4):
                        h = half * 4 + i
                        P.pe((lambda h, i, pb, t: lambda e: e.transpose(out=ps[pb][0:80, i * 128:(i + 1) * 128], in_=qa[:, t, h, :], identity=ident[:]))(h, i, pb, t),
                             reads=[("qa", t), "ident"], writes=[PSK[pb]])
                    srcp = ps[pb][0:80, :].rearrange("p (a b) -> p a b", a=4)
                    dstp = qTa[:, half * 4:(half + 1) * 4, t * 128:(t + 1) * 128]
                    if half == 0:
                        P.act((lambda srcp, dstp: lambda e: e.copy(out=dstp, in_=srcp))(srcp, dstp), reads=[PSK[pb]], writes=["qTa"])
                    else:
                        P.dve((lambda srcp, dstp: lambda e: e.tensor_copy(out=dstp, in_=srcp))(srcp, dstp), reads=[PSK[pb]], writes=["qTa"])
            if STAGE < 4:
                return
            njt = 4 * g + 4
            L = njt * 128
            hist_k = [("KT_d", gg) for gg in range(g + 1)] + ["KTind"]
            hist_v = [("VA_d", gg) for gg in range(g + 1)] + ["VAones"]
            for c in range(4):
                sl = 0
                P.dma("sp", "ktb%d" % sl, (lambda c, sl: lambda e: e.dma_start(out=ktb[sl][0:80, :, 0:L], in_=KT_d[2 * c:2 * c + 2, :, 0:L].rearrange("h p k -> p h k")))(c, sl),
                      reads=hist_k, writes=["ktb%d" % sl])
                P.dma("sp", "vab%d" % sl, (lambda c, sl: lambda e: e.dma_start(out=vab[sl][:, 0:njt, :],
                                                                               in_=VA_d[0:L, c * 192:(c + 1) * 192].rearrange("(j p) f -> p j f", p=128)))(c, sl),
                      reads=hist_v, writes=["vab%d" % sl])
                for hh in range(2):
                    h = 2 * c + hh
                    ob = 4 + hh
                    def qk(j):
                        sb = 2 + (j % 2)
                        diag = j >= 4 * g
                        P.pe((lambda j, sb, sl, hh, h, diag: lambda e: e.matmul(ps[sb][:, :], lhsT=ktb[sl][0:80, hh, j * 128:(j + 1) * 128], rhs=qTa[0:80, h, :],
                                                                                start=True, stop=(not diag)))(j, sb, sl, hh, h, diag),
                             reads=["ktb%d" % sl, "qTa"], writes=[PSK[sb]])
                        if diag:
                            P.pe((lambda j, sb: lambda e: e.matmul(ps[sb][:, :], lhsT=identb[:], rhs=trib[:, j - 4 * g, :], start=False, stop=True))(j, sb),
                                 reads=["identb", "trib"], writes=[PSK[sb]])

                    def ex_pv(j):
                        sb = 2 + (j % 2)
                        pi = j % 3
                        P.act((lambda sb, pi: lambda e: e.activation(out=pT[pi][:, :], in_=ps[sb][:, :], func=AF.Exp))(sb, pi),
                              reads=[PSK[sb]], writes=["pT%d" % pi])
                        P.pe((lambda j, pi, sl, hh, ob: lambda e: e.matmul(ps[ob][:, :], lhsT=vab[sl][:, j, hh * 64:hh * 64 + 128], rhs=pT[pi][:, :],
                                                                           start=(j == 0), stop=(j == njt - 1)))(j, pi, sl, hh, ob),
                             reads=["vab%d" % sl, "pT%d" % pi], writes=[PSK[ob]])

                    qk(0)
                    for j in range(njt):
                        if j + 1 < njt:
                            qk(j + 1)
                        ex_pv(j)
                    if hh == 0:
                        P.dve((lambda ob: lambda e: e.reciprocal(out=rs[64:128, :], in_=ps[ob][64:128, :]))(ob), reads=[PSK[ob]], writes=["rs"])
                        P.dve((lambda ob, c: lambda e: e.tensor_tensor(out=oc[0:64, c, :], in0=ps[ob][0:64, :], in1=rs[64:128, :], op=ALU.mult))(ob, c),
                              reads=[PSK[ob], "rs"], writes=[("fB", c)])
                    else:
                        P.dve((lambda ob: lambda e: e.reciprocal(out=rs[0:64, :], in_=ps[ob][0:64, :]))(ob), reads=[PSK[ob]], writes=["rs"])
                        P.dve((lambda ob, c: lambda e: e.tensor_tensor(out=oc[64:128, c, :], in0=ps[ob][64:128, :], in1=rs[0:64, :], op=ALU.mult))(ob, c),
                              reads=[PSK[ob], "rs"], writes=[("fB", c)])
            if STAGE < 5:
                return
            w10, k10 = wload(*unit_in(l, 10))
            for c in range(4):
                b = proj_F(w10, k10, c, N)
                silu_mul(b, N, fA[:, c, 0:N], ("fA", c), oc[:, c, 0:N], ("fB", c))
            P.pool(lambda e: e.tensor_copy(out=small[:, 0:1], in_=small[:, 0:1]), reads=[("fA", c) for c in range(4)], writes=["fA"])
            branch_proj(l, 2, fA, "fA", N)
            if STAGE < 6:
                return
            w11, k11 = wload(*unit_in(l, 11))
            for c in range(4):
                b = proj_F(w11, k11, c, N)
                P.act((lambda b, c: lambda e: e.copy(out=fC[:, c, 0:N], in_=ps[b][:, 0:N]))(b, c), reads=[PSK[b]], writes=[("fC", c)])
            sc = 128.0 ** -0.5
            for h4 in range(4):
                ob = 4 + (h4 % 2)
                db = 6 + (h4 % 2)
                for mt in range(2):
                    P.pe((lambda mt, h4: lambda e: e.matmul(ps[2 + mt][:, :], lhsT=mkT[:, h4, mt * 128:(mt + 1) * 128], rhs=fC[:, h4, 0:512], start=True, stop=True))(mt, h4),
                         reads=[("fC", h4), "mkT"], writes=[PSK[2 + mt]])
                for mt in range(2):
                    P.act((lambda mt: lambda e: e.activation(out=pT[mt][:, :], in_=ps[2 + mt][:, :], func=AF.Exp, scale=sc))(mt),
                          reads=[PSK[2 + mt]], writes=["pT%d" % mt])
                    P.pe((lambda mt, h4, ob: lambda e: e.matmul(ps[ob][:, :], lhsT=mv_b[:, mt, h4 * 128:(h4 + 1) * 128], rhs=pT[mt][:, :], start=(mt == 0), stop=(mt == 1)))(mt, h4, ob),
                         reads=["mv_b", "pT%d" % mt], writes=[PSK[ob]])
                    P.pe((lambda mt, db: lambda e: e.matmul(ps[db][:, :], lhsT=ones_b[:, :], rhs=pT[mt][:, :], start=(mt == 0), stop=(mt == 1)))(mt, db),
                         reads=["ones_b", "pT%d" % mt], writes=[PSK[db]])
                P.dve((lambda db: lambda e: e.reciprocal(out=rs[:, :], in_=ps[db][:, :]))(db), reads=[PSK[db]], writes=["rs"])
                P.dve((lambda ob, h4: lambda e: e.tensor_tensor(out=oc[:, h4, :], in0=ps[ob][:, :], in1=rs[:, :], op=ALU.mult))(ob, h4),
                      reads=[PSK[ob], "rs"], writes=[("fB", h4)])
            w12, k12 = wload(*unit_in(l, 12))
            for c in range(4):
                b = proj_F(w12, k12, c, N)
                silu_mul(b, N, fA[:, c, 0:N], ("fA", c), oc[:, c, 0:N], ("fB", c))
            P.pool(lambda e: e.tensor_copy(out=small[:, 0:1], in_=small[:, 0:1]), reads=[("fA", c) for c in range(4)], writes=["fA"])
            branch_proj(l, 3, fA, "fA", N)
            dense_back(l, N, TT, tp, xsrc, xdst, "o_x", xkeys)

        def sample_setup():
            P.dma("pool", "lc0", lambda e: e.dma_start(out=pt_i[:], in_=ptab[0:1, :].partition_broadcast(128)), writes=["pt_i"])
            P.dve(lambda e: e.tensor_copy(out=pt_f[:], in_=pt_i[:]), reads=["pt_i"], writes=["pt_f"])
            P.dve(lambda e: e.tensor_scalar(out=pt_f[:], in0=pt_f[:], scalar1=float(PAGE), scalar2=iota[:, 0:1], op0=ALU.mult, op1=ALU.add),
                  reads=["pt_f", "iota"], writes=["pt_f"])
            P.dve(lambda e: e.tensor_copy(out=idx_i[:], in_=pt_f[:]), reads=["pt_f"], writes=["idx_i"])
            if depth > 1:
                P.dve(lambda e: e.tensor_scalar(out=pt_f[:], in0=pt_f[:], scalar1=float(NPOOL * PAGE), scalar2=None, op0=ALU.add), reads=["pt_f"], writes=["pt_f"])
                P.dve(lambda e: e.tensor_copy(out=pt_i[:], in_=pt_f[:]), reads=["pt_f"], writes=["idx_i"])

        def ocs_to_oc():
            for c in range(4):
                P.pe((lambda c: lambda e: e.transpose(out=ps[6][:, c * 64:(c + 1) * 64], in_=ocs[:64, c * 128:(c + 1) * 128], identity=ident[:64, :64]))(c),
                     reads=["ocs", "ident"], writes=[PSK[6]])
            P.act(lambda e: e.copy(out=oc[:, :, 0:64], in_=ps[6][:, 0:256].rearrange("p (c q) -> p c q", c=4)), reads=[PSK[6]],
                  writes=[("fB", c) for c in range(4)])

        def scatter_acc(R, selc, bl, first):
            P.pe(lambda e: e.matmul(ps[4][:64, :], lhsT=selc[:R, bl, :], rhs=On[:R, :], start=True, stop=True),
                 reads=["ncst", "rsel", "rselm"], writes=[PSK[4]])
            if first:
                P.dve(lambda e: e.tensor_copy(out=ocs[:64, :], in_=ps[4][:64, :]), reads=[PSK[4]], writes=["ocs"])
            else:
                P.dve(lambda e: e.tensor_tensor(out=ocs[:64, :], in0=ocs[:64, :], in1=ps[4][:64, :], op=ALU.add), reads=[PSK[4], "ocs"], writes=["ocs"])

        def sample_group(l):
            N, TT, tp = 64, 1, 64
            src = xs if l == 0 else x1s
            dst = x1s if l < depth - 1 else ys
            xkeys = [("xsout", l - 1)] if l > 0 else []

            def xsrc(t):
                return src[0:64, :]

            def xdst(t):
                return dst[0:64, :]

            for c in range(4):
                for r in range(2):
                    P.dma("pool", "lc%d" % c, (lambda c, r: lambda e: e.dma_start(out=zp[:, c, r * 16:(r + 1) * 16],
                                                                                  in_=sconv[l, :, r, c * 128:(c + 1) * 128].rearrange("b p -> p b")))(c, r),
                          writes=[("zp", c)])
            dense_front(l, N, TT, tp, xsrc, True, 0, xkeys)
            w7, k7 = wload(*unit_in(l, 7))
            b = proj_T(w7, k7, 0, tp)
            rope(b, tp, rope_s, None, qa[:64, 0, :, :], 0.125, [("qa", 0)])
            P.act(lambda e: e.copy(out=vn_f[:64, :].rearrange("p (h d) -> p h d", h=8), in_=qa[:64, 0, :, 0:64]), reads=[("qa", 0)], writes=["vn_f"])
            for c in range(4):
                P.pe((lambda c: lambda e: e.transpose(out=ps[6][:, c * 64:(c + 1) * 64], in_=vn_f[:64, c * 128:(c + 1) * 128], identity=ident[:64, :64]))(c),
                     reads=["vn_f", "ident"], writes=[PSK[6]])
            P.act(lambda e: e.copy(out=qTs[:].rearrange("p c q -> p (c q)"), in_=ps[6][:, 0:256]), reads=[PSK[6]], writes=["qTs"])
            w8, k8 = wload(*unit_in(l, 8))
            b = proj_T(w8, k8, 0, tp)
            rope(b, tp, rope_s, None, kst[:64, :].rearrange("p (h d) -> p h d", h=8), 1.0, ["kst"])
            P.dma(XQ, "o_k", lambda e: e.dma_start(out=nks[l], in_=kst[:64, :]), reads=["kst"])
            for c in range(4):
                P.pe((lambda c: lambda e: e.transpose(out=ps[7][:, c * 64:(c + 1) * 64], in_=kst[:64, c * 128:(c + 1) * 128], identity=ident[:64, :64]))(c),
                     reads=["kst", "ident"], writes=[PSK[7]])
            P.act(lambda e: e.copy(out=kTn[:].rearrange("p c q -> p (c q)"), in_=ps[7][:, 0:256]), reads=[PSK[7]], writes=["kTn"])
            w9, k9 = wload(*unit_in(l, 9))
            b = proj_T(w9, k9, 0, tp)
            P.act((lambda b: lambda e: e.copy(out=vst[:64, :], in_=ps[b][:64, :]))(b), reads=[PSK[b]], writes=["vst"])
            P.dma(XQ, "o_v", lambda e: e.dma_start(out=nvs[l], in_=vst[:64, :]), reads=["vst"])
            qTs4 = qTs[:].rearrange("p c (t b) -> p c t b", b=16)
            gcnt = [0, 0]
            assert l < 2
            idx_l = idx_i if l == 0 else pt_i
            for bl in range(NSB):
                qsl = qTs4[:, :, :, bl:bl + 1].rearrange("p c t o -> p c (t o)").unsqueeze(2).to_broadcast([128, 4, 8, 4])
                P.dve((lambda qsl: lambda e: e.tensor_tensor(out=Qblk_f[:].rearrange("p c (h t) -> p c h t", h=8), in0=qsl,
                                                            in1=qmask[:].unsqueeze(3).to_broadcast([128, 4, 8, 4]), op=ALU.mult))(qsl),
                      reads=["qTs", "qmask"], writes=["Qblk_f"])
                P.dve(lambda e: e.tensor_copy(out=Qblk[:], in_=Qblk_f[:]), reads=["Qblk_f"], writes=["Qblk"])
                for j in range(NPAGES):
                    sl = gcnt[0] % 2
                    gcnt[0] += 1
                    col = bl * NPAGES + j
                    P.dma("pool", "gk%d" % sl, (lambda sl, col: lambda e: e.indirect_dma_start(
                        out=kring[:, sl, :], out_offset=None, in_=cache_k.rearrange("l n w -> (l n) w"),
                        in_offset=bass.IndirectOffsetOnAxis(ap=idx_l[:, col:col + 1], axis=0)))(sl, col),
                        reads=["idx_i"], writes=["kring%d" % sl])
                    P.pe((lambda sl, j: lambda e: e.matmul(ps[5][:8, :], lhsT=blkind[:, j, :], rhs=kring[:, sl, :], start=(j == 0), stop=(j == NPAGES - 1)))(sl, j),
                         reads=["kring%d" % sl, "blkind"], writes=[PSK[5]])
                    pb = 6 + (j % 2)
                    for c in range(4):
                        P.pe((lambda sl, c, pb: lambda e: e.transpose(out=ps[pb][:, c * 128:(c + 1) * 128], in_=kring[:, sl, c * 128:(c + 1) * 128], identity=ident[:]))(sl, c, pb),
                             reads=["kring%d" % sl, "ident"], writes=[PSK[pb]])
                    srcp = ps[pb][:, :].rearrange("p (c k) -> p c k", c=4)
                    dstp = KTs[:, :, j * 128:(j + 1) * 128]
                    if j % 2 == 0:
                        P.act((lambda srcp, dstp: lambda e: e.copy(out=dstp, in_=srcp))(srcp, dstp), reads=[PSK[pb]], writes=["ktb0"])
                    else:
                        P.dve((lambda srcp, dstp: lambda e: e.tensor_copy(out=dstp, in_=srcp))(srcp, dstp), reads=[PSK[pb]], writes=["ktb0"])
                P.act(lambda e: e.copy(out=tmpf[:8, :], in_=ps[5][:8, :]), reads=[PSK[5]], writes=["tmpf"])
                for c in range(4):
                    P.pe((lambda c: lambda e: e.transpose(out=ps[5][:, c * 8:(c + 1) * 8], in_=tmpf[:8, c * 128:(c + 1) * 128], identity=ident[:8, :8]))(c),
                         reads=["tmpf", "ident"], writes=[PSK[5]])
                P.act(lambda e: e.copy(out=kmT[:].rearrange("p c n -> p (c n)"), in_=ps[5][:, 0:32]), reads=[PSK[5]], writes=["kmT"])
                for c in range(4):
                    P.pe((lambda c: lambda e: e.matmul(ps[4][:32, 0:8], lhsT=Qblk_f[:, c, :], rhs=kmT[:, c, :], start=(c == 0), stop=(c == 3)))(c),
                         reads=["Qblk_f", "kmT"], writes=[PSK[4]])
                P.dve(lambda e: e.tensor_copy(out=gms[:, :], in_=ps[4][:32, 0:8]), reads=[PSK[4]], writes=["gms"])
                P.dve(lambda e: e.max(out=m8s[:, :], in_=gms[:, :]), reads=["gms"], writes=["m8s"])
                P.dve(lambda e: e.tensor_tensor(out=selb[:, :], in0=gms[:, :], in1=m8s[:, 2:3].to_broadcast([32, 8]), op=ALU.is_ge), reads=["gms", "m8s"], writes=["selb"])
                P.dve(lambda e: e.tensor_scalar(out=selb[:, :], in0=selb[:, :], scalar1=-1.0, scalar2=-NEG, op0=ALU.add, op1=ALU.mult), reads=["selb"], writes=["selb"])
                for qd in range(4):
                    sb = 2 + (qd % 2)
                    for c in range(4):
                        P.pe((lambda c, sb, qd: lambda e: e.matmul(ps[sb][:32, :], lhsT=Qblk[:, c, :], rhs=KTs[:, c, qd * 512:(qd + 1) * 512], start=(c == 0), stop=(c == 3)))(c, sb, qd),
                             reads=["Qblk", "ktb0"], writes=[PSK[sb]])
                    pmt = Pm[qd % 2]
                    pkey = "rs" if qd % 2 == 0 else "cacc"
                    for hf in range(2):
                        n = qd * 2 + hf
                        P.act((lambda sb, hf, n, pmt: lambda e: e.activation(out=pmt[:32, hf * 256:(hf + 1) * 256], in_=ps[sb][:32, hf * 256:(hf + 1) * 256], func=AF.Exp,
                                                                             bias=selb[:, n:n + 1], accum_out=den[:, n:n + 1]))(sb, hf, n, pmt),
                              reads=[PSK[sb], "selb"], writes=[pkey, "den"])
                    for jj in range(4):
                        j = qd * 4 + jj
                        P.pe((lambda jj, j, pmt: lambda e: e.transpose(out=ps[0][:, j * 32:(j + 1) * 32], in_=pmt[:32, jj * 128:(jj + 1) * 128], identity=ident[:32, :32]))(jj, j, pmt),
                             reads=[pkey, "ident"], writes=[PSK[0]])
                P.dve(lambda e: e.tensor_copy(out=vn_f[:, :], in_=ps[0][:, :]), reads=[PSK[0]], writes=["vn_f"])
                for c in range(4):
                    P.pe((lambda c: lambda e: e.matmul(ps[2][:32, 0:64], lhsT=Qblk[:, c, :], rhs=kTn[:, c, :], start=(c == 0), stop=(c == 3)))(c),
                         reads=["Qblk", "kTn"], writes=[PSK[2]])
                P.dve((lambda bl: lambda e: e.tensor_tensor(out=Po[:, :], in0=ps[2][:32, 0:64], in1=ownb[:32, bl, :], op=ALU.add))(bl), reads=[PSK[2], "ownb"], writes=["Po"])
                P.act(lambda e: e.activation(out=Po[:, :], in_=Po[:, :], func=AF.Exp, accum_out=den[:, 8:9]), reads=["Po"], writes=["Po", "den"])
                P.pe(lambda e: e.transpose(out=ps[3][:64, 0:32], in_=Po[:32, :], identity=ident[:32, :32]), reads=["Po", "ident"], writes=[PSK[3]])
                P.act(lambda e: e.copy(out=PTo[:, :], in_=ps[3][:64, 0:32]), reads=[PSK[3]], writes=["PTo"])
                for j in range(NPAGES):
                    sl = gcnt[1] % 2
                    gcnt[1] += 1
                    col = bl * NPAGES + j
                    P.dma("pool", "gv%d" % sl, (lambda sl, col: lambda e: e.indirect_dma_start(
                        out=vring[:, sl, :], out_offset=None, in_=cache_v.rearrange("l n w -> (l n) w"),
                        in_offset=bass.IndirectOffsetOnAxis(ap=idx_l[:, col:col + 1], axis=0)))(sl, col),
                        reads=["idx_i"], writes=["vring%d" % sl])
                    P.pe((lambda sl, j: lambda e: e.matmul(ps[1][:32, :], lhsT=PT[:, j, :], rhs=vring[:, sl, :], start=(j == 0), stop=False))(sl, j),
                         reads=["vring%d" % sl, "vn_f"], writes=[PSK[1]])
                P.pe(lambda e: e.matmul(ps[1][:32, :], lhsT=PTo[:64, :], rhs=vst[:64, :], start=False, stop=True), reads=["PTo", "vst"], writes=[PSK[1]])
                P.dve(lambda e: e.tensor_reduce(out=den[:, 9:10], in_=den[:, 0:9], axis=AX.X, op=ALU.add), reads=["den"], writes=["den"])
                P.dve(lambda e: e.reciprocal(out=den[:, 10:11], in_=den[:, 9:10]), reads=["den"], writes=["den"])
                P.dve(lambda e: e.scalar_tensor_tensor(out=On[:32, :], in0=ps[1][:32, :], scalar=den[:, 10:11], in1=mdiag[:32, :], op0=ALU.mult, op1=ALU.mult),
                      reads=[PSK[1], "den", "mdiag"], writes=["ncst"])
                scatter_acc(32, rsel, bl, bl == 0)
            ocs_to_oc()
            w10, k10 = wload(*unit_in(l, 10))
            for c in range(4):
                b = proj_F(w10, k10, c, N)
                silu_mul(b, N, fA[:, c, 0:N], ("fA", c), oc[:, c, 0:N], ("fB", c))
            P.pool(lambda e: e.tensor_copy(out=small[:, 0:1], in_=small[:, 0:1]), reads=[("fA", c) for c in range(4)], writes=["fA"])
            branch_proj(l, 2, fA, "fA", N)
            w11, k11 = wload(*unit_in(l, 11))
            for c in range(4):
                b = proj_F(w11, k11, c, N)
                P.act((lambda b, c: lambda e: e.copy(out=fC[:, c, 0:N], in_=ps[b][:, 0:N]))(b, c), reads=[PSK[b]], writes=[("fC", c)])
            P.pool(lambda e: e.tensor_copy(out=small[:, 0:1], in_=small[:, 0:1]), reads=[("fC", c) for c in range(4)], writes=["fC"])
            sc = 128.0 ** -0.5
            fC4 = fC[:, :, 0:64].rearrange("p c (t b) -> p c t b", b=16)
            Qm = Qblk[:, :, 0:16]
            for bl in range(NSB):
                qsl = fC4[:, :, :, bl:bl + 1].rearrange("p c t o -> p c (t o)").unsqueeze(2).to_broadcast([128, 4, 4, 4])
                P.dve((lambda qsl: lambda e: e.tensor_tensor(out=Qm.rearrange("p c (h t) -> p c h t", h=4), in0=qsl,
                                                            in1=mmask[:].unsqueeze(3).to_broadcast([128, 4, 4, 4]), op=ALU.mult))(qsl),
                      reads=["fC", "mmask"], writes=["Qblk"])
                for mt in range(2):
                    P.dma("sp", "mk%d" % mt, (lambda mt, bl: lambda e: e.dma_start(out=kring[:, mt, :], in_=cmk[l, bl, mt * 128:(mt + 1) * 128, :]))(mt, bl), writes=["kring%d" % mt])
                    P.dma("sp", "mv%d" % mt, (lambda mt, bl: lambda e: e.dma_start(out=vring[:, mt, :], in_=cmv[l, bl, mt * 128:(mt + 1) * 128, :]))(mt, bl), writes=["vring%d" % mt])
                    pb = 6 + mt
                    for c in range(4):
                        P.pe((lambda mt, c, pb: lambda e: e.transpose(out=ps[pb][:, c * 128:(c + 1) * 128], in_=kring[:, mt, c * 128:(c + 1) * 128], identity=ident[:]))(mt, c, pb),
                             reads=["kring%d" % mt, "ident"], writes=[PSK[pb]])
                    srcp = ps[pb][:, :].rearrange("p (c k) -> p c k", c=4)
                    dstp = mkT[:, :, mt * 128:(mt + 1) * 128]
                    P.act((lambda srcp, dstp: lambda e: e.copy(out=dstp, in_=srcp))(srcp, dstp), reads=[PSK[pb]], writes=["mkT"])
                for c in range(4):
                    P.pe((lambda c: lambda e: e.matmul(ps[2][:16, 0:256], lhsT=Qm[:, c, :], rhs=mkT[:, c, :], start=(c == 0), stop=(c == 3)))(c),
                         reads=["Qblk", "mkT"], writes=[PSK[2]])
                P.dve(lambda e: e.tensor_reduce(out=den[:16, 11:12], in_=ps[2][:16, 0:256], axis=AX.X, op=ALU.max), reads=[PSK[2]], writes=["den"])
                P.dve(lambda e: e.tensor_scalar(out=den[:16, 12:13], in0=den[:16, 11:12], scalar1=-sc, scalar2=None, op0=ALU.mult), reads=["den"], writes=["den"])
                P.act(lambda e: e.activation(out=rs[:16, 0:256], in_=ps[2][:16, 0:256], func=AF.Exp, scale=sc, bias=den[:16, 12:13], accum_out=den[:16, 13:14]),
                      reads=[PSK[2], "den"], writes=["rs", "den"])
                P.dve(lambda e: e.reciprocal(out=den[:16, 14:15], in_=den[:16, 13:14]), reads=["den"], writes=["den"])
                for mt in range(2):
                    P.pe((lambda mt: lambda e: e.transpose(out=ps[0][:, mt * 16:(mt + 1) * 16], in_=rs[:16, mt * 128:(mt + 1) * 128], identity=ident[:16, :16]))(mt),
                         reads=["rs", "ident"], writes=[PSK[0]])
                P.dve(lambda e: e.tensor_copy(out=vn_f[:, 0:32], in_=ps[0][:, 0:32]), reads=[PSK[0]], writes=["vn_f"])
                for mt in range(2):
                    P.pe((lambda mt: lambda e: e.matmul(ps[1][:16, :], lhsT=vn_f[:, mt * 16:(mt + 1) * 16], rhs=vring[:, mt, :], start=(mt == 0), stop=(mt == 1)))(mt),
                         reads=["vn_f", "vring%d" % mt], writes=[PSK[1]])
                P.dve(lambda e: e.scalar_tensor_tensor(out=On[:16, :], in0=ps[1][:16, :], scalar=den[:16, 14:15], in1=mdiagm[:16, :], op0=ALU.mult, op1=ALU.mult),
                      reads=[PSK[1], "den", "mdiagm"], writes=["ncst"])
                scatter_acc(16, rselm, bl, bl == 0)
            ocs_to_oc()
            w12, k12 = wload(*unit_in(l, 12))
            for c in range(4):
                b = proj_F(w12, k12, c, N)
                silu_mul(b, N, fA[:, c, 0:N], ("fA", c), oc[:, c, 0:N], ("fB", c))
            P.pool(lambda e: e.tensor_copy(out=small[:, 0:1], in_=small[:, 0:1]), reads=[("fA", c) for c in range(4)], writes=["fA"])
            branch_proj(l, 3, fA, "fA", N)
            dense_back(l, N, TT, tp, xsrc, xdst, "o_xs", xkeys, okey="xsout")

        if do_sample:
            sample_setup()
        for l in range(depth):
            layer_consts(l)
            if do_prompt:
                for c in range(4):
                    P.dve((lambda c: lambda e: e.memset(zp[:, c, 0:32], 0.0))(c), writes=[("zp", c)])
                P.dve(lambda e: e.memset(KM[:], 0.0), writes=["KM"])
                P.dve(lambda e: e.memset(vaug[:], 1.0), writes=["vaug"])
                mem_kv_prompt(l)
                P.dma("pool", "lc1", lambda e: e.dma_start(out=gpost_bc[:], in_=g_post[l:l + 1, :].partition_broadcast(128)), writes=["gpost_bc"])
                for g in range(n_groups):
                    prompt_group(l, g)
            if do_sample:
                sample_group(l)
        P.emit()
    return nc, hc


_CACHE = {}


def kernel(**inp):
    if "nc" not in _CACHE:
        _CACHE["nc"] = build()
    nc, hc = _CACHE["nc"]
    f = lambda a: np.ascontiguousarray(np.asarray(a, dtype=np.float32))
    ck = f(inp["cache_k"]).reshape(DEPTH, NPOOL * PAGE, W)
    cv = f(inp["cache_v"]).reshape(DEPTH, NPOOL * PAGE, W)
    shared = {
        "cache_k": ck, "cache_v": cv,
        "g_pre": f(inp["g_pre"]), "g_post": f(inp["g_post"]), "w_in": f(inp["w_in"]), "ln_v_gain": f(inp["ln_v_gain"]),
        "w_spatial": f(inp["w_spatial"]), "b_spatial": f(inp["b_spatial"]), "conv_w": f(inp["conv_w"]), "g_mem": f(inp["g_mem"]),
        "w_mem_kv": f(inp["w_mem_kv"]), "w_merge": f(inp["w_merge"]), "b_merge": f(inp["b_merge"]),
        "w_branch": f(inp["w_branch"]).reshape(DEPTH, 4 * W, D), "w_out": f(inp["w_out"]),
    }
    for k, v in hc.items():
        shared["c_" + k] = v
    xpr = f(inp["x_prompt"])
    xsm = f(inp["x_sample"])
    in_maps = []
    for c in range(8):
        b0 = c * NSB
        m = dict(shared)
        m["xp"] = xpr[c % 4]
        m["xs"] = np.ascontiguousarray(xsm[b0:b0 + NSB].transpose(1, 0, 2).reshape(NS, D))
        m["cmk"] = f(inp["cache_mem_k"])[:, b0:b0 + NSB].reshape(DEPTH, NSB, 256, W)
        m["cmv"] = f(inp["cache_mem_v"])[:, b0:b0 + NSB].reshape(DEPTH, NSB, 256, W)
        m["sconv"] = f(inp["state_conv"])[:, b0:b0 + NSB]
        m["ptab"] = np.ascontiguousarray(np.asarray(inp["page_table"], dtype=np.int32)[b0:b0 + NSB]).reshape(1, NSB * NPAGES)
        m["memp"] = f(inp["mem_prompt"])[c % 4]
        in_maps.append(m)
    res = run_bass_kernel_spmd(nc, in_maps, core_ids=list(range(8)))
    R = res.results
    y_prompt = np.stack([R[c]["yp"] for c in range(4)])
    tm = lambda a: a.reshape(-1, 4, NSB, a.shape[-1])
    y_sample = np.concatenate([R[c]["ys"].reshape(4, NSB, D).transpose(1, 0, 2) for c in range(8)], axis=0)
    nkp = np.stack([R[c]["nkp"] for c in range(4)], axis=1).reshape(DEPTH, 4, SEQ, 8, 64)
    nvp = np.stack([R[c]["nvp"] for c in range(4)], axis=1).reshape(DEPTH, 4, SEQ, 8, 64)
    ncp = np.stack([R[c]["ncp"] for c in range(4)], axis=1)
    nmk = np.stack([R[c]["nmk"] for c in range(4)], axis=1).reshape(DEPTH, 4, 256, 4, 128)
    nmv = np.stack([R[c]["nmv"] for c in range(4)], axis=1).reshape(DEPTH, 4, 256, 4, 128)
    nks = np.concatenate([R[c]["nks"].reshape(DEPTH, 4, NSB, W).transpose(0, 2, 1, 3) for c in range(8)], axis=1).reshape(DEPTH, 128, 4, 8, 64)
    nvs = np.concatenate([R[c]["nvs"].reshape(DEPTH, 4, NSB, W).transpose(0, 2, 1, 3) for c in range(8)], axis=1).reshape(DEPTH, 128, 4, 8, 64)
    ncs = np.concatenate([R[c]["ncs"] for c in range(8)], axis=1)
    nvn = np.concatenate([R[c]["nvn"].reshape(DEPTH, 4, NSB, W).transpose(0, 2, 1, 3) for c in range(8)], axis=1)
    out = (y_prompt, y_sample, nkp, nvp, ncp, nmk, nmv, nks, nvs, ncs, nvn)
    return tuple(np.ascontiguousarray(o, dtype=np.float32) for o in out)
```

```python
import contextlib
import os
STAGE = int(os.environ.get('KSTAGE', '9'))
SUB = int(os.environ.get('KSUB', '99'))
FIN = int(os.environ.get('KFIN', '99'))
KC = int(os.environ.get('KC', '99'))
KR = int(os.environ.get('KR', '99'))
PSRR = int(os.environ.get('PSRR', '1'))
NDB = int(os.environ.get('NDB', '4'))
XQ = os.environ.get('XQ', 'sp')
import numpy as np
import concourse.bass as bass
import concourse.mybir as mybir
from concourse.bass_utils import run_bass_kernel_spmd

F32 = mybir.dt.float32
BF16 = mybir.dt.bfloat16
I32 = mybir.dt.int32
ALU = mybir.AluOpType
AF = mybir.ActivationFunctionType
AX = mybir.AxisListType

ENGS = ["pe", "act", "dve", "pool", "sp"]
NEG = -30000.0
EPS = 1e-6


class Op:
    __slots__ = ("eng", "fn", "waits", "needs_inc", "val", "dma_slot", "dma_val")

    def __init__(self, eng, fn, dma_slot=None):
        self.eng = eng
        self.fn = fn
        self.waits = []
        self.needs_inc = False
        self.val = None
        self.dma_slot = dma_slot
        self.dma_val = None


import types


def _freeze(fn):
    if fn.__closure__ is None:
        return fn
    cells = []
    for c in fn.__closure__:
        try:
            cells.append(types.CellType(c.cell_contents))
        except ValueError:
            cells.append(c)
    return types.FunctionType(fn.__code__, fn.__globals__, fn.__name__, fn.__defaults__, tuple(cells))


class Prog:
    def __init__(self, nc, same_engine_sync=True):
        self.nc = nc
        self.ops = {e: [] for e in ENGS}
        self.last_w = {}
        self.readers = {}
        self.same_engine_sync = same_engine_sync
        self.dma_slots = {}

    def op(self, eng, fn, reads=(), writes=(), dma=None):
        o = Op(eng, _freeze(fn), dma_slot=dma)
        deps = []
        for r in reads:
            w = self.last_w.get(r)
            if w is not None:
                deps.append((w, "raw"))
            if PSRR and isinstance(r, str) and r.startswith("ps") and r[2:].isdigit():
                lastrd = {}
                for rd in self.readers.get(r, ()):
                    if rd.eng != eng:
                        lastrd[rd.eng] = rd
                for rd in lastrd.values():
                    deps.append((rd, "rar"))
        for wkey in writes:
            w = self.last_w.get(wkey)
            if w is not None:
                deps.append((w, "waw"))
            lastrd = {}
            for rd in self.readers.get(wkey, ()):
                if rd.dma_slot is not None:
                    deps.append((rd, "war"))
                else:
                    lastrd[rd.eng] = rd
            for rd in lastrd.values():
                deps.append((rd, "war"))
        seen = set()
        for d, kind in deps:
            if d is o or id(d) in seen:
                continue
            seen.add(id(d))
            if d.eng == o.eng and d.dma_slot is None and o.dma_slot is None:
                if o.eng == "pe" or not self.same_engine_sync:
                    continue
            o.waits.append(d)
            if d.dma_slot is None:
                d.needs_inc = True
        if dma is not None:
            st = self.dma_slots.setdefault(dma, [0, None])
            if st[1] is not None:
                o.waits.append(st[1])
            st[0] += 16
            o.dma_val = st[0]
            st[1] = o
        for r in reads:
            self.readers.setdefault(r, []).append(o)
        for wkey in writes:
            self.last_w[wkey] = o
            self.readers[wkey] = []
        self.ops[eng].append(o)
        return o

    def pe(self, fn, reads=(), writes=()):
        return self.op("pe", fn, reads, writes)

    def act(self, fn, reads=(), writes=()):
        return self.op("act", fn, reads, writes)

    def dve(self, fn, reads=(), writes=()):
        return self.op("dve", fn, reads, writes)

    def pool(self, fn, reads=(), writes=()):
        return self.op("pool", fn, reads, writes)

    def dma(self, eng, slot, fn, reads=(), writes=()):
        return self.op(eng, fn, reads, writes, dma=slot)

    def emit(self):
        nc = self.nc
        for e in ENGS:
            c = 0
            for o in self.ops[e]:
                if o.dma_slot is None and o.needs_inc:
                    c += 1
                    o.val = c
        slots = sorted(self.dma_slots.keys(), key=str)
        with contextlib.ExitStack() as es:
            esem = {e: es.enter_context(nc.semaphore("s_" + e)) for e in ENGS}
            dsem = {s: es.enter_context(nc.semaphore("d_%d" % i)) for i, s in enumerate(slots)}
            es.enter_context(nc.allow_non_contiguous_dma(reason="small strided parameter / layout DMAs"))
            block = es.enter_context(nc.Block())

            def run(ename, eng):
                waited = {}
                for o in self.ops[ename]:
                    for d in o.waits:
                        if d.dma_slot is not None:
                            sem, v = dsem[d.dma_slot], d.dma_val
                        else:
                            sem, v = esem[d.eng], d.val
                        k = id(sem)
                        if waited.get(k, 0) >= v:
                            continue
                        waited[k] = v
                        eng.wait_ge(sem, v)
                    ins = o.fn(eng)
                    if o.dma_slot is not None:
                        ins.then_inc(dsem[o.dma_slot], 16)
                    elif o.needs_inc:
                        ins.then_inc(esem[ename], 1)
                if ename == "sp":
                    for s in slots:
                        st = self.dma_slots[s]
                        if waited.get(id(dsem[s]), 0) < st[0]:
                            eng.wait_ge(dsem[s], st[0])

            @block.tensor
            def _(eng):
                run("pe", eng)

            @block.scalar
            def _(eng):
                run("act", eng)

            @block.vector
            def _(eng):
                run("dve", eng)

            @block.gpsimd
            def _(eng):
                run("pool", eng)

            @block.sync
            def _(eng):
                run("sp", eng)


D = 1024
SEQ = 4096
DEPTH = 2
NSB = 16
NS = 64
W = 512
NPARTS = 13
NPOOL = 2560
PAGE = 128
NPAGES = 16


def host_consts():
    c = {}
    c["ident"] = np.eye(128, dtype=np.float32)
    k = np.arange(128)[:, None, None]
    o = np.arange(4)[None, :, None]
    q = np.arange(512)[None, None, :]
    c["tri"] = np.where(q >= o * 128 + k, 0.0, NEG).astype(np.float32)
    n = np.arange(16)[:, None]
    key = np.arange(SEQ)[None, :]
    c["kind"] = (key // 256 == n).astype(np.float32)
    half = 8
    inv = np.power(np.float32(500000.0), -np.arange(half, dtype=np.float32) * np.float32(2.0 / 16)).astype(np.float32)
    pos = np.arange(SEQ, dtype=np.float32)
    ang = (pos[:, None] * inv[None, :]).astype(np.float32)
    cs = np.cos(ang).astype(np.float32).reshape(32, 128, 8).transpose(1, 0, 2)
    sn = np.sin(ang).astype(np.float32).reshape(32, 128, 8).transpose(1, 0, 2)
    c["rope_p"] = np.ascontiguousarray(np.stack([cs, sn], axis=2))
    pos_s = (2048 + np.arange(4, dtype=np.float32))
    ang_s = (pos_s[:, None] * inv[None, :]).astype(np.float32)
    cs_s = np.repeat(np.cos(ang_s).astype(np.float32), 16, axis=0)
    sn_s = np.repeat(np.sin(ang_s).astype(np.float32), 16, axis=0)
    rs = np.zeros((128, 2, 8), np.float32)
    rs[:64, 0] = cs_s
    rs[:64, 1] = sn_s
    c["rope_s"] = rs
    p = np.arange(64)
    dm = np.zeros((128, 4, 16), np.float32)
    for t in range(4):
        for blp in range(16):
            dm[:64, t, blp] = ((p % 16) == blp) & ((p // 16) <= t)
    c["dmask"] = dm
    r = np.arange(32)
    md = np.zeros((128, 512), np.float32)
    md[:32] = ((r // 4)[:, None] == (np.arange(512) // 64)[None, :])
    c["mdiag"] = md
    rsel = np.zeros((128, 16, 64), np.float32)
    for bl in range(16):
        for rr in range(32):
            rsel[rr, bl, (rr % 4) * 16 + bl] = 1.0
    c["rsel"] = rsel
    cown = np.zeros((128, 4), np.float32)
    cown[:32] = np.where(np.arange(4)[None, :] <= (r % 4)[:, None], 0.0, NEG)
    c["cown"] = cown
    selT = np.zeros((128, 16, 4), np.float32)
    for bl in range(16):
        for t in range(4):
            selT[t * 16 + bl, bl, t] = 1.0
    c["selT"] = selT
    r16 = np.arange(16)
    mdm = np.zeros((128, 512), np.float32)
    mdm[:16] = ((r16 // 4)[:, None] == (np.arange(512) // 128)[None, :])
    c["mdiagm"] = mdm
    rselm = np.zeros((128, 16, 64), np.float32)
    for bl in range(16):
        for rr in range(16):
            rselm[rr, bl, (rr % 4) * 16 + bl] = 1.0
    c["rselm"] = rselm
    iot = np.zeros((128, 1), np.float32)
    iot[:, 0] = np.arange(128)
    c["iota"] = iot
    pp = np.arange(128)
    qm = np.zeros((128, 4, 8), np.float32)
    for cc in range(4):
        for h in range(8):
            qm[:, cc, h] = (h == 2 * cc + pp // 64)
    c["qmask"] = qm
    mm = np.zeros((128, 4, 4), np.float32)
    for cc in range(4):
        mm[:, cc, cc] = 1.0
    c["mmask"] = mm
    bi = np.zeros((128, 16, 8), np.float32)
    for j in range(16):
        bi[:, j, j // 2] = 1.0 / 256
    c["blkind"] = bi
    ob = np.full((128, 16, 64), NEG, np.float32)
    for rr in range(32):
        t = rr % 4
        for bl in range(16):
            for t2 in range(t + 1):
                ob[rr, bl, t2 * 16 + bl] = 0.0
    c["ownb"] = ob
    return c


def build(do_prompt=True, do_sample=True, n_groups=8, depth=DEPTH):
    nc = bass.Bass("TRN2", target_bir_lowering=False)
    P = Prog(nc, same_engine_sync=bool(int(os.environ.get('KSES', '1'))))

    def din(name, shape, dt=F32):
        return nc.dram_tensor(name, list(shape), dt, kind="ExternalInput").ap()

    def dout(name, shape, dt=F32):
        return nc.dram_tensor(name, list(shape), dt, kind="ExternalOutput").ap()

    def dscr(name, shape, dt):
        return nc.dram_tensor(name, list(shape), dt).ap()

    xp = din("xp", [SEQ, D])
    xs = din("xs", [NS, D])
    cache_k = din("cache_k", [DEPTH, NPOOL * PAGE, W])
    cache_v = din("cache_v", [DEPTH, NPOOL * PAGE, W])
    cmk = din("cmk", [DEPTH, NSB, 256, W])
    cmv = din("cmv", [DEPTH, NSB, 256, W])
    sconv = din("sconv", [DEPTH, NSB, 2, W])
    ptab = din("ptab", [1, NSB * NPAGES], I32)
    memp = din("memp", [256, D])
    g_pre = din("g_pre", [DEPTH, D])
    g_post = din("g_post", [DEPTH, D])
    w_in = din("w_in", [DEPTH, D, NPARTS * W])
    ln_v_gain = din("ln_v_gain", [DEPTH, W])
    w_spatial = din("w_spatial", [DEPTH, 4, 128, 128])
    b_spatial = din("b_spatial", [DEPTH, 4, 128])
    conv_w = din("conv_w", [DEPTH, 3, W])
    g_mem = din("g_mem", [DEPTH, D])
    w_mem_kv = din("w_mem_kv", [DEPTH, D, D])
    w_merge = din("w_merge", [DEPTH, D, 4 * D])
    b_merge = din("b_merge", [DEPTH, 4 * D])
    w_branch = din("w_branch", [DEPTH, 4 * W, D])
    w_out = din("w_out", [DEPTH, D, D])
    hc = host_consts()
    cin = {k: din("c_" + k, v.shape) for k, v in hc.items()}

    yp = dout("yp", [SEQ, D])
    ys = dout("ys", [NS, D])
    nkp = dout("nkp", [DEPTH, SEQ, W])
    nvp = dout("nvp", [DEPTH, SEQ, W])
    ncp = dout("ncp", [DEPTH, 2, W])
    nmk = dout("nmk", [DEPTH, 256, W])
    nmv = dout("nmv", [DEPTH, 256, W])
    nks = dout("nks", [DEPTH, NS, W])
    nvs = dout("nvs", [DEPTH, NS, W])
    ncs = dout("ncs", [DEPTH, NSB, 2, W])
    nvn = dout("nvn", [DEPTH, NS, W])

    w_in_b = dscr("w_in_b", [DEPTH, NPARTS, 128, 8 * W], BF16)
    w_merge_b = dscr("w_merge_b", [DEPTH, 8, 128, 8 * W], BF16)
    w_branch_b = dscr("w_branch_b", [DEPTH, 4, 128, 4 * D], BF16)
    w_out_b = dscr("w_out_b", [DEPTH, 2, 128, 8 * W], BF16)
    w_mem_b = dscr("w_mem_b", [DEPTH, 2, 128, 8 * W], BF16)
    x1p = dscr("x1p", [SEQ, D], F32)
    x1s = dscr("x1s", [NS, D], F32)
    KT_d = dscr("KT_d", [8, 80, SEQ], BF16)
    VA_d = dscr("VA_d", [SEQ, 4 * 192], BF16)

    es = contextlib.ExitStack()
    with es:
        def T(name, shape, dt):
            return es.enter_context(nc.sbuf_tensor(name, list(shape), dt))

        ps = [es.enter_context(nc.psum_tensor("ps%d" % i, [128, 512], F32)) for i in range(8)]
        PSK = ["ps%d" % i for i in range(8)]

        ident = T("ident", [128, 128], F32)
        identb = T("identb", [128, 128], BF16)
        trib = T("trib", [128, 4, 512], BF16)
        ones_f = T("ones_f", [128, 128], F32)
        ones_b = T("ones_b", [128, 128], BF16)
        rope_p = T("rope_p", [128, 32, 2, 8], F32)
        rope_s = T("rope_s", [128, 2, 8], F32)
        dmask = T("dmask", [128, 4, 16], F32)
        mdiag = T("mdiag", [128, 512], F32)
        rsel = T("rsel", [128, 16, 64], F32)
        cown = T("cown", [128, 4], F32)
        selT = T("selT", [128, 16, 4], F32)
        mdiagm = T("mdiagm", [128, 512], F32)
        rselm = T("rselm", [128, 16, 64], F32)
        iota = T("iota", [128, 1], F32)
        epsc = T("epsc", [128, 1], F32)
        qmask = T("qmask", [128, 4, 8], F32)
        mmask = T("mmask", [128, 4, 4], F32)
        blkind = T("blkind", [128, 16, 8], F32)
        ownb = T("ownb", [128, 16, 64], F32)

        cq = [0]

        def cload(dst, src, key):
            cq[0] += 1
            P.dma("pool", "c%d" % (cq[0] % 4), lambda e: e.dma_start(out=dst, in_=src), writes=[key])

        cload(ident[:], cin["ident"], "ident")
        cload(rope_p[:], cin["rope_p"], "rope_p")
        cload(rope_s[:], cin["rope_s"], "rope_s")
        cload(dmask[:], cin["dmask"], "dmask")
        cload(mdiag[:], cin["mdiag"], "mdiag")
        cload(rsel[:], cin["rsel"], "rsel")
        cload(cown[:], cin["cown"], "cown")
        cload(selT[:], cin["selT"], "selT")
        cload(mdiagm[:], cin["mdiagm"], "mdiagm")
        cload(rselm[:], cin["rselm"], "rselm")
        cload(iota[:], cin["iota"], "iota")
        cload(qmask[:], cin["qmask"], "qmask")
        cload(mmask[:], cin["mmask"], "mmask")
        cload(blkind[:], cin["blkind"], "blkind")
        cload(ownb[:], cin["ownb"], "ownb")
        P.dma("pool", "c0", lambda e: e.dma_start(out=trib[:], in_=cin["tri"]), writes=["trib"])
        P.dve(lambda e: e.tensor_copy(out=identb[:], in_=ident[:]), reads=["ident"], writes=["identb"])
        P.dve(lambda e: e.memset(ones_f[:], 1.0), writes=["ones_f"])
        P.dve(lambda e: e.memset(ones_b[:], 1.0), writes=["ones_b"])
        P.dve(lambda e: e.memset(epsc[:], EPS), writes=["epsc"])
        for h in range(8):
            for q4 in range(4):
                P.dma("pool", "c1", (lambda h, q4: lambda e: e.dma_start(out=KT_d[h, 64:80, q4 * 1024:(q4 + 1) * 1024], in_=cin["kind"][:, q4 * 1024:(q4 + 1) * 1024]))(h, q4), writes=["KTind"])
        def _va_ones():
          P.dve(lambda e: e.memset(vab[0][:], 1.0), writes=["vab0"])
          for c in range(4):
            P.dma("pool", "c2", (lambda c: lambda e: e.dma_start(
                out=VA_d.rearrange("(j p) f -> p j f", p=128)[:, :, c * 192 + 64:c * 192 + 128], in_=vab[0][:, :, 0:64]))(c),
                reads=["vab0"], writes=["VAones"])

        wq = [0]

        def wconv(dst, src, key):
            wq[0] += 1
            P.dma("pool", "wc%d" % (wq[0] % 4), lambda e: e.dma_start(out=dst, in_=src), writes=[key])

        def kpc(ap2d):
            return ap2d.rearrange("(k p) c -> p k c", p=128)

        for l in range(depth):
            for j in range(NPARTS):
                wconv(w_in_b[l, j].rearrange("p (k c) -> p k c", k=8), kpc(w_in[l, :, j * W:(j + 1) * W]), ("w_in_b", l, j))
            for n in range(4):
                for hf in range(2):
                    wconv(w_merge_b[l, n * 2 + hf].rearrange("p (k c) -> p k c", k=8), kpc(w_merge[l, :, n * D + hf * W:n * D + (hf + 1) * W]), ("w_merge_b", l, n, hf))
                wconv(w_branch_b[l, n].rearrange("p (k c) -> p k c", k=4), kpc(w_branch[l, n * W:(n + 1) * W, :]), ("w_branch_b", l, n))
            for hf in range(2):
                wconv(w_out_b[l, hf].rearrange("p (k c) -> p k c", k=8), kpc(w_out[l, :, hf * W:(hf + 1) * W]), ("w_out_b", l, hf))
                wconv(w_mem_b[l, hf].rearrange("p (k c) -> p k c", k=8), kpc(w_mem_kv[l, :, hf * W:(hf + 1) * W]), ("w_mem_b", l, hf))

        NR = 3
        ring = [T("wr%d" % i, [128, 8, 512], BF16) for i in range(NR)]
        wcount = [0]

        def wload(src_ap_fn, srckey):
            i = wcount[0] % NR
            wcount[0] += 1
            key = "wr%d" % i
            P.dma("sp", "w%d" % i, lambda e: e.dma_start(out=src_ap_fn[0](ring[i]), in_=src_ap_fn[1]), reads=[srckey], writes=[key])
            return ring[i], key

        flat = (lambda r: r[:].rearrange("p k c -> p (k c)"))

        def unit_in(l, j):
            return (flat, w_in_b[l, j]), ("w_in_b", l, j)

        def unit_merge(l, n, hf):
            return (flat, w_merge_b[l, n * 2 + hf]), ("w_merge_b", l, n, hf)

        def unit_branch(l, n):
            return (flat, w_branch_b[l, n]), ("w_branch_b", l, n)

        def unit_out(l, hf):
            return (flat, w_out_b[l, hf]), ("w_out_b", l, hf)

        def unit_mem(l, hf):
            return (flat, w_mem_b[l, hf]), ("w_mem_b", l, hf)

        xt = T("xt", [128, D], F32)
        junk = T("junk", [128, D], BF16)
        st = T("st", [128, 8], F32)
        gpre_bc = T("gpre_bc", [128, D], F32)
        gpost_bc = T("gpost_bc", [128, D], F32)
        gln_bc = T("gln_bc", [128, W], F32)
        bm_col = T("bm_col", [128, 32], F32)
        cw_col = T("cw_col", [128, 4, 3], F32)
        hT = T("hT", [128, 8, 512], BF16)
        yacc = T("yacc", [128, 8, 512], BF16)
        fA = T("fA", [128, 4, 512], BF16)
        fB = T("fB", [128, 4, 512], BF16)
        fC = T("fC", [128, 4, 512], BF16)
        fT = T("fT", [128, 512], BF16)
        gT = T("gT", [128, 512], BF16)
        tmpf = T("tmpf", [128, 512], F32)
        zp = T("zp", [128, 4, 32 + 512], F32)
        cacc = T("cacc", [128, 512], F32)
        vn_b = T("vn_b", [128, 4, 512], BF16)
        vn_f = T("vn_f", [128, 512], F32)
        bnst = T("bnst", [128, 8], F32)
        wsT = T("wsT", [128, 4, 128], BF16)
        ws_nat = T("ws_nat", [128, 4, 128], F32)
        bsp_row = T("bsp_row", [1, 4, 128], BF16)
        bsp_f = T("bsp_f", [1, 4, 128], F32)
        ws64 = T("ws64", [128, 4, 64], BF16)
        w4bc = T("w4bc", [128, 4, 4], F32)
        bsp64 = T("bsp64", [1, 4, 64], BF16)
        qa = T("qa", [128, 4, 8, 80], F32)
        kst = T("kst", [128, 512], F32)
        vst = T("vst", [128, 512], F32)
        lqq = T("lqq", [128, 4, 8], F32)
        ktst = T("ktst", [64, 8, 128], BF16)
        vaug = T("vaug", [128, 4, 192], BF16)
        KM = T("KM", [128, 4, 128], BF16)
        qT4 = T("qT4", [128, 4, 128], BF16)
        gm = T("gm", [128, 8, 16], F32)
        m8 = T("m8", [128, 8, 8], F32)
        selm = T("selm", [128, 8, 16], F32)
        qTa = T("qTa", [80, 8, 512], BF16)
        ktb = [T("ktb0", [128, 2, SEQ], BF16)] * 2
        vab = [T("vab0", [128, 32, 192], BF16)] * 2
        pT = [T("pT%d" % i, [128, 512], BF16) for i in range(3)]
        rs = T("rs", [128, 512], F32)
        oc = fB
        memT = fA[:].rearrange("p c (a n) -> p (c a) n", a=2)
        mkT = T("mkT", [128, 4, 256], BF16)
        mv_b = T("mv_b", [128, 2, 512], BF16)
        pm = T("pm", [128, 256], F32)
        pmT = T("pmT", [128, 2, 128], BF16)
        small = T("small", [128, 16], F32)
        ncst = T("ncst", [32, 512], F32)
        ksum = T("ksum", [128, 4], F32)
        kring = T("kring", [128, 2, 512], F32)
        vring = T("vring", [128, 2, 512], F32)
        ocs = T("ocs", [64, 512], F32)
        pt_i = T("pt_i", [128, NSB * NPAGES], I32)
        pt_f = T("pt_f", [128, NSB * NPAGES], F32)
        idx_i = T("idx_i", [128, NSB * NPAGES], I32)
        qTs = T("qTs", [128, 4, 64], F32)
        kTn = T("kTn", [128, 4, 64], BF16)
        Qblk_f = T("Qblk_f", [128, 4, 32], F32)
        Qblk = T("Qblk", [128, 4, 32], BF16)
        kmT = T("kmT", [128, 4, 8], F32)
        gms = T("gms", [32, 8], F32)
        m8s = T("m8s", [32, 8], F32)
        selb = T("selb", [32, 8], F32)
        den = T("den", [32, 16], F32)
        PTo = T("PTo", [64, 32], F32)
        Po = T("Po", [32, 64], F32)
        KTs = ktb[0][:].rearrange("p a (b k) -> p (a b) k", b=2)
        blkind_b = T("blkind_b", [128, 16, 8], BF16)
        P.dve(lambda e: e.tensor_copy(out=blkind_b[:], in_=blkind[:]), reads=["blkind"], writes=["blkind_b"])
        PTb = fT[:].rearrange("p (j r) -> p j r", j=16)
        kslots = [(kring[:, 0, :], "kring0"), (kring[:, 1, :], "kring1"), (qa[:, 1, :, :].rearrange("p h d -> p (h d)")[:, 0:512], ("qa", 1))]
        vslots = [(vring[:, 0, :], "vring0"), (vring[:, 1, :], "vring1"), (qa[:, 2, :, :].rearrange("p h d -> p (h d)")[:, 0:512], ("qa", 2)),
                  (qa[:, 3, :, :].rearrange("p h d -> p (h d)")[:, 0:512], ("qa", 3))]
        kb16 = [(pT[0][:, :], "pT0"), (pT[1][:, :], "pT1")]
        vb16 = [(pT[2][:, :], "pT2"), (junk[:, 0:512], "junk")]
        Pm = [rs, cacc]
        PT = vn_f[:].rearrange("p (j r) -> p j r", j=16)
        On = ncst
        P.pool(lambda e: e.memset(small[:], 0.0), writes=["small"])
        _va_ones()

        def rstd_from_sumsq(col_in, col_out, np_, n_elem, keys_r, keys_w):
            P.act(lambda e: e.activation(out=st[:np_, col_out:col_out + 1], in_=st[:np_, col_in:col_in + 1], func=AF.Ln,
                                         scale=1.0 / n_elem, bias=epsc[:np_, 0:1]), reads=keys_r + ["epsc"], writes=keys_w)
            P.act(lambda e: e.activation(out=st[:np_, col_out:col_out + 1], in_=st[:np_, col_out:col_out + 1], func=AF.Exp,
                                         scale=-0.5), reads=keys_w, writes=keys_w)

        def norm_rows_to_T(src_rows_ap, np_, gbc, gkey, dstT, dst_key, col0, xkeys=()):
            P.dma(XQ, "xin", lambda e: e.dma_start(out=xt[:np_, :], in_=src_rows_ap), reads=list(xkeys), writes=["xt"])
            P.act(lambda e: e.activation(out=junk[:np_, :], in_=xt[:np_, :], func=AF.Square, accum_out=st[:np_, 0:1]),
                  reads=["xt"], writes=["junk", "st0"])
            rstd_from_sumsq(0, 1, np_, D, ["st0"], ["st1"])
            P.dve(lambda e: e.scalar_tensor_tensor(out=xt[:np_, :], in0=xt[:np_, :], scalar=st[:np_, 1:2], in1=gbc[:np_, :],
                                                   op0=ALU.mult, op1=ALU.mult), reads=["xt", "st1", gkey], writes=["xt"])
            for half in range(2):
                pb = 6 + half
                for i in range(4):
                    k = half * 4 + i
                    P.pe((lambda k, i, pb: lambda e: e.transpose(out=ps[pb][:, i * 128:i * 128 + np_], in_=xt[:np_, k * 128:(k + 1) * 128],
                                                                 identity=ident[:np_, :np_]))(k, i, pb),
                         reads=["xt", "ident"], writes=[PSK[pb]])
                src = ps[pb][:].rearrange("p (a b) -> p a b", a=4)[:, :, 0:np_]
                dst = dstT[:, half * 4:(half + 1) * 4, col0:col0 + np_]
                if half == 0:
                    P.act((lambda src, dst: lambda e: e.copy(out=dst, in_=src))(src, dst), reads=[PSK[pb]], writes=[dst_key])
                else:
                    P.dve((lambda src, dst: lambda e: e.tensor_copy(out=dst, in_=src))(src, dst), reads=[PSK[pb]], writes=[dst_key])

        dps = [0]

        def dense_bank():
            dps[0] = (dps[0] + 1) % NDB
            return dps[0]

        def proj_F(wt, wkey, c, N, src=None, srckey="hT", nk=8, cols=None):
            b = dense_bank()
            s = hT if src is None else src
            for k in range(nk):
                lw = wt[:, k, c * 128:(c + 1) * 128] if cols is None else cols(k)
                P.pe((lambda k, lw: lambda e: e.matmul(ps[b][:, 0:N], lhsT=lw, rhs=s[:, k, 0:N], start=(k == 0), stop=(k == nk - 1)))(k, lw),
                     reads=[wkey, srckey], writes=[PSK[b]])
            return b

        def proj_T(wt, wkey, t, tp, src=None, srckey="hT"):
            b = dense_bank()
            s = hT if src is None else src
            for k in range(8):
                P.pe((lambda k: lambda e: e.matmul(ps[b][:tp, 0:512], lhsT=s[:, k, t * tp:(t + 1) * tp], rhs=wt[:, k, :],
                                                   start=(k == 0), stop=(k == 7)))(k),
                     reads=[wkey, srckey], writes=[PSK[b]])
            return b

        def rope(b, tp, tab, jt, dst3, scale, keys_w):
            src = ps[b][:tp, :].rearrange("p (h d) -> p h d", h=8)
            if jt is None:
                cs = tab[:tp, 0, :].unsqueeze(1).to_broadcast([tp, 8, 8])
                sn = tab[:tp, 1, :].unsqueeze(1).to_broadcast([tp, 8, 8])
            else:
                cs = tab[:tp, jt, 0, :].unsqueeze(1).to_broadcast([tp, 8, 8])
                sn = tab[:tp, jt, 1, :].unsqueeze(1).to_broadcast([tp, 8, 8])
            t1 = tmpf[:tp, 0:64].rearrange("p (h d) -> p h d", h=8)
            t2 = tmpf[:tp, 64:128].rearrange("p (h d) -> p h d", h=8)
            x1 = src[:, :, 0:8]
            x2 = src[:, :, 8:16]
            rk = [PSK[b], "rope"]
            if KR < 1:
                return
            P.act(lambda e: e.mul(out=dst3[:, :, 16:64], in_=src[:, :, 16:64], mul=scale), reads=[PSK[b]], writes=keys_w)
            if KR < 2:
                return
            P.dve(lambda e: e.tensor_tensor(out=t1, in0=x1, in1=cs, op=ALU.mult), reads=rk, writes=["tmpf"])
            P.dve(lambda e: e.tensor_tensor(out=t2, in0=x2, in1=sn, op=ALU.mult), reads=rk, writes=["tmpf"])
            P.dve(lambda e: e.tensor_tensor(out=dst3[:, :, 0:8], in0=t1, in1=t2, op=ALU.subtract), reads=["tmpf"], writes=keys_w)
            if KR < 3:
                return
            P.dve(lambda e: e.tensor_tensor(out=t1, in0=x2, in1=cs, op=ALU.mult), reads=rk, writes=["tmpf"])
            P.dve(lambda e: e.tensor_tensor(out=t2, in0=x1, in1=sn, op=ALU.mult), reads=rk, writes=["tmpf"])
            P.dve(lambda e: e.tensor_tensor(out=dst3[:, :, 8:16], in0=t1, in1=t2, op=ALU.add), reads=["tmpf"], writes=keys_w)
            if scale != 1.0:
                P.dve(lambda e: e.tensor_scalar(out=dst3[:, :, 0:16], in0=dst3[:, :, 0:16], scalar1=scale, scalar2=None, op0=ALU.mult),
                      reads=keys_w, writes=keys_w)

        def silu_mul(b, N, dst, dkey, other, okey):
            P.act(lambda e: e.activation(out=fT[:, 0:N], in_=ps[b][:, 0:N], func=AF.Silu), reads=[PSK[b]], writes=["fT"])
            P.pool(lambda e: e.tensor_tensor(out=dst, in0=fT[:, 0:N], in1=other, op=ALU.mult), reads=["fT", okey], writes=[dkey])

        def branch_proj(l, n, brT, brkey, N):
            wm0, km0 = wload(*unit_merge(l, n, 0))
            wm1, km1 = wload(*unit_merge(l, n, 1))
            wb, kb = wload(*unit_branch(l, n))
            wbv = wb[:].rearrange("p (a b) c -> p a (b c)", a=4)
            for ocn in range(8):
                wm, km = (wm0, km0) if ocn < 4 else (wm1, km1)
                bg = proj_F(wm, km, ocn % 4, N)
                P.act((lambda bg, ocn: lambda e: e.activation(out=gT[:, 0:N], in_=ps[bg][:, 0:N], func=AF.Sigmoid,
                                                              bias=bm_col[:, n * 8 + ocn:n * 8 + ocn + 1]))(bg, ocn),
                      reads=[PSK[bg], "bm_col"], writes=["gT"])
                bp = proj_F(wb, kb, ocn, N, src=brT, srckey=brkey, nk=4, cols=(lambda ocn: lambda k: wbv[:, k, ocn * 128:(ocn + 1) * 128])(ocn))
                if n == 0:
                    P.dve((lambda bp, ocn: lambda e: e.tensor_tensor(out=yacc[:, ocn, 0:N], in0=ps[bp][:, 0:N], in1=gT[:, 0:N], op=ALU.mult))(bp, ocn),
                          reads=[PSK[bp], "gT"], writes=[("yacc", ocn)])
                else:
                    P.dve((lambda bp: lambda e: e.tensor_tensor(out=fT[:, 0:N], in0=ps[bp][:, 0:N], in1=gT[:, 0:N], op=ALU.mult))(bp),
                          reads=[PSK[bp], "gT"], writes=["fT"])
                    P.pool((lambda ocn: lambda e: e.tensor_tensor(out=yacc[:, ocn, 0:N], in0=yacc[:, ocn, 0:N], in1=fT[:, 0:N], op=ALU.add))(ocn),
                           reads=["fT", ("yacc", ocn)], writes=[("yacc", ocn)])

        def layer_consts(l):
            P.dma("pool", "lc0", lambda e: e.dma_start(out=gpre_bc[:], in_=g_pre[l:l + 1, :].partition_broadcast(128)), writes=["gpre_bc"])
            P.dma("pool", "lc1", lambda e: e.dma_start(out=gpost_bc[:], in_=g_post[l:l + 1, :].partition_broadcast(128)), writes=["gpost_bc"])
            P.dma("pool", "lc2", lambda e: e.dma_start(out=gln_bc[:], in_=ln_v_gain[l:l + 1, :].partition_broadcast(128)), writes=["gln_bc"])
            with nc.allow_non_contiguous_dma(reason="small per-layer vectors"):
                P.dma("pool", "lc3", lambda e: e.dma_start(out=bm_col[:], in_=b_merge[l].rearrange("(a p) -> p a", p=128)), writes=["bm_col"])
                for j3 in range(3):
                    P.dma("pool", "lc0", (lambda j3: lambda e: e.dma_start(out=cw_col[:, :, j3], in_=conv_w[l, j3].rearrange("(c p) -> p c", p=128)))(j3), writes=["cw_col"])
            P.dma("pool", "lc1", lambda e: e.dma_start(out=ws_nat[:], in_=w_spatial[l].rearrange("g t s -> t g s")), writes=["ws_nat"])
            for g4 in range(4):
                P.pe((lambda g4: lambda e: e.transpose(out=ps[6][:, g4 * 128:(g4 + 1) * 128], in_=ws_nat[:, g4, :], identity=ident[:]))(g4),
                     reads=["ws_nat", "ident"], writes=[PSK[6]])
            P.act(lambda e: e.copy(out=tmpf[:, :], in_=ps[6][:, :]), reads=[PSK[6]], writes=["tmpf"])
            P.pool(lambda e: e.affine_select(out=tmpf[:].rearrange("p (g t) -> p g t", g=4), in_=tmpf[:].rearrange("p (g t) -> p g t", g=4),
                                             pattern=[[0, 4], [1, 128]], compare_op=ALU.is_ge, fill=0.0, base=0, channel_multiplier=-1),
                   reads=["tmpf"], writes=["tmpf"])
            P.dve(lambda e: e.tensor_copy(out=wsT[:].rearrange("p g t -> p (g t)"), in_=tmpf[:, :]), reads=["tmpf"], writes=["wsT"])
            P.dma("pool", "lc2", lambda e: e.dma_start(out=bsp_f[:], in_=b_spatial[l:l + 1, :, :]), writes=["bsp_f"])
            P.dve(lambda e: e.tensor_copy(out=bsp_row[:], in_=bsp_f[:]), reads=["bsp_f"], writes=["bsp_row"])
            with nc.allow_non_contiguous_dma(reason="tiny 4x4 spatial block"):
                for s in range(4):
                    for g4 in range(4):
                        P.dma("pool", "lc3", (lambda s, g4: lambda e: e.dma_start(
                            out=w4bc[s * 16:(s + 1) * 16, g4, :],
                            in_=w_spatial[l, g4, 0:4, s:s + 1].rearrange("t o -> o t").partition_broadcast(16)))(s, g4), writes=["w4bc"])
            for g4 in range(4):
                for t in range(4):
                    P.dve((lambda g4, t: lambda e: e.tensor_scalar(out=ws64[:64, g4, t * 16:(t + 1) * 16], in0=dmask[:64, t, :],
                                                                    scalar1=w4bc[:64, g4, t:t + 1], scalar2=None, op0=ALU.mult))(g4, t),
                          reads=["w4bc", "dmask"], writes=["ws64"])
            P.dve(lambda e: e.tensor_copy(out=bsp64[:].rearrange("o g (t b) -> o g t b", b=16),
                                          in_=bsp_f[0:1, :, 0:4].unsqueeze(3).to_broadcast([1, 4, 4, 16])), reads=["bsp_f"], writes=["bsp64"])

        def dense_front(l, N, TT, tp, xsrc, is_sample, g, xkeys=()):
            for t in range(TT):
                norm_rows_to_T(xsrc(t), tp, gpre_bc, "gpre_bc", hT, "hT", t * tp, xkeys)
            if SUB < 1:
                return
            w0, k0 = wload(*unit_in(l, 0))
            if os.environ.get('KW1'):
                return
            w1, k1 = wload(*unit_in(l, 1))
            w2, k2 = wload(*unit_in(l, 2))
            for c in range(4):
                b = proj_F(w0, k0, c, N)
                P.act((lambda b, c: lambda e: e.copy(out=fA[:, c, 0:N], in_=ps[b][:, 0:N]))(b, c), reads=[PSK[b]], writes=[("fA", c)])
            if FIN < 1:
                return
            for t in range(TT):
                b = proj_T(w1, k1, t, tp)
                if FIN < 2:
                    continue
                P.act((lambda b: lambda e: e.activation(out=junk[:tp, 0:512], in_=ps[b][:tp, :], func=AF.Identity, accum_out=st[:tp, 2:3]))(b),
                      reads=[PSK[b]], writes=["junk", "st2"])
                P.dve(lambda e: e.tensor_scalar(out=st[:tp, 3:4], in0=st[:tp, 2:3], scalar1=-1.0 / 512, scalar2=None, op0=ALU.mult), reads=["st2"], writes=["st3"])
                P.act((lambda b: lambda e: e.activation(out=junk[:tp, 0:512], in_=ps[b][:tp, :], func=AF.Square, bias=st[:tp, 3:4], accum_out=st[:tp, 4:5]))(b),
                      reads=[PSK[b], "st3"], writes=["junk", "st4"])
                P.act(lambda e: e.activation(out=st[:tp, 4:5], in_=st[:tp, 4:5], func=AF.Ln, scale=1.0 / 512, bias=epsc[:tp, 0:1]), reads=["st4", "epsc"], writes=["st4"])
                P.act(lambda e: e.activation(out=st[:tp, 4:5], in_=st[:tp, 4:5], func=AF.Exp, scale=-0.5), reads=["st4"], writes=["st4"])
                P.dve(lambda e: e.tensor_tensor(out=st[:tp, 2:3], in0=st[:tp, 3:4], in1=st[:tp, 4:5], op=ALU.mult), reads=["st3", "st4"], writes=["st2"])
                P.act((lambda b: lambda e: e.activation(out=vn_f[:tp, :], in_=ps[b][:tp, :], func=AF.Identity, scale=st[:tp, 4:5], bias=st[:tp, 2:3]))(b),
                      reads=[PSK[b], "st2", "st4"], writes=["vn_f"])
                P.dve(lambda e: e.tensor_tensor(out=vn_f[:tp, :], in0=vn_f[:tp, :], in1=gln_bc[:tp, :], op=ALU.mult), reads=["vn_f", "gln_bc"], writes=["vn_f"])
                P.act((lambda t: lambda e: e.copy(out=vn_b[:tp, t, :], in_=vn_f[:tp, :]))(t), reads=["vn_f"], writes=["vn_b"])
                if is_sample:
                    P.dma(XQ, "o_vn", lambda e: e.dma_start(out=nvn[l], in_=vn_f[:tp, :]), reads=["vn_f"])
            if SUB < 2:
                return
            for g4 in range(4):
                b = dense_bank()
                for t in range(TT):
                    rhs_w = ws64[:tp, g4, :] if is_sample else wsT[:, g4, :]
                    rhs_b = bsp64[0:1, g4, :] if is_sample else bsp_row[0:1, g4, :]
                    P.pe((lambda t, g4, rhs_w: lambda e: e.matmul(ps[b][:, t * tp:(t + 1) * tp], lhsT=vn_b[:tp, t, g4 * 128:(g4 + 1) * 128], rhs=rhs_w,
                                                                  start=True, stop=False))(t, g4, rhs_w),
                         reads=["vn_b", "wsT", "ws64"], writes=[PSK[b]])
                    P.pe((lambda t, rhs_b: lambda e: e.matmul(ps[b][:, t * tp:(t + 1) * tp], lhsT=ones_b[0:1, :], rhs=rhs_b, start=False, stop=True))(t, rhs_b),
                         reads=["ones_b", "bsp_row", "bsp64"], writes=[PSK[b]])
                P.dve((lambda b, g4: lambda e: e.tensor_tensor(out=fA[:, g4, 0:N], in0=ps[b][:, 0:N], in1=fA[:, g4, 0:N], op=ALU.mult))(b, g4),
                      reads=[PSK[b], ("fA", g4)], writes=[("fA", g4)])
            if SUB < 3:
                return
            for c in range(4):
                b = proj_F(w2, k2, c, N)
                silu_mul(b, N, fA[:, c, 0:N], ("fA", c), fA[:, c, 0:N], ("fA", c))
            fAk = [("fA", c) for c in range(4)]
            P.pool(lambda e: e.tensor_copy(out=small[:, 0:1], in_=small[:, 0:1]), reads=fAk, writes=["fA"])
            if SUB < 4:
                return
            branch_proj(l, 0, fA, "fA", N)
            if SUB < 5:
                return
            S0 = 32 if is_sample else 2
            sh = 16 if is_sample else 1
            w3, k3 = wload(*unit_in(l, 3))
            for c in range(4):
                b = proj_F(w3, k3, c, N)
                P.act((lambda b, c: lambda e: e.copy(out=fB[:, c, 0:N], in_=ps[b][:, 0:N]))(b, c), reads=[PSK[b]], writes=[("fB", c)])
            w4, k4 = wload(*unit_in(l, 4))
            for c in range(4):
                b = proj_F(w4, k4, c, N)
                P.act((lambda b, c: lambda e: e.copy(out=fC[:, c, 0:N], in_=ps[b][:, 0:N]))(b, c), reads=[PSK[b]], writes=[("fC", c)])
            w5, k5 = wload(*unit_in(l, 5))
            for c in range(4):
                b = proj_F(w5, k5, c, N)
                P.dve((lambda b, c: lambda e: e.tensor_tensor(out=zp[:, c, S0:S0 + N], in0=ps[b][:, 0:N], in1=fC[:, c, 0:N], op=ALU.mult))(b, c),
                      reads=[PSK[b], ("fC", c)], writes=[("zp", c)])
                P.dve((lambda c: lambda e: e.tensor_scalar(out=cacc[:, 0:N], in0=zp[:, c, 0:N], scalar1=cw_col[:, c, 0:1], scalar2=None, op0=ALU.mult))(c),
                      reads=[("zp", c), "cw_col"], writes=["cacc"])
                P.dve((lambda c: lambda e: e.scalar_tensor_tensor(out=cacc[:, 0:N], in0=zp[:, c, sh:sh + N], scalar=cw_col[:, c, 1:2], in1=cacc[:, 0:N],
                                                                   op0=ALU.mult, op1=ALU.add))(c), reads=[("zp", c), "cw_col", "cacc"], writes=["cacc"])
                P.dve((lambda c: lambda e: e.scalar_tensor_tensor(out=cacc[:, 0:N], in0=zp[:, c, 2 * sh:2 * sh + N], scalar=cw_col[:, c, 2:3], in1=cacc[:, 0:N],
                                                                   op0=ALU.mult, op1=ALU.add))(c), reads=[("zp", c), "cw_col", "cacc"], writes=["cacc"])
                P.dve((lambda c: lambda e: e.tensor_tensor(out=fB[:, c, 0:N], in0=cacc[:, 0:N], in1=fB[:, c, 0:N], op=ALU.mult))(c),
                      reads=["cacc", ("fB", c)], writes=[("fB", c)])
            if SUB < 6:
                return
            last = is_sample or (g == n_groups - 1)
            if last:
                ncol = 32 if is_sample else 2
                for c in range(4):
                    P.pe((lambda c: lambda e: e.transpose(out=ps[7][:ncol, c * 128:(c + 1) * 128], in_=zp[:, c, S0 + N - ncol:S0 + N], identity=ident[:]))(c),
                         reads=[("zp", c), "ident"], writes=[PSK[7]])
                P.act(lambda e: e.copy(out=ncst[:ncol, :], in_=ps[7][:ncol, :]), reads=[PSK[7]], writes=["ncst"])
                if is_sample:
                    for r in range(2):
                        P.dma(XQ, "o_nc", (lambda r: lambda e: e.dma_start(out=ncs[l, :, r, :], in_=ncst[r * 16:(r + 1) * 16, :]))(r), reads=["ncst"])
                else:
                    P.dma(XQ, "o_nc", lambda e: e.dma_start(out=ncp[l], in_=ncst[0:2, :]), reads=["ncst"])
            if not is_sample:
                for c in range(4):
                    P.act((lambda c: lambda e: e.copy(out=zp[:, c, 0:2], in_=zp[:, c, N:N + 2]))(c), reads=[("zp", c)], writes=[("zp", c)])
            w6, k6 = wload(*unit_in(l, 6))
            for c in range(4):
                b = proj_F(w6, k6, c, N)
                silu_mul(b, N, fB[:, c, 0:N], ("fB", c), fB[:, c, 0:N], ("fB", c))
            P.pool(lambda e: e.tensor_copy(out=small[:, 0:1], in_=small[:, 0:1]), reads=[("fB", c) for c in range(4)], writes=["fB"])
            branch_proj(l, 1, fB, "fB", N)

        def dense_back(l, N, TT, tp, xsrc, dst_rows, dslot, xkeys=(), okey="xout"):
            wo0, ko0 = wload(*unit_out(l, 0))
            wo1, ko1 = wload(*unit_out(l, 1))
            ykeys = [("yacc", i) for i in range(8)]
            for t in range(TT):
                bs = []
                for hf, (wo, ko) in enumerate(((wo0, ko0), (wo1, ko1))):
                    b = 2 + hf
                    for k in range(8):
                        P.pe((lambda k, b, wo: lambda e: e.matmul(ps[b][:tp, :], lhsT=yacc[:, k, t * tp:(t + 1) * tp], rhs=wo[:, k, :],
                                                                  start=(k == 0), stop=(k == 7)))(k, b, wo),
                             reads=[ko] + ykeys, writes=[PSK[b]])
                    P.act((lambda b, hf: lambda e: e.activation(out=junk[:tp, 0:512], in_=ps[b][:tp, :], func=AF.Square,
                                                                accum_out=st[:tp, 5 + hf:6 + hf]))(b, hf),
                          reads=[PSK[b]], writes=["junk", "st56"])
                    bs.append(b)
                P.dve(lambda e: e.tensor_tensor(out=st[:tp, 5:6], in0=st[:tp, 5:6], in1=st[:tp, 6:7], op=ALU.add), reads=["st56"], writes=["st56"])
                rstd_from_sumsq(5, 7, tp, D, ["st56"], ["st7"])
                P.dma(XQ, "xin", (lambda t: lambda e: e.dma_start(out=xt[:tp, :], in_=xsrc(t)))(t), reads=list(xkeys), writes=["xt"])
                for hf in range(2):
                    b = bs[hf]
                    P.dve((lambda b, hf: lambda e: e.scalar_tensor_tensor(out=tmpf[:tp, :], in0=ps[b][:tp, :], scalar=st[:tp, 7:8],
                                                                          in1=gpost_bc[:tp, hf * 512:(hf + 1) * 512], op0=ALU.mult, op1=ALU.mult))(b, hf),
                          reads=[PSK[b], "st7", "gpost_bc"], writes=["tmpf"])
                    P.pool((lambda hf: lambda e: e.tensor_tensor(out=xt[:tp, hf * 512:(hf + 1) * 512], in0=xt[:tp, hf * 512:(hf + 1) * 512],
                                                                  in1=tmpf[:tp, :], op=ALU.add))(hf), reads=["tmpf", "xt"], writes=["xt"])
                P.dma(XQ, dslot, (lambda t: lambda e: e.dma_start(out=dst_rows(t), in_=xt[:tp, :]))(t), reads=["xt"], writes=[(okey, l)])

        def mem_kv_prompt(l):
            P.dma("pool", "lc0", lambda e: e.dma_start(out=gpost_bc[:], in_=g_mem[l:l + 1, :].partition_broadcast(128)), writes=["gpost_bc"])
            for t in range(2):
                norm_rows_to_T(memp[t * 128:(t + 1) * 128, :], 128, gpost_bc, "gpost_bc", memT, "fA", t * 128)
            wk, kk = wload(*unit_mem(l, 0))
            wv, kv = wload(*unit_mem(l, 1))
            for t in range(2):
                b = proj_T(wk, kk, t, 128, src=memT, srckey="fA")
                P.act((lambda b: lambda e: e.copy(out=kst[:, :], in_=ps[b][:, :]))(b), reads=[PSK[b]], writes=["kst"])
                P.dma(XQ, "o_mk", (lambda t: lambda e: e.dma_start(out=nmk[l, t * 128:(t + 1) * 128, :], in_=kst[:, :]))(t), reads=["kst"])
                b = proj_T(wv, kv, t, 128, src=memT, srckey="fA")
                P.act((lambda b: lambda e: e.copy(out=vst[:, :], in_=ps[b][:, :]))(b), reads=[PSK[b]], writes=["vst"])
                P.dve((lambda t: lambda e: e.tensor_copy(out=mv_b[:, t, :], in_=vst[:, :]))(t), reads=["vst"], writes=["mv_b"])
                P.dma(XQ, "o_mv", (lambda t: lambda e: e.dma_start(out=nmv[l, t * 128:(t + 1) * 128, :], in_=vst[:, :]))(t), reads=["vst"])
            for h4 in range(4):
                b = proj_F(wk, kk, h4, 256, src=memT, srckey="fA")
                P.act((lambda b, h4: lambda e: e.copy(out=mkT[:, h4, :], in_=ps[b][:, 0:256]))(b, h4), reads=[PSK[b]], writes=["mkT"])

        def prompt_group(l, g):
            N, TT, tp = 512, 4, 128
            src = xp if l == 0 else x1p
            dst = x1p if l < depth - 1 else yp

            def xsrc(t):
                return src[(g * 4 + t) * 128:(g * 4 + t + 1) * 128, :]

            def xdst(t):
                return dst[(g * 4 + t) * 128:(g * 4 + t + 1) * 128, :]

            if STAGE < 1:
                return
            xkeys = [("xout", l - 1)] if l > 0 else []
            dense_front(l, N, TT, tp, xsrc, False, g, xkeys)
            if STAGE < 2:
                return
            w7, k7 = wload(*unit_in(l, 7))
            for t in range(TT):
                b = proj_T(w7, k7, t, tp)
                rope(b, tp, rope_p, g * 4 + t, qa[:, t, :, :], 0.125, [("qa", t)])
            if KC < 1:
                return
            w8, k8 = wload(*unit_in(l, 8))
            for t in range(TT):
                jt = g * 4 + t
                b = proj_T(w8, k8, t, tp)
                rope(b, tp, rope_p, jt, kst[:, :].rearrange("p (h d) -> p h d", h=8), 1.0, ["kst"])
                P.dma(XQ, "o_k", (lambda jt: lambda e: e.dma_start(out=nkp[l, jt * 128:(jt + 1) * 128, :], in_=kst[:, :]))(jt), reads=["kst"])
                if KC < 2:
                    continue
                P.dve((lambda t: lambda e: e.tensor_tensor(out=tmpf[:, :].rearrange("p (h d) -> p h d", h=8), in0=qa[:, t, :, 0:64],
                                                          in1=kst[:, :].rearrange("p (h d) -> p h d", h=8), op=ALU.mult))(t),
                      reads=[("qa", t), "kst"], writes=["tmpf"])
                P.dve((lambda t: lambda e: e.tensor_reduce(out=lqq[:, t, :], in_=tmpf[:, :].rearrange("p (h d) -> p h d", h=8), axis=AX.X, op=ALU.add))(t),
                      reads=["tmpf"], writes=["lqq"])
                if KC < 3:
                    continue
                for half in range(2):
                    pb = 6 + half
                    for i in range(4):
                        h = half * 4 + i
                        P.pe((lambda h, i, pb: lambda e: e.transpose(out=ps[pb][0:64, i * 128:(i + 1) * 128], in_=kst[:, h * 64:(h + 1) * 64], identity=ident[:]))(h, i, pb),
                             reads=["kst", "ident"], writes=[PSK[pb]])
                    srcp = ps[pb][0:64, :].rearrange("p (a b) -> p a b", a=4)
                    if half == 0:
                        P.act((lambda srcp: lambda e: e.copy(out=ktst[:, 0:4, :], in_=srcp))(srcp), reads=[PSK[pb]], writes=["ktst"])
                    else:
                        P.dve((lambda srcp: lambda e: e.tensor_copy(out=ktst[:, 4:8, :], in_=srcp))(srcp), reads=[PSK[pb]], writes=["ktst"])
                P.dma(XQ, "kt_w", (lambda jt: lambda e: e.dma_start(out=KT_d[:, 0:64, jt * 128:(jt + 1) * 128].rearrange("h p k -> p h k"), in_=ktst[:, :, :]))(jt),
                      reads=["ktst", "KTind"], writes=[("KT_d", g)])
                if KC < 4:
                    continue
                for c in range(4):
                    P.pe((lambda c: lambda e: e.matmul(ps[5][:, c:c + 1], lhsT=kst[:, c * 128:(c + 1) * 128], rhs=ones_f[:, 0:1],
                                                      start=True, stop=True))(c),
                         reads=["kst", "ones_f"], writes=[PSK[5]])
                if jt % 2 == 0:
                    P.dve(lambda e: e.tensor_scalar(out=ksum[:, 0:4], in0=ps[5][:, 0:4], scalar1=1.0 / 256, scalar2=None, op0=ALU.mult),
                          reads=[PSK[5]], writes=["ksum"])
                else:
                    nb = jt // 2
                    KMf = KM[:].rearrange("p c n -> p (c n)")
                    for c in range(4):
                        P.dve((lambda c, nb: lambda e: e.scalar_tensor_tensor(out=KMf[0:64, c * 128 + (2 * c) * 16 + nb:c * 128 + (2 * c) * 16 + nb + 1], in0=ps[5][0:64, c:c + 1],
                                                                               scalar=1.0 / 256, in1=ksum[0:64, c:c + 1], op0=ALU.mult, op1=ALU.add))(c, nb),
                              reads=[PSK[5], "ksum"], writes=["KM"])
                        P.dve((lambda c, nb: lambda e: e.scalar_tensor_tensor(out=KMf[64:128, c * 128 + (2 * c + 1) * 16 + nb:c * 128 + (2 * c + 1) * 16 + nb + 1],
                                                                               in0=ps[5][64:128, c:c + 1], scalar=1.0 / 256, in1=ksum[64:128, c:c + 1], op0=ALU.mult, op1=ALU.add))(c, nb),
                              reads=[PSK[5], "ksum"], writes=["KM"])
            if KC < 5:
                return
            w9, k9 = wload(*unit_in(l, 9))
            for t in range(TT):
                jt = g * 4 + t
                b = proj_T(w9, k9, t, tp)
                P.act((lambda b: lambda e: e.copy(out=vst[:, :], in_=ps[b][:, :]))(b), reads=[PSK[b]], writes=["vst"])
                if KC < 6:
                    continue
                P.dma(XQ, "o_v", (lambda jt: lambda e: e.dma_start(out=nvp[l, jt * 128:(jt + 1) * 128, :], in_=vst[:, :]))(jt), reads=["vst"])
                P.dve((lambda b: lambda e: e.tensor_copy(out=vaug[:].rearrange("p c (s d) -> p c s d", s=3)[:, :, 0:3:2, :],
                                                        in_=ps[b][:, :].rearrange("p (c s d) -> p c s d", c=4, s=2)))(b), reads=[PSK[b]], writes=["vaug"])
                P.dma(XQ, "va_w", (lambda jt: lambda e: e.dma_start(out=VA_d[jt * 128:(jt + 1) * 128, :], in_=vaug[:].rearrange("p c f -> p (c f)")))(jt),
                      reads=["vaug", "VAones"], writes=[("VA_d", g)])
            if STAGE < 3:
                return
            for t in range(TT):
                jt = g * 4 + t
                own = jt // 2
                P.act((lambda t: lambda e: e.copy(out=vn_f[:, :].rearrange("p (h d) -> p h d", h=8), in_=qa[:, t, :, 0:64]))(t), reads=[("qa", t)], writes=["vn_f"])
                for c in range(4):
                    P.pe((lambda c, t: lambda e: e.transpose(out=ps[6][:, c * 128:(c + 1) * 128], in_=vn_f[:, c * 128:(c + 1) * 128], identity=ident[:]))(c, t),
                         reads=["vn_f", "ident"], writes=[PSK[6]])
                P.act(lambda e: e.copy(out=qT4[:].rearrange("p c q -> p (c q)"), in_=ps[6][:, :]), reads=[PSK[6]], writes=["qT4"])
                for c in range(4):
                    P.pe((lambda c: lambda e: e.matmul(ps[7][:, 0:128], lhsT=qT4[:, c, :], rhs=KM[:, c, :], start=(c == 0), stop=(c == 3)))(c),
                         reads=["qT4", "KM"], writes=[PSK[7]])
                P.dve(lambda e: e.tensor_copy(out=gm[:].rearrange("p h n -> p (h n)"), in_=ps[7][:, 0:128]), reads=[PSK[7]], writes=["gm"])
                if own < 16:
                    P.dve((lambda own: lambda e: e.memset(gm[:, :, own:16], -1e30))(own), reads=["gm"], writes=["gm"])
                for h in range(8):
                    P.dve((lambda h: lambda e: e.max(out=m8[:, h, :], in_=gm[:, h, :]))(h), reads=["gm"], writes=["m8"])
                P.dve(lambda e: e.tensor_tensor(out=selm[:], in0=gm[:], in1=m8[:, :, 2:3].to_broadcast([128, 8, 16]), op=ALU.is_ge), reads=["gm", "m8"], writes=["selm"])
                P.dve(lambda e: e.tensor_scalar(out=selm[:], in0=selm[:], scalar1=-1.0, scalar2=-NEG, op0=ALU.add, op1=ALU.mult), reads=["selm"], writes=["selm"])
                P.dve((lambda t: lambda e: e.tensor_tensor(out=qa[:, t, :, 64:80], in0=selm[:], in1=lqq[:, t, :].unsqueeze(2).to_broadcast([128, 8, 16]),
                                                          op=ALU.subtract))(t), reads=["selm", "lqq", ("qa", t)], writes=[("qa", t)])
                P.dve((lambda t, own: lambda e: e.tensor_scalar(out=qa[:, t, :, 64 + own:65 + own], in0=lqq[:, t, :].unsqueeze(2), scalar1=-1.0, scalar2=None,
                                                               op0=ALU.mult))(t, own), reads=["lqq", ("qa", t)], writes=[("qa", t)])
                for half in range(2):
                    pb = 6 + half
                    for i in range(4):
                        h = half * 4 + i
                        P.pe((lambda h, i, pb, t: lambda e: e.transpose(out=ps[pb][0:80, i * 128:(i + 1) * 128], in_=qa[:, t, h, :], identity=ident[:]))(h, i, pb, t),
                             reads=[("qa", t), "ident"], writes=[PSK[pb]])
                    srcp = ps[pb][0:80, :].rearrange("p (a b) -> p a b", a=4)
                    dstp = qTa[:, half * 4:(half + 1) * 4, t * 128:(t + 1) * 128]
                    if half == 0:
                        P.act((lambda srcp, dstp: lambda e: e.copy(out=dstp, in_=srcp))(srcp, dstp), reads=[PSK[pb]], writes=["qTa"])
                    else:
                        P.dve((lambda srcp, dstp: lambda e: e.tensor_copy(out=dstp, in_=srcp))(srcp, dstp), reads=[PSK[pb]], writes=["qTa"])
            if STAGE < 4:
                return
            njt = 4 * g + 4
            L = njt * 128
            hist_k = [("KT_d", gg) for gg in range(g + 1)] + ["KTind"]
            hist_v = [("VA_d", gg) for gg in range(g + 1)] + ["VAones"]
            for c in range(4):
                sl = 0
                P.dma("sp", "ktb%d" % sl, (lambda c, sl: lambda e: e.dma_start(out=ktb[sl][0:80, :, 0:L], in_=KT_d[2 * c:2 * c + 2, :, 0:L].rearrange("h p k -> p h k")))(c, sl),
                      reads=hist_k, writes=["ktb%d" % sl])
                P.dma("sp", "vab%d" % sl, (lambda c, sl: lambda e: e.dma_start(out=vab[sl][:, 0:njt, :],
                                                                               in_=VA_d[0:L, c * 192:(c + 1) * 192].rearrange("(j p) f -> p j f", p=128)))(c, sl),
                      reads=hist_v, writes=["vab%d" % sl])
                for hh in range(2):
                    h = 2 * c + hh
                    ob = 4 + hh
                    def qk(j):
                        sb = 2 + (j % 2)
                        diag = j >= 4 * g
                        P.pe((lambda j, sb, sl, hh, h, diag: lambda e: e.matmul(ps[sb][:, :], lhsT=ktb[sl][0:80, hh, j * 128:(j + 1) * 128], rhs=qTa[0:80, h, :],
                                                                                start=True, stop=(not diag)))(j, sb, sl, hh, h, diag),
                             reads=["ktb%d" % sl, "qTa"], writes=[PSK[sb]])
                        if diag:
                            P.pe((lambda j, sb: lambda e: e.matmul(ps[sb][:, :], lhsT=identb[:], rhs=trib[:, j - 4 * g, :], start=False, stop=True))(j, sb),
                                 reads=["identb", "trib"], writes=[PSK[sb]])

                    def ex_pv(j):
                        sb = 2 + (j % 2)
                        pi = j % 3
                        P.act((lambda sb, pi: lambda e: e.activation(out=pT[pi][:, :], in_=ps[sb][:, :], func=AF.Exp))(sb, pi),
                              reads=[PSK[sb]], writes=["pT%d" % pi])
                        P.pe((lambda j, pi, sl, hh, ob: lambda e: e.matmul(ps[ob][:, :], lhsT=vab[sl][:, j, hh * 64:hh * 64 + 128], rhs=pT[pi][:, :],
                                                                           start=(j == 0), stop=(j == njt - 1)))(j, pi, sl, hh, ob),
                             reads=["vab%d" % sl, "pT%d" % pi], writes=[PSK[ob]])

                    qk(0)
                    for j in range(njt):
                        if j + 1 < njt:
                            qk(j + 1)
                        ex_pv(j)
                    if hh == 0:
                        P.dve((lambda ob: lambda e: e.reciprocal(out=rs[64:128, :], in_=ps[ob][64:128, :]))(ob), reads=[PSK[ob]], writes=["rs"])
                        P.dve((lambda ob, c: lambda e: e.tensor_tensor(out=oc[0:64, c, :], in0=ps[ob][0:64, :], in1=rs[64:128, :], op=ALU.mult))(ob, c),
                              reads=[PSK[ob], "rs"], writes=[("fB", c)])
                    else:
                        P.dve((lambda ob: lambda e: e.reciprocal(out=rs[0:64, :], in_=ps[ob][0:64, :]))(ob), reads=[PSK[ob]], writes=["rs"])
                        P.dve((lambda ob, c: lambda e: e.tensor_tensor(out=oc[64:128, c, :], in0=ps[ob][64:128, :], in1=rs[0:64, :], op=ALU.mult))(ob, c),
                              reads=[PSK[ob], "rs"], writes=[("fB", c)])
            if STAGE < 5:
                return
            w10, k10 = wload(*unit_in(l, 10))
            for c in range(4):
                b = proj_F(w10, k10, c, N)
                silu_mul(b, N, fA[:, c, 0:N], ("fA", c), oc[:, c, 0:N], ("fB", c))
            P.pool(lambda e: e.tensor_copy(out=small[:, 0:1], in_=small[:, 0:1]), reads=[("fA", c) for c in range(4)], writes=["fA"])
            branch_proj(l, 2, fA, "fA", N)
            if STAGE < 6:
                return
            w11, k11 = wload(*unit_in(l, 11))
            for c in range(4):
                b = proj_F(w11, k11, c, N)
                P.act((lambda b, c: lambda e: e.copy(out=fC[:, c, 0:N], in_=ps[b][:, 0:N]))(b, c), reads=[PSK[b]], writes=[("fC", c)])
            sc = 128.0 ** -0.5
            for h4 in range(4):
                ob = 4 + (h4 % 2)
                db = 6 + (h4 % 2)
                for mt in range(2):
                    P.pe((lambda mt, h4: lambda e: e.matmul(ps[2 + mt][:, :], lhsT=mkT[:, h4, mt * 128:(mt + 1) * 128], rhs=fC[:, h4, 0:512], start=True, stop=True))(mt, h4),
                         reads=[("fC", h4), "mkT"], writes=[PSK[2 + mt]])
                for mt in range(2):
                    P.act((lambda mt: lambda e: e.activation(out=pT[mt][:, :], in_=ps[2 + mt][:, :], func=AF.Exp, scale=sc))(mt),
                          reads=[PSK[2 + mt]], writes=["pT%d" % mt])
                    P.pe((lambda mt, h4, ob: lambda e: e.matmul(ps[ob][:, :], lhsT=mv_b[:, mt, h4 * 128:(h4 + 1) * 128], rhs=pT[mt][:, :], start=(mt == 0), stop=(mt == 1)))(mt, h4, ob),
                         reads=["mv_b", "pT%d" % mt], writes=[PSK[ob]])
                    P.pe((lambda mt, db: lambda e: e.matmul(ps[db][:, :], lhsT=ones_b[:, :], rhs=pT[mt][:, :], start=(mt == 0), stop=(mt == 1)))(mt, db),
                         reads=["ones_b", "pT%d" % mt], writes=[PSK[db]])
                P.dve((lambda db: lambda e: e.reciprocal(out=rs[:, :], in_=ps[db][:, :]))(db), reads=[PSK[db]], writes=["rs"])
                P.dve((lambda ob, h4: lambda e: e.tensor_tensor(out=oc[:, h4, :], in0=ps[ob][:, :], in1=rs[:, :], op=ALU.mult))(ob, h4),
                      reads=[PSK[ob], "rs"], writes=[("fB", h4)])
            w12, k12 = wload(*unit_in(l, 12))
            for c in range(4):
                b = proj_F(w12, k12, c, N)
                silu_mul(b, N, fA[:, c, 0:N], ("fA", c), oc[:, c, 0:N], ("fB", c))
            P.pool(lambda e: e.tensor_copy(out=small[:, 0:1], in_=small[:, 0:1]), reads=[("fA", c) for c in range(4)], writes=["fA"])
            branch_proj(l, 3, fA, "fA", N)
            dense_back(l, N, TT, tp, xsrc, xdst, "o_x", xkeys)

        def sample_setup():
            P.dma("pool", "lc0", lambda e: e.dma_start(out=pt_i[:], in_=ptab[0:1, :].partition_broadcast(128)), writes=["pt_i"])
            P.dve(lambda e: e.tensor_copy(out=pt_f[:], in_=pt_i[:]), reads=["pt_i"], writes=["pt_f"])
            P.dve(lambda e: e.tensor_scalar(out=pt_f[:], in0=pt_f[:], scalar1=float(PAGE), scalar2=iota[:, 0:1], op0=ALU.mult, op1=ALU.add),
                  reads=["pt_f", "iota"], writes=["pt_f"])
            P.dve(lambda e: e.tensor_copy(out=idx_i[:], in_=pt_f[:]), reads=["pt_f"], writes=["idx_i"])
            if depth > 1:
                P.dve(lambda e: e.tensor_scalar(out=pt_f[:], in0=pt_f[:], scalar1=float(NPOOL * PAGE), scalar2=None, op0=ALU.add), reads=["pt_f"], writes=["pt_f"])
                P.dve(lambda e: e.tensor_copy(out=pt_i[:], in_=pt_f[:]), reads=["pt_f"], writes=["idx_i"])

        def ocs_to_oc():
            for c in range(4):
                P.pe((lambda c: lambda e: e.transpose(out=ps[6][:, c * 64:(c + 1) * 64], in_=ocs[:64, c * 128:(c + 1) * 128], identity=ident[:64, :64]))(c),
                     reads=["ocs", "ident"], writes=[PSK[6]])
            P.act(lambda e: e.copy(out=oc[:, :, 0:64], in_=ps[6][:, 0:256].rearrange("p (c q) -> p c q", c=4)), reads=[PSK[6]],
                  writes=[("fB", c) for c in range(4)])

        def scatter_acc(R, selc, bl, first):
            P.pe(lambda e: e.matmul(ps[4][:64, :], lhsT=selc[:R, bl, :], rhs=On[:R, :], start=True, stop=True),
                 reads=["ncst", "rsel", "rselm"], writes=[PSK[4]])
            if first:
                P.dve(lambda e: e.tensor_copy(out=ocs[:64, :], in_=ps[4][:64, :]), reads=[PSK[4]], writes=["ocs"])
            else:
                P.dve(lambda e: e.tensor_tensor(out=ocs[:64, :], in0=ocs[:64, :], in1=ps[4][:64, :], op=ALU.add), reads=[PSK[4], "ocs"], writes=["ocs"])

        def sample_group(l):
            N, TT, tp = 64, 1, 64
            src = xs if l == 0 else x1s
            dst = x1s if l < depth - 1 else ys
            xkeys = [("xsout", l - 1)] if l > 0 else []

            def xsrc(t):
                return src[0:64, :]

            def xdst(t):
                return dst[0:64, :]

            for c in range(4):
                for r in range(2):
                    P.dma("pool", "lc%d" % c, (lambda c, r: lambda e: e.dma_start(out=zp[:, c, r * 16:(r + 1) * 16],
                                                                                  in_=sconv[l, :, r, c * 128:(c + 1) * 128].rearrange("b p -> p b")))(c, r),
                          writes=[("zp", c)])
            dense_front(l, N, TT, tp, xsrc, True, 0, xkeys)
            w7, k7 = wload(*unit_in(l, 7))
            b = proj_T(w7, k7, 0, tp)
            rope(b, tp, rope_s, None, qa[:64, 0, :, :], 0.125, [("qa", 0)])
            P.act(lambda e: e.copy(out=vn_f[:64, :].rearrange("p (h d) -> p h d", h=8), in_=qa[:64, 0, :, 0:64]), reads=[("qa", 0)], writes=["vn_f"])
            for c in range(4):
                P.pe((lambda c: lambda e: e.transpose(out=ps[6][:, c * 64:(c + 1) * 64], in_=vn_f[:64, c * 128:(c + 1) * 128], identity=ident[:64, :64]))(c),
                     reads=["vn_f", "ident"], writes=[PSK[6]])
            P.act(lambda e: e.copy(out=qTs[:].rearrange("p c q -> p (c q)"), in_=ps[6][:, 0:256]), reads=[PSK[6]], writes=["qTs"])
            w8, k8 = wload(*unit_in(l, 8))
            b = proj_T(w8, k8, 0, tp)
            rope(b, tp, rope_s, None, kst[:64, :].rearrange("p (h d) -> p h d", h=8), 1.0, ["kst"])
            P.dma(XQ, "o_k", lambda e: e.dma_start(out=nks[l], in_=kst[:64, :]), reads=["kst"])
            for c in range(4):
                P.pe((lambda c: lambda e: e.transpose(out=ps[7][:, c * 64:(c + 1) * 64], in_=kst[:64, c * 128:(c + 1) * 128], identity=ident[:64, :64]))(c),
                     reads=["kst", "ident"], writes=[PSK[7]])
            P.act(lambda e: e.copy(out=kTn[:].rearrange("p c q -> p (c q)"), in_=ps[7][:, 0:256]), reads=[PSK[7]], writes=["kTn"])
            w9, k9 = wload(*unit_in(l, 9))
            b = proj_T(w9, k9, 0, tp)
            P.act((lambda b: lambda e: e.copy(out=vst[:64, :], in_=ps[b][:64, :]))(b), reads=[PSK[b]], writes=["vst"])
            P.dma(XQ, "o_v", lambda e: e.dma_start(out=nvs[l], in_=vst[:64, :]), reads=["vst"])
            qTs4 = qTs[:].rearrange("p c (t b) -> p c t b", b=16)
            gcnt = [0, 0]
            assert l < 2
            idx_l = idx_i if l == 0 else pt_i
            kv_i = [0, 0]
            ck_flat = cache_k.rearrange("l n w -> (l n) w")
            cv_flat = cache_v.rearrange("l n w -> (l n) w")

            def issue_k(upto):
                while kv_i[0] < min(upto, NSB * NPAGES):
                    n = kv_i[0]
                    kv_i[0] += 1
                    ap, key = kslots[n % 3]
                    P.dma("pool", "gk%d" % (n % 3), (lambda ap, n: lambda e: e.indirect_dma_start(
                        out=ap, out_offset=None, in_=ck_flat, in_offset=bass.IndirectOffsetOnAxis(ap=idx_l[:, n:n + 1], axis=0)))(ap, n),
                        reads=["idx_i"], writes=[key])

            def issue_v(upto):
                while kv_i[1] < min(upto, NSB * NPAGES):
                    n = kv_i[1]
                    kv_i[1] += 1
                    ap, key = vslots[n % 4]
                    P.dma("pool", "gv%d" % (n % 4), (lambda ap, n: lambda e: e.indirect_dma_start(
                        out=ap, out_offset=None, in_=cv_flat, in_offset=bass.IndirectOffsetOnAxis(ap=idx_l[:, n:n + 1], axis=0)))(ap, n),
                        reads=["idx_i"], writes=[key])
            for bl in range(NSB):
                qsl = qTs4[:, :, :, bl:bl + 1].rearrange("p c t o -> p c (t o)").unsqueeze(2).to_broadcast([128, 4, 8, 4])
                P.dve((lambda qsl: lambda e: e.tensor_tensor(out=Qblk_f[:].rearrange("p c (h t) -> p c h t", h=8), in0=qsl,
                                                            in1=qmask[:].unsqueeze(3).to_broadcast([128, 4, 8, 4]), op=ALU.mult))(qsl),
                      reads=["qTs", "qmask"], writes=["Qblk_f"])
                P.dve(lambda e: e.tensor_copy(out=Qblk[:], in_=Qblk_f[:]), reads=["Qblk_f"], writes=["Qblk"])
                if bl == 0:
                    issue_k(3)
                    issue_v(4)
                for j in range(NPAGES):
                    n = bl * NPAGES + j
                    ksl, kkey = kslots[n % 3]
                    kb, kbkey = kb16[n % 2]
                    if n % 2 == 0:
                        P.act((lambda kb, ksl: lambda e: e.copy(out=kb, in_=ksl))(kb, ksl), reads=[kkey], writes=[kbkey])
                    else:
                        P.dve((lambda kb, ksl: lambda e: e.tensor_copy(out=kb, in_=ksl))(kb, ksl), reads=[kkey], writes=[kbkey])
                    issue_k(n + 4)
                    P.pe((lambda kb, j: lambda e: e.matmul(ps[5][:8, :], lhsT=blkind_b[:, j, :], rhs=kb, start=(j == 0), stop=(j == NPAGES - 1)))(kb, j),
                         reads=[kbkey, "blkind_b"], writes=[PSK[5]])
                    pb = 6 + (j % 2)
                    for c in range(4):
                        P.pe((lambda kb, c, pb: lambda e: e.matmul(ps[pb][:, c * 128:(c + 1) * 128], lhsT=kb[:, c * 128:(c + 1) * 128], rhs=identb[:, :], start=True, stop=True))(kb, c, pb),
                             reads=[kbkey, "identb"], writes=[PSK[pb]])
                    srcp = ps[pb][:, :].rearrange("p (c k) -> p c k", c=4)
                    dstp = KTs[:, :, j * 128:(j + 1) * 128]
                    if j % 2 == 1:
                        P.act((lambda srcp, dstp: lambda e: e.copy(out=dstp, in_=srcp))(srcp, dstp), reads=[PSK[pb]], writes=["ktb0"])
                    else:
                        P.dve((lambda srcp, dstp: lambda e: e.tensor_copy(out=dstp, in_=srcp))(srcp, dstp), reads=[PSK[pb]], writes=["ktb0"])
                P.act(lambda e: e.copy(out=tmpf[:8, :], in_=ps[5][:8, :]), reads=[PSK[5]], writes=["tmpf"])
                for c in range(4):
                    P.pe((lambda c: lambda e: e.transpose(out=ps[5][:, c * 8:(c + 1) * 8], in_=tmpf[:8, c * 128:(c + 1) * 128], identity=ident[:8, :8]))(c),
                         reads=["tmpf", "ident"], writes=[PSK[5]])
                P.act(lambda e: e.copy(out=kmT[:].rearrange("p c n -> p (c n)"), in_=ps[5][:, 0:32]), reads=[PSK[5]], writes=["kmT"])
                for c in range(4):
                    P.pe((lambda c: lambda e: e.matmul(ps[4][:32, 0:8], lhsT=Qblk_f[:, c, :], rhs=kmT[:, c, :], start=(c == 0), stop=(c == 3)))(c),
                         reads=["Qblk_f", "kmT"], writes=[PSK[4]])
                P.dve(lambda e: e.tensor_copy(out=gms[:, :], in_=ps[4][:32, 0:8]), reads=[PSK[4]], writes=["gms"])
                P.dve(lambda e: e.max(out=m8s[:, :], in_=gms[:, :]), reads=["gms"], writes=["m8s"])
                P.dve(lambda e: e.tensor_tensor(out=selb[:, :], in0=gms[:, :], in1=m8s[:, 2:3].to_broadcast([32, 8]), op=ALU.is_ge), reads=["gms", "m8s"], writes=["selb"])
                P.dve(lambda e: e.tensor_scalar(out=selb[:, :], in0=selb[:, :], scalar1=-1.0, scalar2=-NEG, op0=ALU.add, op1=ALU.mult), reads=["selb"], writes=["selb"])
                for qd in range(4):
                    sb = 2 + (qd % 2)
                    for c in range(4):
                        P.pe((lambda c, sb, qd: lambda e: e.matmul(ps[sb][:32, :], lhsT=Qblk[:, c, :], rhs=KTs[:, c, qd * 512:(qd + 1) * 512], start=(c == 0), stop=(c == 3)))(c, sb, qd),
                             reads=["Qblk", "ktb0"], writes=[PSK[sb]])
                    pmt = Pm[qd % 2]
                    pkey = "rs" if qd % 2 == 0 else "cacc"
                    for hf in range(2):
                        n = qd * 2 + hf
                        P.act((lambda sb, hf, n, pmt: lambda e: e.activation(out=pmt[:32, hf * 256:(hf + 1) * 256], in_=ps[sb][:32, hf * 256:(hf + 1) * 256], func=AF.Exp,
                                                                             bias=selb[:, n:n + 1], accum_out=den[:, n:n + 1]))(sb, hf, n, pmt),
                              reads=[PSK[sb], "selb"], writes=[pkey, "den"])
                    for jj in range(4):
                        j = qd * 4 + jj
                        P.pe((lambda jj, j, pmt: lambda e: e.transpose(out=ps[0][:, j * 32:(j + 1) * 32], in_=pmt[:32, jj * 128:(jj + 1) * 128], identity=ident[:32, :32]))(jj, j, pmt),
                             reads=[pkey, "ident"], writes=[PSK[0]])
                P.dve(lambda e: e.tensor_copy(out=fT[:, :], in_=ps[0][:, :]), reads=[PSK[0]], writes=["fT"])
                for c in range(4):
                    P.pe((lambda c: lambda e: e.matmul(ps[2][:32, 0:64], lhsT=Qblk[:, c, :], rhs=kTn[:, c, :], start=(c == 0), stop=(c == 3)))(c),
                         reads=["Qblk", "kTn"], writes=[PSK[2]])
                P.dve((lambda bl: lambda e: e.tensor_tensor(out=Po[:, :], in0=ps[2][:32, 0:64], in1=ownb[:32, bl, :], op=ALU.add))(bl), reads=[PSK[2], "ownb"], writes=["Po"])
                P.act(lambda e: e.activation(out=Po[:, :], in_=Po[:, :], func=AF.Exp, accum_out=den[:, 8:9]), reads=["Po"], writes=["Po", "den"])
                P.pe(lambda e: e.transpose(out=ps[3][:64, 0:32], in_=Po[:32, :], identity=ident[:32, :32]), reads=["Po", "ident"], writes=[PSK[3]])
                P.act(lambda e: e.copy(out=PTo[:, :], in_=ps[3][:64, 0:32]), reads=[PSK[3]], writes=["PTo"])
                for j in range(NPAGES):
                    n = bl * NPAGES + j
                    vsl, vkey = vslots[n % 4]
                    vb, vbkey = vb16[n % 2]
                    if n % 2 == 1:
                        P.act((lambda vb, vsl: lambda e: e.copy(out=vb, in_=vsl))(vb, vsl), reads=[vkey], writes=[vbkey])
                    else:
                        P.dve((lambda vb, vsl: lambda e: e.tensor_copy(out=vb, in_=vsl))(vb, vsl), reads=[vkey], writes=[vbkey])
                    issue_v(n + 5)
                    P.pe((lambda vb, j: lambda e: e.matmul(ps[1][:32, :], lhsT=PTb[:, j, :], rhs=vb, start=(j == 0), stop=False))(vb, j),
                         reads=[vbkey, "fT"], writes=[PSK[1]])
                P.pe(lambda e: e.matmul(ps[1][:32, :], lhsT=PTo[:64, :], rhs=vst[:64, :], start=False, stop=True), reads=["PTo", "vst"], writes=[PSK[1]])
                P.dve(lambda e: e.tensor_reduce(out=den[:, 9:10], in_=den[:, 0:9], axis=AX.X, op=ALU.add), reads=["den"], writes=["den"])
                P.dve(lambda e: e.reciprocal(out=den[:, 10:11], in_=den[:, 9:10]), reads=["den"], writes=["den"])
                P.dve(lambda e: e.scalar_tensor_tensor(out=On[:32, :], in0=ps[1][:32, :], scalar=den[:, 10:11], in1=mdiag[:32, :], op0=ALU.mult, op1=ALU.mult),
                      reads=[PSK[1], "den", "mdiag"], writes=["ncst"])
                scatter_acc(32, rsel, bl, bl == 0)
            ocs_to_oc()
            w10, k10 = wload(*unit_in(l, 10))
            for c in range(4):
                b = proj_F(w10, k10, c, N)
                silu_mul(b, N, fA[:, c, 0:N], ("fA", c), oc[:, c, 0:N], ("fB", c))
            P.pool(lambda e: e.tensor_copy(out=small[:, 0:1], in_=small[:, 0:1]), reads=[("fA", c) for c in range(4)], writes=["fA"])
            branch_proj(l, 2, fA, "fA", N)
            w11, k11 = wload(*unit_in(l, 11))
            for c in range(4):
                b = proj_F(w11, k11, c, N)
                P.act((lambda b, c: lambda e: e.copy(out=fC[:, c, 0:N], in_=ps[b][:, 0:N]))(b, c), reads=[PSK[b]], writes=[("fC", c)])
            P.pool(lambda e: e.tensor_copy(out=small[:, 0:1], in_=small[:, 0:1]), reads=[("fC", c) for c in range(4)], writes=["fC"])
            sc = 128.0 ** -0.5
            fC4 = fC[:, :, 0:64].rearrange("p c (t b) -> p c t b", b=16)
            Qm = Qblk[:, :, 0:16]
            for bl in range(NSB):
                qsl = fC4[:, :, :, bl:bl + 1].rearrange("p c t o -> p c (t o)").unsqueeze(2).to_broadcast([128, 4, 4, 4])
                P.dve((lambda qsl: lambda e: e.tensor_tensor(out=Qm.rearrange("p c (h t) -> p c h t", h=4), in0=qsl,
                                                            in1=mmask[:].unsqueeze(3).to_broadcast([128, 4, 4, 4]), op=ALU.mult))(qsl),
                      reads=["fC", "mmask"], writes=["Qblk"])
                for mt in range(2):
                    P.dma("sp", "mk%d" % mt, (lambda mt, bl: lambda e: e.dma_start(out=kring[:, mt, :], in_=cmk[l, bl, mt * 128:(mt + 1) * 128, :]))(mt, bl), writes=["kring%d" % mt])
                    P.dma("sp", "mv%d" % mt, (lambda mt, bl: lambda e: e.dma_start(out=vring[:, mt, :], in_=cmv[l, bl, mt * 128:(mt + 1) * 128, :]))(mt, bl), writes=["vring%d" % mt])
                    pb = 6 + mt
                    for c in range(4):
                        P.pe((lambda mt, c, pb: lambda e: e.transpose(out=ps[pb][:, c * 128:(c + 1) * 128], in_=kring[:, mt, c * 128:(c + 1) * 128], identity=ident[:]))(mt, c, pb),
                             reads=["kring%d" % mt, "ident"], writes=[PSK[pb]])
                    srcp = ps[pb][:, :].rearrange("p (c k) -> p c k", c=4)
                    dstp = mkT[:, :, mt * 128:(mt + 1) * 128]
                    P.act((lambda srcp, dstp: lambda e: e.copy(out=dstp, in_=srcp))(srcp, dstp), reads=[PSK[pb]], writes=["mkT"])
                for c in range(4):
                    P.pe((lambda c: lambda e: e.matmul(ps[2][:16, 0:256], lhsT=Qm[:, c, :], rhs=mkT[:, c, :], start=(c == 0), stop=(c == 3)))(c),
                         reads=["Qblk", "mkT"], writes=[PSK[2]])
                P.dve(lambda e: e.tensor_reduce(out=den[:16, 11:12], in_=ps[2][:16, 0:256], axis=AX.X, op=ALU.max), reads=[PSK[2]], writes=["den"])
                P.dve(lambda e: e.tensor_scalar(out=den[:16, 12:13], in0=den[:16, 11:12], scalar1=-sc, scalar2=None, op0=ALU.mult), reads=["den"], writes=["den"])
                P.act(lambda e: e.activation(out=rs[:16, 0:256], in_=ps[2][:16, 0:256], func=AF.Exp, scale=sc, bias=den[:16, 12:13], accum_out=den[:16, 13:14]),
                      reads=[PSK[2], "den"], writes=["rs", "den"])
                P.dve(lambda e: e.reciprocal(out=den[:16, 14:15], in_=den[:16, 13:14]), reads=["den"], writes=["den"])
                for mt in range(2):
                    P.pe((lambda mt: lambda e: e.transpose(out=ps[0][:, mt * 16:(mt + 1) * 16], in_=rs[:16, mt * 128:(mt + 1) * 128], identity=ident[:16, :16]))(mt),
                         reads=["rs", "ident"], writes=[PSK[0]])
                P.dve(lambda e: e.tensor_copy(out=vn_f[:, 0:32], in_=ps[0][:, 0:32]), reads=[PSK[0]], writes=["vn_f"])
                for mt in range(2):
                    P.pe((lambda mt: lambda e: e.matmul(ps[1][:16, :], lhsT=vn_f[:, mt * 16:(mt + 1) * 16], rhs=vring[:, mt, :], start=(mt == 0), stop=(mt == 1)))(mt),
                         reads=["vn_f", "vring%d" % mt], writes=[PSK[1]])
                P.dve(lambda e: e.scalar_tensor_tensor(out=On[:16, :], in0=ps[1][:16, :], scalar=den[:16, 14:15], in1=mdiagm[:16, :], op0=ALU.mult, op1=ALU.mult),
                      reads=[PSK[1], "den", "mdiagm"], writes=["ncst"])
                scatter_acc(16, rselm, bl, bl == 0)
            ocs_to_oc()
            w12, k12 = wload(*unit_in(l, 12))
            for c in range(4):
                b = proj_F(w12, k12, c, N)
                silu_mul(b, N, fA[:, c, 0:N], ("fA", c), oc[:, c, 0:N], ("fB", c))
            P.pool(lambda e: e.tensor_copy(out=small[:, 0:1], in_=small[:, 0:1]), reads=[("fA", c) for c in range(4)], writes=["fA"])
            branch_proj(l, 3, fA, "fA", N)
            dense_back(l, N, TT, tp, xsrc, xdst, "o_xs", xkeys, okey="xsout")

        if do_sample:
            sample_setup()
        for l in range(depth):
            layer_consts(l)
            if do_prompt:
                for c in range(4):
                    P.dve((lambda c: lambda e: e.memset(zp[:, c, 0:32], 0.0))(c), writes=[("zp", c)])
                P.dve(lambda e: e.memset(KM[:], 0.0), writes=["KM"])
                P.dve(lambda e: e.memset(vaug[:], 1.0), writes=["vaug"])
                mem_kv_prompt(l)
                P.dma("pool", "lc1", lambda e: e.dma_start(out=gpost_bc[:], in_=g_post[l:l + 1, :].partition_broadcast(128)), writes=["gpost_bc"])
                for g in range(n_groups):
                    prompt_group(l, g)
            if do_sample:
                sample_group(l)
        P.emit()
    return nc, hc


_CACHE = {}


def kernel(**inp):
    if "nc" not in _CACHE:
        _CACHE["nc"] = build()
    nc, hc = _CACHE["nc"]
    f = lambda a: np.ascontiguousarray(np.asarray(a, dtype=np.float32))
    ck = f(inp["cache_k"]).reshape(DEPTH, NPOOL * PAGE, W)
    cv = f(inp["cache_v"]).reshape(DEPTH, NPOOL * PAGE, W)
    shared = {
        "cache_k": ck, "cache_v": cv,
        "g_pre": f(inp["g_pre"]), "g_post": f(inp["g_post"]), "w_in": f(inp["w_in"]), "ln_v_gain": f(inp["ln_v_gain"]),
        "w_spatial": f(inp["w_spatial"]), "b_spatial": f(inp["b_spatial"]), "conv_w": f(inp["conv_w"]), "g_mem": f(inp["g_mem"]),
        "w_mem_kv": f(inp["w_mem_kv"]), "w_merge": f(inp["w_merge"]), "b_merge": f(inp["b_merge"]),
        "w_branch": f(inp["w_branch"]).reshape(DEPTH, 4 * W, D), "w_out": f(inp["w_out"]),
    }
    for k, v in hc.items():
        shared["c_" + k] = v
    xpr = f(inp["x_prompt"])
    xsm = f(inp["x_sample"])
    in_maps = []
    for c in range(8):
        b0 = c * NSB
        m = dict(shared)
        m["xp"] = xpr[c % 4]
        m["xs"] = np.ascontiguousarray(xsm[b0:b0 + NSB].transpose(1, 0, 2).reshape(NS, D))
        m["cmk"] = f(inp["cache_mem_k"])[:, b0:b0 + NSB].reshape(DEPTH, NSB, 256, W)
        m["cmv"] = f(inp["cache_mem_v"])[:, b0:b0 + NSB].reshape(DEPTH, NSB, 256, W)
        m["sconv"] = f(inp["state_conv"])[:, b0:b0 + NSB]
        m["ptab"] = np.ascontiguousarray(np.asarray(inp["page_table"], dtype=np.int32)[b0:b0 + NSB]).reshape(1, NSB * NPAGES)
        m["memp"] = f(inp["mem_prompt"])[c % 4]
        in_maps.append(m)
    res = run_bass_kernel_spmd(nc, in_maps, core_ids=list(range(8)))
    R = res.results
    y_prompt = np.stack([R[c]["yp"] for c in range(4)])
    tm = lambda a: a.reshape(-1, 4, NSB, a.shape[-1])
    y_sample = np.concatenate([R[c]["ys"].reshape(4, NSB, D).transpose(1, 0, 2) for c in range(8)], axis=0)
    nkp = np.stack([R[c]["nkp"] for c in range(4)], axis=1).reshape(DEPTH, 4, SEQ, 8, 64)
    nvp = np.stack([R[c]["nvp"] for c in range(4)], axis=1).reshape(DEPTH, 4, SEQ, 8, 64)
    ncp = np.stack([R[c]["ncp"] for c in range(4)], axis=1)
    nmk = np.stack([R[c]["nmk"] for c in range(4)], axis=1).reshape(DEPTH, 4, 256, 4, 128)
    nmv = np.stack([R[c]["nmv"] for c in range(4)], axis=1).reshape(DEPTH, 4, 256, 4, 128)
    nks = np.concatenate([R[c]["nks"].reshape(DEPTH, 4, NSB, W).transpose(0, 2, 1, 3) for c in range(8)], axis=1).reshape(DEPTH, 128, 4, 8, 64)
    nvs = np.concatenate([R[c]["nvs"].reshape(DEPTH, 4, NSB, W).transpose(0, 2, 1, 3) for c in range(8)], axis=1).reshape(DEPTH, 128, 4, 8, 64)
    ncs = np.concatenate([R[c]["ncs"] for c in range(8)], axis=1)
    nvn = np.concatenate([R[c]["nvn"].reshape(DEPTH, 4, NSB, W).transpose(0, 2, 1, 3) for c in range(8)], axis=1)
    out = (y_prompt, y_sample, nkp, nvp, ncp, nmk, nmv, nks, nvs, ncs, nvn)
    return tuple(np.ascontiguousarray(o, dtype=np.float32) for o in out)
```

```python
import contextlib
import os
STAGE = int(os.environ.get('KSTAGE', '9'))
SUB = int(os.environ.get('KSUB', '99'))
FIN = int(os.environ.get('KFIN', '99'))
KC = int(os.environ.get('KC', '99'))
KR = int(os.environ.get('KR', '99'))
PSRR = int(os.environ.get('PSRR', '1'))
NDB = int(os.environ.get('NDB', '4'))
PREF = int(os.environ.get('PREF', '1'))
XQ = os.environ.get('XQ', 'sp')
import numpy as np
import concourse.bass as bass
import concourse.mybir as mybir
from concourse.bass_utils import run_bass_kernel_spmd

F32 = mybir.dt.float32
BF16 = mybir.dt.bfloat16
I32 = mybir.dt.int32
ALU = mybir.AluOpType
AF = mybir.ActivationFunctionType
AX = mybir.AxisListType

ENGS = ["pe", "act", "dve", "pool", "sp"]
NEG = -30000.0
EPS = 1e-6


class Op:
    __slots__ = ("eng", "fn", "waits", "needs_inc", "val", "dma_slot", "dma_val")

    def __init__(self, eng, fn, dma_slot=None):
        self.eng = eng
        self.fn = fn
        self.waits = []
        self.needs_inc = False
        self.val = None
        self.dma_slot = dma_slot
        self.dma_val = None


import types


def _freeze(fn):
    if fn.__closure__ is None:
        return fn
    cells = []
    for c in fn.__closure__:
        try:
            cells.append(types.CellType(c.cell_contents))
        except ValueError:
            cells.append(c)
    return types.FunctionType(fn.__code__, fn.__globals__, fn.__name__, fn.__defaults__, tuple(cells))


class Prog:
    def __init__(self, nc, same_engine_sync=True):
        self.nc = nc
        self.ops = {e: [] for e in ENGS}
        self.last_w = {}
        self.readers = {}
        self.same_engine_sync = same_engine_sync
        self.dma_slots = {}

    def op(self, eng, fn, reads=(), writes=(), dma=None):
        o = Op(eng, _freeze(fn), dma_slot=dma)
        deps = []
        for r in reads:
            w = self.last_w.get(r)
            if w is not None:
                deps.append((w, "raw"))
            if PSRR and isinstance(r, str) and r.startswith("ps") and r[2:].isdigit():
                lastrd = {}
                for rd in self.readers.get(r, ()):
                    if rd.eng != eng:
                        lastrd[rd.eng] = rd
                for rd in lastrd.values():
                    deps.append((rd, "rar"))
        for wkey in writes:
            w = self.last_w.get(wkey)
            if w is not None:
                deps.append((w, "waw"))
            lastrd = {}
            for rd in self.readers.get(wkey, ()):
                if rd.dma_slot is not None:
                    deps.append((rd, "war"))
                else:
                    lastrd[rd.eng] = rd
            for rd in lastrd.values():
                deps.append((rd, "war"))
        seen = set()
        for d, kind in deps:
            if d is o or id(d) in seen:
                continue
            seen.add(id(d))
            if d.eng == o.eng and d.dma_slot is None and o.dma_slot is None:
                if o.eng == "pe" or not self.same_engine_sync:
                    continue
            o.waits.append(d)
            if d.dma_slot is None:
                d.needs_inc = True
        if dma is not None:
            st = self.dma_slots.setdefault(dma, [0, None])
            if st[1] is not None:
                o.waits.append(st[1])
            st[0] += 16
            o.dma_val = st[0]
            st[1] = o
        for r in reads:
            self.readers.setdefault(r, []).append(o)
        for wkey in writes:
            self.last_w[wkey] = o
            self.readers[wkey] = []
        self.ops[eng].append(o)
        return o

    def pe(self, fn, reads=(), writes=()):
        return self.op("pe", fn, reads, writes)

    def act(self, fn, reads=(), writes=()):
        return self.op("act", fn, reads, writes)

    def dve(self, fn, reads=(), writes=()):
        return self.op("dve", fn, reads, writes)

    def pool(self, fn, reads=(), writes=()):
        return self.op("pool", fn, reads, writes)

    def dma(self, eng, slot, fn, reads=(), writes=()):
        return self.op(eng, fn, reads, writes, dma=slot)

    def emit(self):
        nc = self.nc
        for e in ENGS:
            c = 0
            for o in self.ops[e]:
                if o.dma_slot is None and o.needs_inc:
                    c += 1
                    o.val = c
        slots = sorted(self.dma_slots.keys(), key=str)
        with contextlib.ExitStack() as es:
            esem = {e: es.enter_context(nc.semaphore("s_" + e)) for e in ENGS}
            dsem = {s: es.enter_context(nc.semaphore("d_%d" % i)) for i, s in enumerate(slots)}
            es.enter_context(nc.allow_non_contiguous_dma(reason="small strided parameter / layout DMAs"))
            block = es.enter_context(nc.Block())

            def run(ename, eng):
                waited = {}
                for o in self.ops[ename]:
                    for d in o.waits:
                        if d.dma_slot is not None:
                            sem, v = dsem[d.dma_slot], d.dma_val
                        else:
                            sem, v = esem[d.eng], d.val
                        k = id(sem)
                        if waited.get(k, 0) >= v:
                            continue
                        waited[k] = v
                        eng.wait_ge(sem, v)
                    ins = o.fn(eng)
                    if o.dma_slot is not None:
                        ins.then_inc(dsem[o.dma_slot], 16)
                    elif o.needs_inc:
                        ins.then_inc(esem[ename], 1)
                if ename == "sp":
                    for s in slots:
                        st = self.dma_slots[s]
                        if waited.get(id(dsem[s]), 0) < st[0]:
                            eng.wait_ge(dsem[s], st[0])

            @block.tensor
            def _(eng):
                run("pe", eng)

            @block.scalar
            def _(eng):
                run("act", eng)

            @block.vector
            def _(eng):
                run("dve", eng)

            @block.gpsimd
            def _(eng):
                run("pool", eng)

            @block.sync
            def _(eng):
                run("sp", eng)


D = 1024
SEQ = 4096
DEPTH = 2
NSB = 16
NS = 64
W = 512
NPARTS = 13
NPOOL = 2560
PAGE = 128
NPAGES = 16


def host_consts():
    c = {}
    c["ident"] = np.eye(128, dtype=np.float32)
    k = np.arange(128)[:, None, None]
    o = np.arange(4)[None, :, None]
    q = np.arange(512)[None, None, :]
    c["tri"] = np.where(q >= o * 128 + k, 0.0, NEG).astype(np.float32)
    n = np.arange(16)[:, None]
    key = np.arange(SEQ)[None, :]
    c["kind"] = (key // 256 == n).astype(np.float32)
    half = 8
    inv = np.power(np.float32(500000.0), -np.arange(half, dtype=np.float32) * np.float32(2.0 / 16)).astype(np.float32)
    pos = np.arange(SEQ, dtype=np.float32)
    ang = (pos[:, None] * inv[None, :]).astype(np.float32)
    cs = np.cos(ang).astype(np.float32).reshape(32, 128, 8).transpose(1, 0, 2)
    sn = np.sin(ang).astype(np.float32).reshape(32, 128, 8).transpose(1, 0, 2)
    c["rope_p"] = np.ascontiguousarray(np.stack([cs, sn], axis=2))
    pos_s = (2048 + np.arange(4, dtype=np.float32))
    ang_s = (pos_s[:, None] * inv[None, :]).astype(np.float32)
    cs_s = np.repeat(np.cos(ang_s).astype(np.float32), 16, axis=0)
    sn_s = np.repeat(np.sin(ang_s).astype(np.float32), 16, axis=0)
    rs = np.zeros((128, 2, 8), np.float32)
    rs[:64, 0] = cs_s
    rs[:64, 1] = sn_s
    c["rope_s"] = rs
    p = np.arange(64)
    dm = np.zeros((128, 4, 16), np.float32)
    for t in range(4):
        for blp in range(16):
            dm[:64, t, blp] = ((p % 16) == blp) & ((p // 16) <= t)
    c["dmask"] = dm
    r = np.arange(32)
    md = np.zeros((128, 512), np.float32)
    md[:32] = ((r // 4)[:, None] == (np.arange(512) // 64)[None, :])
    c["mdiag"] = md
    rsel = np.zeros((128, 16, 64), np.float32)
    for bl in range(16):
        for rr in range(32):
            rsel[rr, bl, (rr % 4) * 16 + bl] = 1.0
    c["rsel"] = rsel
    cown = np.zeros((128, 4), np.float32)
    cown[:32] = np.where(np.arange(4)[None, :] <= (r % 4)[:, None], 0.0, NEG)
    c["cown"] = cown
    selT = np.zeros((128, 16, 4), np.float32)
    for bl in range(16):
        for t in range(4):
            selT[t * 16 + bl, bl, t] = 1.0
    c["selT"] = selT
    r16 = np.arange(16)
    mdm = np.zeros((128, 512), np.float32)
    mdm[:16] = ((r16 // 4)[:, None] == (np.arange(512) // 128)[None, :])
    c["mdiagm"] = mdm
    rselm = np.zeros((128, 16, 64), np.float32)
    for bl in range(16):
        for rr in range(16):
            rselm[rr, bl, (rr % 4) * 16 + bl] = 1.0
    c["rselm"] = rselm
    iot = np.zeros((128, 1), np.float32)
    iot[:, 0] = np.arange(128)
    c["iota"] = iot
    pp = np.arange(128)
    qm = np.zeros((128, 4, 8), np.float32)
    for cc in range(4):
        for h in range(8):
            qm[:, cc, h] = (h == 2 * cc + pp // 64)
    c["qmask"] = qm
    mm = np.zeros((128, 4, 4), np.float32)
    for cc in range(4):
        mm[:, cc, cc] = 1.0
    c["mmask"] = mm
    bi = np.zeros((128, 16, 8), np.float32)
    for j in range(16):
        bi[:, j, j // 2] = 1.0 / 256
    c["blkind"] = bi
    ob = np.full((128, 16, 64), NEG, np.float32)
    for rr in range(32):
        t = rr % 4
        for bl in range(16):
            for t2 in range(t + 1):
                ob[rr, bl, t2 * 16 + bl] = 0.0
    c["ownb"] = ob
    return c


def build(do_prompt=True, do_sample=True, n_groups=8, depth=DEPTH):
    nc = bass.Bass("TRN2", target_bir_lowering=False)
    P = Prog(nc, same_engine_sync=bool(int(os.environ.get('KSES', '1'))))

    def din(name, shape, dt=F32):
        return nc.dram_tensor(name, list(shape), dt, kind="ExternalInput").ap()

    def dout(name, shape, dt=F32):
        return nc.dram_tensor(name, list(shape), dt, kind="ExternalOutput").ap()

    def dscr(name, shape, dt):
        return nc.dram_tensor(name, list(shape), dt).ap()

    xp = din("xp", [SEQ, D])
    xs = din("xs", [NS, D])
    cache_k = din("cache_k", [DEPTH, NPOOL * PAGE, W])
    cache_v = din("cache_v", [DEPTH, NPOOL * PAGE, W])
    cmk = din("cmk", [DEPTH, NSB, 256, W])
    cmv = din("cmv", [DEPTH, NSB, 256, W])
    sconv = din("sconv", [DEPTH, NSB, 2, W])
    ptab = din("ptab", [1, NSB * NPAGES], I32)
    memp = din("memp", [256, D])
    g_pre = din("g_pre", [DEPTH, D])
    g_post = din("g_post", [DEPTH, D])
    w_in = din("w_in", [DEPTH, D, NPARTS * W])
    ln_v_gain = din("ln_v_gain", [DEPTH, W])
    w_spatial = din("w_spatial", [DEPTH, 4, 128, 128])
    b_spatial = din("b_spatial", [DEPTH, 4, 128])
    conv_w = din("conv_w", [DEPTH, 3, W])
    g_mem = din("g_mem", [DEPTH, D])
    w_mem_kv = din("w_mem_kv", [DEPTH, D, D])
    w_merge = din("w_merge", [DEPTH, D, 4 * D])
    b_merge = din("b_merge", [DEPTH, 4 * D])
    w_branch = din("w_branch", [DEPTH, 4 * W, D])
    w_out = din("w_out", [DEPTH, D, D])
    hc = host_consts()
    cin = {k: din("c_" + k, v.shape) for k, v in hc.items()}

    yp = dout("yp", [SEQ, D])
    ys = dout("ys", [NS, D])
    nkp = dout("nkp", [DEPTH, SEQ, W])
    nvp = dout("nvp", [DEPTH, SEQ, W])
    ncp = dout("ncp", [DEPTH, 2, W])
    nmk = dout("nmk", [DEPTH, 256, W])
    nmv = dout("nmv", [DEPTH, 256, W])
    nks = dout("nks", [DEPTH, NS, W])
    nvs = dout("nvs", [DEPTH, NS, W])
    ncs = dout("ncs", [DEPTH, NSB, 2, W])
    nvn = dout("nvn", [DEPTH, NS, W])

    w_in_b = dscr("w_in_b", [DEPTH, NPARTS, 128, 8 * W], BF16)
    w_merge_b = dscr("w_merge_b", [DEPTH, 8, 128, 8 * W], BF16)
    w_branch_b = dscr("w_branch_b", [DEPTH, 4, 128, 4 * D], BF16)
    w_out_b = dscr("w_out_b", [DEPTH, 2, 128, 8 * W], BF16)
    w_mem_b = dscr("w_mem_b", [DEPTH, 2, 128, 8 * W], BF16)
    x1p = dscr("x1p", [SEQ, D], F32)
    x1s = dscr("x1s", [NS, D], F32)
    KT_d = dscr("KT_d", [8, 80, SEQ], BF16)
    VA_d = dscr("VA_d", [SEQ, 4 * 192], BF16)

    es = contextlib.ExitStack()
    with es:
        def T(name, shape, dt):
            return es.enter_context(nc.sbuf_tensor(name, list(shape), dt))

        ps = [es.enter_context(nc.psum_tensor("ps%d" % i, [128, 512], F32)) for i in range(8)]
        PSK = ["ps%d" % i for i in range(8)]

        ident = T("ident", [128, 128], F32)
        identb = T("identb", [128, 128], BF16)
        trib = T("trib", [128, 4, 512], BF16)
        ones_f = T("ones_f", [128, 128], F32)
        ones_b = T("ones_b", [128, 128], BF16)
        rope_p = T("rope_p", [128, 32, 2, 8], F32)
        rope_s = T("rope_s", [128, 2, 8], F32)
        dmask = T("dmask", [128, 4, 16], F32)
        mdiag = T("mdiag", [128, 512], F32)
        rsel = T("rsel", [128, 16, 64], F32)
        cown = T("cown", [128, 4], F32)
        selT = T("selT", [128, 16, 4], F32)
        mdiagm = T("mdiagm", [128, 512], F32)
        rselm = T("rselm", [128, 16, 64], F32)
        iota = T("iota", [128, 1], F32)
        epsc = T("epsc", [128, 1], F32)
        qmask = T("qmask", [128, 4, 8], F32)
        mmask = T("mmask", [128, 4, 4], F32)
        blkind = T("blkind", [128, 16, 8], F32)
        ownb = T("ownb", [128, 16, 64], F32)

        cq = [0]

        def cload(dst, src, key):
            cq[0] += 1
            P.dma("pool", "c%d" % (cq[0] % 4), lambda e: e.dma_start(out=dst, in_=src), writes=[key])

        cload(ident[:], cin["ident"], "ident")
        cload(rope_p[:], cin["rope_p"], "rope_p")
        cload(rope_s[:], cin["rope_s"], "rope_s")
        cload(dmask[:], cin["dmask"], "dmask")
        cload(mdiag[:], cin["mdiag"], "mdiag")
        cload(rsel[:], cin["rsel"], "rsel")
        cload(cown[:], cin["cown"], "cown")
        cload(selT[:], cin["selT"], "selT")
        cload(mdiagm[:], cin["mdiagm"], "mdiagm")
        cload(rselm[:], cin["rselm"], "rselm")
        cload(iota[:], cin["iota"], "iota")
        cload(qmask[:], cin["qmask"], "qmask")
        cload(mmask[:], cin["mmask"], "mmask")
        cload(blkind[:], cin["blkind"], "blkind")
        cload(ownb[:], cin["ownb"], "ownb")
        P.dma("pool", "c0", lambda e: e.dma_start(out=trib[:], in_=cin["tri"]), writes=["trib"])
        P.dve(lambda e: e.tensor_copy(out=identb[:], in_=ident[:]), reads=["ident"], writes=["identb"])
        P.dve(lambda e: e.memset(ones_f[:], 1.0), writes=["ones_f"])
        P.dve(lambda e: e.memset(ones_b[:], 1.0), writes=["ones_b"])
        P.dve(lambda e: e.memset(epsc[:], EPS), writes=["epsc"])
        for h in range(8):
            for q4 in range(4):
                P.dma("pool", "c1", (lambda h, q4: lambda e: e.dma_start(out=KT_d[h, 64:80, q4 * 1024:(q4 + 1) * 1024], in_=cin["kind"][:, q4 * 1024:(q4 + 1) * 1024]))(h, q4), writes=["KTind"])
        def _va_ones():
          P.dve(lambda e: e.memset(vab[0][:], 1.0), writes=["vab0"])
          for c in range(4):
            P.dma("pool", "c2", (lambda c: lambda e: e.dma_start(
                out=VA_d.rearrange("(j p) f -> p j f", p=128)[:, :, c * 192 + 64:c * 192 + 128], in_=vab[0][:, :, 0:64]))(c),
                reads=["vab0"], writes=["VAones"])

        wq = [0]

        def wconv(dst, src, key):
            wq[0] += 1
            P.dma("pool", "wc%d" % (wq[0] % 4), lambda e: e.dma_start(out=dst, in_=src), writes=[key])

        def kpc(ap2d):
            return ap2d.rearrange("(k p) c -> p k c", p=128)

        def conv_in(l, j):
            wconv(w_in_b[l, j].rearrange("p (k c) -> p k c", k=8), kpc(w_in[l, :, j * W:(j + 1) * W]), ("w_in_b", l, j))

        def conv_br(l, n):
            for hf in range(2):
                wconv(w_merge_b[l, n * 2 + hf].rearrange("p (k c) -> p k c", k=8), kpc(w_merge[l, :, n * D + hf * W:n * D + (hf + 1) * W]), ("w_merge_b", l, n, hf))
            wconv(w_branch_b[l, n].rearrange("p (k c) -> p k c", k=4), kpc(w_branch[l, n * W:(n + 1) * W, :]), ("w_branch_b", l, n))

        for l in range(depth):
            for hf in range(2):
                wconv(w_mem_b[l, hf].rearrange("p (k c) -> p k c", k=8), kpc(w_mem_kv[l, :, hf * W:(hf + 1) * W]), ("w_mem_b", l, hf))
            for j in (0, 1, 2):
                conv_in(l, j)
            conv_br(l, 0)
            for j in (3, 4, 5, 6):
                conv_in(l, j)
            conv_br(l, 1)
            for j in (7, 8, 9, 10):
                conv_in(l, j)
            conv_br(l, 2)
            for j in (11, 12):
                conv_in(l, j)
            conv_br(l, 3)
            for hf in range(2):
                wconv(w_out_b[l, hf].rearrange("p (k c) -> p k c", k=8), kpc(w_out[l, :, hf * W:(hf + 1) * W]), ("w_out_b", l, hf))

        NR = 3
        ring = [T("wr%d" % i, [128, 8, 512], BF16) for i in range(NR)]
        wcount = [0]

        def wload(src_ap_fn, srckey):
            i = wcount[0] % NR
            wcount[0] += 1
            key = "wr%d" % i
            P.dma("sp", "w%d" % i, lambda e: e.dma_start(out=src_ap_fn[0](ring[i]), in_=src_ap_fn[1]), reads=[srckey], writes=[key])
            return ring[i], key

        flat = (lambda r: r[:].rearrange("p k c -> p (k c)"))

        def unit_in(l, j):
            return (flat, w_in_b[l, j]), ("w_in_b", l, j)

        def unit_merge(l, n, hf):
            return (flat, w_merge_b[l, n * 2 + hf]), ("w_merge_b", l, n, hf)

        def unit_branch(l, n):
            return (flat, w_branch_b[l, n]), ("w_branch_b", l, n)

        def unit_out(l, hf):
            return (flat, w_out_b[l, hf]), ("w_out_b", l, hf)

        def unit_mem(l, hf):
            return (flat, w_mem_b[l, hf]), ("w_mem_b", l, hf)

        xt = T("xt", [128, D], F32)
        junk = T("junk", [128, D], BF16)
        st = T("st", [128, 8], F32)
        gpre_bc = T("gpre_bc", [128, D], F32)
        gpost_bc = T("gpost_bc", [128, D], F32)
        gln_bc = T("gln_bc", [128, W], F32)
        bm_col = T("bm_col", [128, 32], F32)
        cw_col = T("cw_col", [128, 4, 3], F32)
        hT = T("hT", [128, 8, 512], BF16)
        yacc = T("yacc", [128, 8, 512], BF16)
        fA = T("fA", [128, 4, 512], BF16)
        fB = T("fB", [128, 4, 512], BF16)
        fC = T("fC", [128, 4, 512], BF16)
        fT = T("fT", [128, 512], BF16)
        gT = T("gT", [128, 512], BF16)
        tmpf = T("tmpf", [128, 512], F32)
        zp = T("zp", [128, 4, 32 + 512], F32)
        cacc = T("cacc", [128, 512], F32)
        vn_b = T("vn_b", [128, 4, 512], BF16)
        vn_f = T("vn_f", [128, 512], F32)
        bnst = T("bnst", [128, 8], F32)
        wsT = T("wsT", [128, 4, 128], BF16)
        ws_nat = T("ws_nat", [128, 4, 128], F32)
        bsp_row = T("bsp_row", [1, 4, 128], BF16)
        bsp_f = T("bsp_f", [1, 4, 128], F32)
        ws64 = T("ws64", [128, 4, 64], BF16)
        w4bc = T("w4bc", [128, 4, 4], F32)
        bsp64 = T("bsp64", [1, 4, 64], BF16)
        qa = T("qa", [128, 4, 8, 80], F32)
        kst = T("kst", [128, 512], F32)
        vst = T("vst", [128, 512], F32)
        lqq = T("lqq", [128, 4, 8], F32)
        ktst = T("ktst", [64, 8, 128], BF16)
        vaug = T("vaug", [128, 4, 192], BF16)
        KM = T("KM", [128, 4, 128], BF16)
        qT4 = T("qT4", [128, 4, 128], BF16)
        gm = T("gm", [128, 8, 16], F32)
        m8 = T("m8", [128, 8, 8], F32)
        selm = T("selm", [128, 8, 16], F32)
        qTa = T("qTa", [80, 8, 512], BF16)
        ktb = [T("ktb0", [128, 2, SEQ], BF16)] * 2
        vab = [T("vab0", [128, 32, 192], BF16)] * 2
        pT = [T("pT%d" % i, [128, 512], BF16) for i in range(3)]
        rs = T("rs", [128, 512], F32)
        oc = fB
        memT = fA[:].rearrange("p c (a n) -> p (c a) n", a=2)
        mkT = T("mkT", [128, 4, 256], BF16)
        mv_b = T("mv_b", [128, 2, 512], BF16)
        pm = T("pm", [128, 256], F32)
        pmT = T("pmT", [128, 2, 128], BF16)
        small = T("small", [128, 16], F32)
        ncst = T("ncst", [32, 512], F32)
        ksum = T("ksum", [128, 4], F32)
        kring = T("kring", [128, 2, 512], F32)
        vring = T("vring", [128, 2, 512], F32)
        ocs = T("ocs", [64, 512], F32)
        pt_i = T("pt_i", [128, NSB * NPAGES], I32)
        pt_f = T("pt_f", [128, NSB * NPAGES], F32)
        idx_i = T("idx_i", [128, NSB * NPAGES], I32)
        qTs = T("qTs", [128, 4, 64], F32)
        kTn = T("kTn", [128, 4, 64], BF16)
        Qblk_f = T("Qblk_f", [128, 4, 32], F32)
        Qblk = T("Qblk", [128, 4, 32], BF16)
        kmT = T("kmT", [128, 4, 8], F32)
        gms = T("gms", [32, 8], F32)
        m8s = T("m8s", [32, 8], F32)
        selb = T("selb", [32, 8], F32)
        den = T("den", [32, 16], F32)
        PTo = T("PTo", [64, 32], F32)
        Po = T("Po", [32, 64], F32)
        KTs = ktb[0][:].rearrange("p a (b k) -> p (a b) k", b=2)
        blkind_b = T("blkind_b", [128, 16, 8], BF16)
        P.dve(lambda e: e.tensor_copy(out=blkind_b[:], in_=blkind[:]), reads=["blkind"], writes=["blkind_b"])
        PTb = fT[:].rearrange("p (j r) -> p j r", j=16)
        kslots = [(kring[:, 0, :], "kring0"), (kring[:, 1, :], "kring1"), (qa[:, 1, :, :].rearrange("p h d -> p (h d)")[:, 0:512], ("qa", 1))]
        vslots = [(vring[:, 0, :], "vring0"), (vring[:, 1, :], "vring1"), (qa[:, 2, :, :].rearrange("p h d -> p (h d)")[:, 0:512], ("qa", 2)),
                  (qa[:, 3, :, :].rearrange("p h d -> p (h d)")[:, 0:512], ("qa", 3))]
        kb16 = [(pT[0][:, :], "pT0"), (pT[1][:, :], "pT1")]
        vb16 = [(pT[2][:, :], "pT2"), (junk[:, 0:512], "junk")]
        Pm = [rs, cacc]
        PT = vn_f[:].rearrange("p (j r) -> p j r", j=16)
        On = ncst
        P.pool(lambda e: e.memset(small[:], 0.0), writes=["small"])
        _va_ones()

        def rstd_from_sumsq(col_in, col_out, np_, n_elem, keys_r, keys_w):
            P.act(lambda e: e.activation(out=st[:np_, col_out:col_out + 1], in_=st[:np_, col_in:col_in + 1], func=AF.Ln,
                                         scale=1.0 / n_elem, bias=epsc[:np_, 0:1]), reads=keys_r + ["epsc"], writes=keys_w)
            P.act(lambda e: e.activation(out=st[:np_, col_out:col_out + 1], in_=st[:np_, col_out:col_out + 1], func=AF.Exp,
                                         scale=-0.5), reads=keys_w, writes=keys_w)

        def norm_rows_to_T(src_rows_ap, np_, gbc, gkey, dstT, dst_key, col0, xkeys=()):
            norm_pre(src_rows_ap, np_, gbc, gkey, xkeys)
            norm_post(np_, dstT, dst_key, col0)

        def norm_pre(src_rows_ap, np_, gbc, gkey, xkeys=()):
            P.dma(XQ, "xin", lambda e: e.dma_start(out=xt[:np_, :], in_=src_rows_ap), reads=list(xkeys), writes=["xt"])
            P.act(lambda e: e.activation(out=junk[:np_, :], in_=xt[:np_, :], func=AF.Square, accum_out=st[:np_, 0:1]),
                  reads=["xt"], writes=["junk", "st0"])
            rstd_from_sumsq(0, 1, np_, D, ["st0"], ["st1"])
            P.dve(lambda e: e.scalar_tensor_tensor(out=xt[:np_, :], in0=xt[:np_, :], scalar=st[:np_, 1:2], in1=gbc[:np_, :],
                                                   op0=ALU.mult, op1=ALU.mult), reads=["xt", "st1", gkey], writes=["xt"])

        def norm_post(np_, dstT, dst_key, col0):
            for half in range(2):
                pb = 6 + half
                for i in range(4):
                    k = half * 4 + i
                    P.pe((lambda k, i, pb: lambda e: e.transpose(out=ps[pb][:, i * 128:i * 128 + np_], in_=xt[:np_, k * 128:(k + 1) * 128],
                                                                 identity=ident[:np_, :np_]))(k, i, pb),
                         reads=["xt", "ident"], writes=[PSK[pb]])
                src = ps[pb][:].rearrange("p (a b) -> p a b", a=4)[:, :, 0:np_]
                dst = dstT[:, half * 4:(half + 1) * 4, col0:col0 + np_]
                if half == 0:
                    P.act((lambda src, dst: lambda e: e.copy(out=dst, in_=src))(src, dst), reads=[PSK[pb]], writes=[dst_key])
                else:
                    P.dve((lambda src, dst: lambda e: e.tensor_copy(out=dst, in_=src))(src, dst), reads=[PSK[pb]], writes=[dst_key])

        dps = [0]

        def dense_bank():
            dps[0] = (dps[0] + 1) % NDB
            return dps[0]

        def proj_F(wt, wkey, c, N, src=None, srckey="hT", nk=8, cols=None):
            b = dense_bank()
            s = hT if src is None else src
            for k in range(nk):
                lw = wt[:, k, c * 128:(c + 1) * 128] if cols is None else cols(k)
                P.pe((lambda k, lw: lambda e: e.matmul(ps[b][:, 0:N], lhsT=lw, rhs=s[:, k, 0:N], start=(k == 0), stop=(k == nk - 1)))(k, lw),
                     reads=[wkey, srckey], writes=[PSK[b]])
            return b

        def proj_T(wt, wkey, t, tp, src=None, srckey="hT"):
            b = dense_bank()
            s = hT if src is None else src
            for k in range(8):
                P.pe((lambda k: lambda e: e.matmul(ps[b][:tp, 0:512], lhsT=s[:, k, t * tp:(t + 1) * tp], rhs=wt[:, k, :],
                                                   start=(k == 0), stop=(k == 7)))(k),
                     reads=[wkey, srckey], writes=[PSK[b]])
            return b

        def rope(b, tp, tab, jt, dst3, scale, keys_w):
            src = ps[b][:tp, :].rearrange("p (h d) -> p h d", h=8)
            if jt is None:
                cs = tab[:tp, 0, :].unsqueeze(1).to_broadcast([tp, 8, 8])
                sn = tab[:tp, 1, :].unsqueeze(1).to_broadcast([tp, 8, 8])
            else:
                cs = tab[:tp, jt, 0, :].unsqueeze(1).to_broadcast([tp, 8, 8])
                sn = tab[:tp, jt, 1, :].unsqueeze(1).to_broadcast([tp, 8, 8])
            t1 = tmpf[:tp, 0:64].rearrange("p (h d) -> p h d", h=8)
            t2 = tmpf[:tp, 64:128].rearrange("p (h d) -> p h d", h=8)
            x1 = src[:, :, 0:8]
            x2 = src[:, :, 8:16]
            rk = [PSK[b], "rope"]
            if KR < 1:
                return
            P.act(lambda e: e.mul(out=dst3[:, :, 16:64], in_=src[:, :, 16:64], mul=scale), reads=[PSK[b]], writes=keys_w)
            if KR < 2:
                return
            P.dve(lambda e: e.tensor_tensor(out=t1, in0=x1, in1=cs, op=ALU.mult), reads=rk, writes=["tmpf"])
            P.dve(lambda e: e.tensor_tensor(out=t2, in0=x2, in1=sn, op=ALU.mult), reads=rk, writes=["tmpf"])
            P.dve(lambda e: e.tensor_tensor(out=dst3[:, :, 0:8], in0=t1, in1=t2, op=ALU.subtract), reads=["tmpf"], writes=keys_w)
            if KR < 3:
                return
            P.dve(lambda e: e.tensor_tensor(out=t1, in0=x2, in1=cs, op=ALU.mult), reads=rk, writes=["tmpf"])
            P.dve(lambda e: e.tensor_tensor(out=t2, in0=x1, in1=sn, op=ALU.mult), reads=rk, writes=["tmpf"])
            P.dve(lambda e: e.tensor_tensor(out=dst3[:, :, 8:16], in0=t1, in1=t2, op=ALU.add), reads=["tmpf"], writes=keys_w)
            if scale != 1.0:
                P.dve(lambda e: e.tensor_scalar(out=dst3[:, :, 0:16], in0=dst3[:, :, 0:16], scalar1=scale, scalar2=None, op0=ALU.mult),
                      reads=keys_w, writes=keys_w)

        def silu_mul(b, N, dst, dkey, other, okey):
            P.act(lambda e: e.activation(out=fT[:, 0:N], in_=ps[b][:, 0:N], func=AF.Silu), reads=[PSK[b]], writes=["fT"])
            P.pool(lambda e: e.tensor_tensor(out=dst, in0=fT[:, 0:N], in1=other, op=ALU.mult), reads=["fT", okey], writes=[dkey])

        def branch_proj(l, n, brT, brkey, N):
            wm0, km0 = wload(*unit_merge(l, n, 0))
            wm1, km1 = wload(*unit_merge(l, n, 1))
            wb, kb = wload(*unit_branch(l, n))
            wbv = wb[:].rearrange("p (a b) c -> p a (b c)", a=4)
            for ocn in range(8):
                wm, km = (wm0, km0) if ocn < 4 else (wm1, km1)
                bg = proj_F(wm, km, ocn % 4, N)
                P.act((lambda bg, ocn: lambda e: e.activation(out=gT[:, 0:N], in_=ps[bg][:, 0:N], func=AF.Sigmoid,
                                                              bias=bm_col[:, n * 8 + ocn:n * 8 + ocn + 1]))(bg, ocn),
                      reads=[PSK[bg], "bm_col"], writes=["gT"])
                bp = proj_F(wb, kb, ocn, N, src=brT, srckey=brkey, nk=4, cols=(lambda ocn: lambda k: wbv[:, k, ocn * 128:(ocn + 1) * 128])(ocn))
                if n == 0:
                    P.dve((lambda bp, ocn: lambda e: e.tensor_tensor(out=yacc[:, ocn, 0:N], in0=ps[bp][:, 0:N], in1=gT[:, 0:N], op=ALU.mult))(bp, ocn),
                          reads=[PSK[bp], "gT"], writes=[("yacc", ocn)])
                else:
                    P.dve((lambda bp: lambda e: e.tensor_tensor(out=fT[:, 0:N], in0=ps[bp][:, 0:N], in1=gT[:, 0:N], op=ALU.mult))(bp),
                          reads=[PSK[bp], "gT"], writes=["fT"])
                    P.pool((lambda ocn: lambda e: e.tensor_tensor(out=yacc[:, ocn, 0:N], in0=yacc[:, ocn, 0:N], in1=fT[:, 0:N], op=ALU.add))(ocn),
                           reads=["fT", ("yacc", ocn)], writes=[("yacc", ocn)])

        def layer_consts(l):
            P.dma("pool", "lc0", lambda e: e.dma_start(out=gpre_bc[:], in_=g_pre[l:l + 1, :].partition_broadcast(128)), writes=["gpre_bc"])
            P.dma("pool", "lc1", lambda e: e.dma_start(out=gpost_bc[:], in_=g_post[l:l + 1, :].partition_broadcast(128)), writes=["gpost_bc"])
            P.dma("pool", "lc2", lambda e: e.dma_start(out=gln_bc[:], in_=ln_v_gain[l:l + 1, :].partition_broadcast(128)), writes=["gln_bc"])
            with nc.allow_non_contiguous_dma(reason="small per-layer vectors"):
                P.dma("pool", "lc3", lambda e: e.dma_start(out=bm_col[:], in_=b_merge[l].rearrange("(a p) -> p a", p=128)), writes=["bm_col"])
                for j3 in range(3):
                    P.dma("pool", "lc0", (lambda j3: lambda e: e.dma_start(out=cw_col[:, :, j3], in_=conv_w[l, j3].rearrange("(c p) -> p c", p=128)))(j3), writes=["cw_col"])
            P.dma("pool", "lc1", lambda e: e.dma_start(out=ws_nat[:], in_=w_spatial[l].rearrange("g t s -> t g s")), writes=["ws_nat"])
            for g4 in range(4):
                P.pe((lambda g4: lambda e: e.transpose(out=ps[6][:, g4 * 128:(g4 + 1) * 128], in_=ws_nat[:, g4, :], identity=ident[:]))(g4),
                     reads=["ws_nat", "ident"], writes=[PSK[6]])
            P.act(lambda e: e.copy(out=tmpf[:, :], in_=ps[6][:, :]), reads=[PSK[6]], writes=["tmpf"])
            P.pool(lambda e: e.affine_select(out=tmpf[:].rearrange("p (g t) -> p g t", g=4), in_=tmpf[:].rearrange("p (g t) -> p g t", g=4),
                                             pattern=[[0, 4], [1, 128]], compare_op=ALU.is_ge, fill=0.0, base=0, channel_multiplier=-1),
                   reads=["tmpf"], writes=["tmpf"])
            P.dve(lambda e: e.tensor_copy(out=wsT[:].rearrange("p g t -> p (g t)"), in_=tmpf[:, :]), reads=["tmpf"], writes=["wsT"])
            P.dma("pool", "lc2", lambda e: e.dma_start(out=bsp_f[:], in_=b_spatial[l:l + 1, :, :]), writes=["bsp_f"])
            P.dve(lambda e: e.tensor_copy(out=bsp_row[:], in_=bsp_f[:]), reads=["bsp_f"], writes=["bsp_row"])
            with nc.allow_non_contiguous_dma(reason="tiny 4x4 spatial block"):
                for s in range(4):
                    for g4 in range(4):
                        P.dma("pool", "lc3", (lambda s, g4: lambda e: e.dma_start(
                            out=w4bc[s * 16:(s + 1) * 16, g4, :],
                            in_=w_spatial[l, g4, 0:4, s:s + 1].rearrange("t o -> o t").partition_broadcast(16)))(s, g4), writes=["w4bc"])
            for g4 in range(4):
                for t in range(4):
                    P.dve((lambda g4, t: lambda e: e.tensor_scalar(out=ws64[:64, g4, t * 16:(t + 1) * 16], in0=dmask[:64, t, :],
                                                                    scalar1=w4bc[:64, g4, t:t + 1], scalar2=None, op0=ALU.mult))(g4, t),
                          reads=["w4bc", "dmask"], writes=["ws64"])
            P.dve(lambda e: e.tensor_copy(out=bsp64[:].rearrange("o g (t b) -> o g t b", b=16),
                                          in_=bsp_f[0:1, :, 0:4].unsqueeze(3).to_broadcast([1, 4, 4, 16])), reads=["bsp_f"], writes=["bsp64"])

        def dense_front(l, N, TT, tp, xsrc, is_sample, g, xkeys=(), skip_norm=False):
            for t in range(TT):
                if not skip_norm:
                    norm_rows_to_T(xsrc(t), tp, gpre_bc, "gpre_bc", hT, "hT", t * tp, xkeys)
            if SUB < 1:
                return
            w0, k0 = wload(*unit_in(l, 0))
            if os.environ.get('KW1'):
                return
            w1, k1 = wload(*unit_in(l, 1))
            w2, k2 = wload(*unit_in(l, 2))
            for c in range(4):
                b = proj_F(w0, k0, c, N)
                P.act((lambda b, c: lambda e: e.copy(out=fA[:, c, 0:N], in_=ps[b][:, 0:N]))(b, c), reads=[PSK[b]], writes=[("fA", c)])
            if FIN < 1:
                return
            for t in range(TT):
                b = proj_T(w1, k1, t, tp)
                if FIN < 2:
                    continue
                P.act((lambda b: lambda e: e.activation(out=junk[:tp, 0:512], in_=ps[b][:tp, :], func=AF.Identity, accum_out=st[:tp, 2:3]))(b),
                      reads=[PSK[b]], writes=["junk", "st2"])
                P.dve(lambda e: e.tensor_scalar(out=st[:tp, 3:4], in0=st[:tp, 2:3], scalar1=-1.0 / 512, scalar2=None, op0=ALU.mult), reads=["st2"], writes=["st3"])
                P.act((lambda b: lambda e: e.activation(out=junk[:tp, 0:512], in_=ps[b][:tp, :], func=AF.Square, bias=st[:tp, 3:4], accum_out=st[:tp, 4:5]))(b),
                      reads=[PSK[b], "st3"], writes=["junk", "st4"])
                P.act(lambda e: e.activation(out=st[:tp, 4:5], in_=st[:tp, 4:5], func=AF.Ln, scale=1.0 / 512, bias=epsc[:tp, 0:1]), reads=["st4", "epsc"], writes=["st4"])
                P.act(lambda e: e.activation(out=st[:tp, 4:5], in_=st[:tp, 4:5], func=AF.Exp, scale=-0.5), reads=["st4"], writes=["st4"])
                P.dve(lambda e: e.tensor_tensor(out=st[:tp, 2:3], in0=st[:tp, 3:4], in1=st[:tp, 4:5], op=ALU.mult), reads=["st3", "st4"], writes=["st2"])
                P.act((lambda b: lambda e: e.activation(out=vn_f[:tp, :], in_=ps[b][:tp, :], func=AF.Identity, scale=st[:tp, 4:5], bias=st[:tp, 2:3]))(b),
                      reads=[PSK[b], "st2", "st4"], writes=["vn_f"])
                P.dve(lambda e: e.tensor_tensor(out=vn_f[:tp, :], in0=vn_f[:tp, :], in1=gln_bc[:tp, :], op=ALU.mult), reads=["vn_f", "gln_bc"], writes=["vn_f"])
                P.act((lambda t: lambda e: e.copy(out=vn_b[:tp, t, :], in_=vn_f[:tp, :]))(t), reads=["vn_f"], writes=["vn_b"])
                if is_sample:
                    P.dma(XQ, "o_vn", lambda e: e.dma_start(out=nvn[l], in_=vn_f[:tp, :]), reads=["vn_f"])
            if SUB < 2:
                return
            for g4 in range(4):
                b = dense_bank()
                for t in range(TT):
                    rhs_w = ws64[:tp, g4, :] if is_sample else wsT[:, g4, :]
                    rhs_b = bsp64[0:1, g4, :] if is_sample else bsp_row[0:1, g4, :]
                    P.pe((lambda t, g4, rhs_w: lambda e: e.matmul(ps[b][:, t * tp:(t + 1) * tp], lhsT=vn_b[:tp, t, g4 * 128:(g4 + 1) * 128], rhs=rhs_w,
                                                                  start=True, stop=False))(t, g4, rhs_w),
                         reads=["vn_b", "wsT", "ws64"], writes=[PSK[b]])
                    P.pe((lambda t, rhs_b: lambda e: e.matmul(ps[b][:, t * tp:(t + 1) * tp], lhsT=ones_b[0:1, :], rhs=rhs_b, start=False, stop=True))(t, rhs_b),
                         reads=["ones_b", "bsp_row", "bsp64"], writes=[PSK[b]])
                P.dve((lambda b, g4: lambda e: e.tensor_tensor(out=fA[:, g4, 0:N], in0=ps[b][:, 0:N], in1=fA[:, g4, 0:N], op=ALU.mult))(b, g4),
                      reads=[PSK[b], ("fA", g4)], writes=[("fA", g4)])
            if SUB < 3:
                return
            for c in range(4):
                b = proj_F(w2, k2, c, N)
                silu_mul(b, N, fA[:, c, 0:N], ("fA", c), fA[:, c, 0:N], ("fA", c))
            fAk = [("fA", c) for c in range(4)]
            P.pool(lambda e: e.tensor_copy(out=small[:, 0:1], in_=small[:, 0:1]), reads=fAk, writes=["fA"])
            if SUB < 4:
                return
            branch_proj(l, 0, fA, "fA", N)
            if SUB < 5:
                return
            S0 = 32 if is_sample else 2
            sh = 16 if is_sample else 1
            w3, k3 = wload(*unit_in(l, 3))
            for c in range(4):
                b = proj_F(w3, k3, c, N)
                P.act((lambda b, c: lambda e: e.copy(out=fB[:, c, 0:N], in_=ps[b][:, 0:N]))(b, c), reads=[PSK[b]], writes=[("fB", c)])
            w4, k4 = wload(*unit_in(l, 4))
            for c in range(4):
                b = proj_F(w4, k4, c, N)
                P.act((lambda b, c: lambda e: e.copy(out=fC[:, c, 0:N], in_=ps[b][:, 0:N]))(b, c), reads=[PSK[b]], writes=[("fC", c)])
            w5, k5 = wload(*unit_in(l, 5))
            for c in range(4):
                b = proj_F(w5, k5, c, N)
                P.dve((lambda b, c: lambda e: e.tensor_tensor(out=zp[:, c, S0:S0 + N], in0=ps[b][:, 0:N], in1=fC[:, c, 0:N], op=ALU.mult))(b, c),
                      reads=[PSK[b], ("fC", c)], writes=[("zp", c)])
                P.dve((lambda c: lambda e: e.tensor_scalar(out=cacc[:, 0:N], in0=zp[:, c, 0:N], scalar1=cw_col[:, c, 0:1], scalar2=None, op0=ALU.mult))(c),
                      reads=[("zp", c), "cw_col"], writes=["cacc"])
                P.dve((lambda c: lambda e: e.scalar_tensor_tensor(out=cacc[:, 0:N], in0=zp[:, c, sh:sh + N], scalar=cw_col[:, c, 1:2], in1=cacc[:, 0:N],
                                                                   op0=ALU.mult, op1=ALU.add))(c), reads=[("zp", c), "cw_col", "cacc"], writes=["cacc"])
                P.dve((lambda c: lambda e: e.scalar_tensor_tensor(out=cacc[:, 0:N], in0=zp[:, c, 2 * sh:2 * sh + N], scalar=cw_col[:, c, 2:3], in1=cacc[:, 0:N],
                                                                   op0=ALU.mult, op1=ALU.add))(c), reads=[("zp", c), "cw_col", "cacc"], writes=["cacc"])
                P.dve((lambda c: lambda e: e.tensor_tensor(out=fB[:, c, 0:N], in0=cacc[:, 0:N], in1=fB[:, c, 0:N], op=ALU.mult))(c),
                      reads=["cacc", ("fB", c)], writes=[("fB", c)])
            if SUB < 6:
                return
            last = is_sample or (g == n_groups - 1)
            if last:
                ncol = 32 if is_sample else 2
                for c in range(4):
                    P.pe((lambda c: lambda e: e.transpose(out=ps[7][:ncol, c * 128:(c + 1) * 128], in_=zp[:, c, S0 + N - ncol:S0 + N], identity=ident[:]))(c),
                         reads=[("zp", c), "ident"], writes=[PSK[7]])
                P.act(lambda e: e.copy(out=ncst[:ncol, :], in_=ps[7][:ncol, :]), reads=[PSK[7]], writes=["ncst"])
                if is_sample:
                    for r in range(2):
                        P.dma(XQ, "o_nc", (lambda r: lambda e: e.dma_start(out=ncs[l, :, r, :], in_=ncst[r * 16:(r + 1) * 16, :]))(r), reads=["ncst"])
                else:
                    P.dma(XQ, "o_nc", lambda e: e.dma_start(out=ncp[l], in_=ncst[0:2, :]), reads=["ncst"])
            if not is_sample:
                for c in range(4):
                    P.act((lambda c: lambda e: e.copy(out=zp[:, c, 0:2], in_=zp[:, c, N:N + 2]))(c), reads=[("zp", c)], writes=[("zp", c)])
            w6, k6 = wload(*unit_in(l, 6))
            for c in range(4):
                b = proj_F(w6, k6, c, N)
                silu_mul(b, N, fB[:, c, 0:N], ("fB", c), fB[:, c, 0:N], ("fB", c))
            P.pool(lambda e: e.tensor_copy(out=small[:, 0:1], in_=small[:, 0:1]), reads=[("fB", c) for c in range(4)], writes=["fB"])
            branch_proj(l, 1, fB, "fB", N)

        def dense_back(l, N, TT, tp, xsrc, dst_rows, dslot, xkeys=(), okey="xout", hook=None):
            wo0, ko0 = wload(*unit_out(l, 0))
            wo1, ko1 = wload(*unit_out(l, 1))
            ykeys = [("yacc", i) for i in range(8)]
            xh = [(vn_f, "vn_f"), (cacc, "cacc")]
            if hook is not None:
                hook(-1)
            for t in range(TT):
                bs = []
                for hf, (wo, ko) in enumerate(((wo0, ko0), (wo1, ko1))):
                    b = 2 + hf
                    for k in range(8):
                        P.pe((lambda k, b, wo: lambda e: e.matmul(ps[b][:tp, :], lhsT=yacc[:, k, t * tp:(t + 1) * tp], rhs=wo[:, k, :],
                                                                  start=(k == 0), stop=(k == 7)))(k, b, wo),
                             reads=[ko] + ykeys, writes=[PSK[b]])
                    P.act((lambda b, hf: lambda e: e.activation(out=junk[:tp, 0:512], in_=ps[b][:tp, :], func=AF.Square,
                                                                accum_out=st[:tp, 5 + hf:6 + hf]))(b, hf),
                          reads=[PSK[b]], writes=["junk", "st56"])
                    bs.append(b)
                P.dve(lambda e: e.tensor_tensor(out=st[:tp, 5:6], in0=st[:tp, 5:6], in1=st[:tp, 6:7], op=ALU.add), reads=["st56"], writes=["st56"])
                rstd_from_sumsq(5, 7, tp, D, ["st56"], ["st7"])
                for hf in range(2):
                    xb, xbk = xh[hf]
                    P.dma(XQ, "xin%d" % (2 + hf), (lambda t, hf, xb: lambda e: e.dma_start(out=xb[:tp, :], in_=xsrc(t)[:, hf * 512:(hf + 1) * 512]))(t, hf, xb),
                          reads=list(xkeys), writes=[xbk])
                for hf in range(2):
                    b = bs[hf]
                    xb, xbk = xh[hf]
                    P.dve((lambda b, hf: lambda e: e.scalar_tensor_tensor(out=tmpf[:tp, :], in0=ps[b][:tp, :], scalar=st[:tp, 7:8],
                                                                          in1=gpost_bc[:tp, hf * 512:(hf + 1) * 512], op0=ALU.mult, op1=ALU.mult))(b, hf),
                          reads=[PSK[b], "st7", "gpost_bc"], writes=["tmpf"])
                    P.pool((lambda xb: lambda e: e.tensor_tensor(out=xb[:tp, :], in0=xb[:tp, :], in1=tmpf[:tp, :], op=ALU.add))(xb), reads=["tmpf", xbk], writes=[xbk])
                    P.dma(XQ, dslot, (lambda t, hf, xb: lambda e: e.dma_start(out=dst_rows(t)[:, hf * 512:(hf + 1) * 512], in_=xb[:tp, :]))(t, hf, xb),
                          reads=[xbk], writes=[(okey, l)])
                if hook is not None:
                    hook(t)

        def mem_kv_prompt(l):
            P.dma("pool", "lc0", lambda e: e.dma_start(out=gpost_bc[:], in_=g_mem[l:l + 1, :].partition_broadcast(128)), writes=["gpost_bc"])
            for t in range(2):
                norm_rows_to_T(memp[t * 128:(t + 1) * 128, :], 128, gpost_bc, "gpost_bc", memT, "fA", t * 128)
            wk, kk = wload(*unit_mem(l, 0))
            wv, kv = wload(*unit_mem(l, 1))
            for t in range(2):
                b = proj_T(wk, kk, t, 128, src=memT, srckey="fA")
                P.act((lambda b: lambda e: e.copy(out=kst[:, :], in_=ps[b][:, :]))(b), reads=[PSK[b]], writes=["kst"])
                P.dma(XQ, "o_mk", (lambda t: lambda e: e.dma_start(out=nmk[l, t * 128:(t + 1) * 128, :], in_=kst[:, :]))(t), reads=["kst"])
                b = proj_T(wv, kv, t, 128, src=memT, srckey="fA")
                P.act((lambda b: lambda e: e.copy(out=vst[:, :], in_=ps[b][:, :]))(b), reads=[PSK[b]], writes=["vst"])
                P.dve((lambda t: lambda e: e.tensor_copy(out=mv_b[:, t, :], in_=vst[:, :]))(t), reads=["vst"], writes=["mv_b"])
                P.dma(XQ, "o_mv", (lambda t: lambda e: e.dma_start(out=nmv[l, t * 128:(t + 1) * 128, :], in_=vst[:, :]))(t), reads=["vst"])
            for h4 in range(4):
                b = proj_F(wk, kk, h4, 256, src=memT, srckey="fA")
                P.act((lambda b, h4: lambda e: e.copy(out=mkT[:, h4, :], in_=ps[b][:, 0:256]))(b, h4), reads=[PSK[b]], writes=["mkT"])

        def prompt_group(l, g):
            N, TT, tp = 512, 4, 128
            src = xp if l == 0 else x1p
            dst = x1p if l < depth - 1 else yp

            def xsrc(t):
                return src[(g * 4 + t) * 128:(g * 4 + t + 1) * 128, :]

            def xdst(t):
                return dst[(g * 4 + t) * 128:(g * 4 + t + 1) * 128, :]

            if STAGE < 1:
                return
            xkeys = [("xout", l - 1)] if l > 0 else []
            dense_front(l, N, TT, tp, xsrc, False, g, xkeys, skip_norm=(PREF and g > 0))
            if STAGE < 2:
                return
            w7, k7 = wload(*unit_in(l, 7))
            for t in range(TT):
                b = proj_T(w7, k7, t, tp)
                rope(b, tp, rope_p, g * 4 + t, qa[:, t, :, :], 0.125, [("qa", t)])
            if KC < 1:
                return
            w8, k8 = wload(*unit_in(l, 8))
            for t in range(TT):
                jt = g * 4 + t
                b = proj_T(w8, k8, t, tp)
                rope(b, tp, rope_p, jt, kst[:, :].rearrange("p (h d) -> p h d", h=8), 1.0, ["kst"])
                P.dma(XQ, "o_k", (lambda jt: lambda e: e.dma_start(out=nkp[l, jt * 128:(jt + 1) * 128, :], in_=kst[:, :]))(jt), reads=["kst"])
                if KC < 2:
                    continue
                P.dve((lambda t: lambda e: e.tensor_tensor(out=tmpf[:, :].rearrange("p (h d) -> p h d", h=8), in0=qa[:, t, :, 0:64],
                                                          in1=kst[:, :].rearrange("p (h d) -> p h d", h=8), op=ALU.mult))(t),
                      reads=[("qa", t), "kst"], writes=["tmpf"])
                P.dve((lambda t: lambda e: e.tensor_reduce(out=lqq[:, t, :], in_=tmpf[:, :].rearrange("p (h d) -> p h d", h=8), axis=AX.X, op=ALU.add))(t),
                      reads=["tmpf"], writes=["lqq"])
                if KC < 3:
                    continue
                for half in range(2):
                    pb = 6 + half
                    for i in range(4):
                        h = half * 4 + i
                        P.pe((lambda h, i, pb: lambda e: e.transpose(out=ps[pb][0:64, i * 128:(i + 1) * 128], in_=kst[:, h * 64:(h + 1) * 64], identity=ident[:]))(h, i, pb),
                             reads=["kst", "ident"], writes=[PSK[pb]])
                    srcp = ps[pb][0:64, :].rearrange("p (a b) -> p a b", a=4)
                    if half == 0:
                        P.act((lambda srcp: lambda e: e.copy(out=ktst[:, 0:4, :], in_=srcp))(srcp), reads=[PSK[pb]], writes=["ktst"])
                    else:
                        P.dve((lambda srcp: lambda e: e.tensor_copy(out=ktst[:, 4:8, :], in_=srcp))(srcp), reads=[PSK[pb]], writes=["ktst"])
                P.dma(XQ, "kt_w", (lambda jt: lambda e: e.dma_start(out=KT_d[:, 0:64, jt * 128:(jt + 1) * 128].rearrange("h p k -> p h k"), in_=ktst[:, :, :]))(jt),
                      reads=["ktst", "KTind"], writes=[("KT_d", g)])
                if KC < 4:
                    continue
                for c in range(4):
                    P.pe((lambda c: lambda e: e.matmul(ps[5][:, c:c + 1], lhsT=kst[:, c * 128:(c + 1) * 128], rhs=ones_f[:, 0:1],
                                                      start=True, stop=True))(c),
                         reads=["kst", "ones_f"], writes=[PSK[5]])
                if jt % 2 == 0:
                    P.dve(lambda e: e.tensor_scalar(out=ksum[:, 0:4], in0=ps[5][:, 0:4], scalar1=1.0 / 256, scalar2=None, op0=ALU.mult),
                          reads=[PSK[5]], writes=["ksum"])
                else:
                    nb = jt // 2
                    KMf = KM[:].rearrange("p c n -> p (c n)")
                    for c in range(4):
                        P.dve((lambda c, nb: lambda e: e.scalar_tensor_tensor(out=KMf[0:64, c * 128 + (2 * c) * 16 + nb:c * 128 + (2 * c) * 16 + nb + 1], in0=ps[5][0:64, c:c + 1],
                                                                               scalar=1.0 / 256, in1=ksum[0:64, c:c + 1], op0=ALU.mult, op1=ALU.add))(c, nb),
                              reads=[PSK[5], "ksum"], writes=["KM"])
                        P.dve((lambda c, nb: lambda e: e.scalar_tensor_tensor(out=KMf[64:128, c * 128 + (2 * c + 1) * 16 + nb:c * 128 + (2 * c + 1) * 16 + nb + 1],
                                                                               in0=ps[5][64:128, c:c + 1], scalar=1.0 / 256, in1=ksum[64:128, c:c + 1], op0=ALU.mult, op1=ALU.add))(c, nb),
                              reads=[PSK[5], "ksum"], writes=["KM"])
            if KC < 5:
                return
            w9, k9 = wload(*unit_in(l, 9))
            for t in range(TT):
                jt = g * 4 + t
                b = proj_T(w9, k9, t, tp)
                P.act((lambda b: lambda e: e.copy(out=vst[:, :], in_=ps[b][:, :]))(b), reads=[PSK[b]], writes=["vst"])
                if KC < 6:
                    continue
                P.dma(XQ, "o_v", (lambda jt: lambda e: e.dma_start(out=nvp[l, jt * 128:(jt + 1) * 128, :], in_=vst[:, :]))(jt), reads=["vst"])
                P.dve((lambda b: lambda e: e.tensor_copy(out=vaug[:].rearrange("p c (s d) -> p c s d", s=3)[:, :, 0:3:2, :],
                                                        in_=ps[b][:, :].rearrange("p (c s d) -> p c s d", c=4, s=2)))(b), reads=[PSK[b]], writes=["vaug"])
                P.dma(XQ, "va_w", (lambda jt: lambda e: e.dma_start(out=VA_d[jt * 128:(jt + 1) * 128, :], in_=vaug[:].rearrange("p c f -> p (c f)")))(jt),
                      reads=["vaug", "VAones"], writes=[("VA_d", g)])
            if STAGE < 3:
                return
            for t in range(TT):
                jt = g * 4 + t
                own = jt // 2
                P.act((lambda t: lambda e: e.copy(out=vn_f[:, :].rearrange("p (h d) -> p h d", h=8), in_=qa[:, t, :, 0:64]))(t), reads=[("qa", t)], writes=["vn_f"])
                for c in range(4):
                    P.pe((lambda c, t: lambda e: e.transpose(out=ps[6][:, c * 128:(c + 1) * 128], in_=vn_f[:, c * 128:(c + 1) * 128], identity=ident[:]))(c, t),
                         reads=["vn_f", "ident"], writes=[PSK[6]])
                P.act(lambda e: e.copy(out=qT4[:].rearrange("p c q -> p (c q)"), in_=ps[6][:, :]), reads=[PSK[6]], writes=["qT4"])
                for c in range(4):
                    P.pe((lambda c: lambda e: e.matmul(ps[7][:, 0:128], lhsT=qT4[:, c, :], rhs=KM[:, c, :], start=(c == 0), stop=(c == 3)))(c),
                         reads=["qT4", "KM"], writes=[PSK[7]])
                P.dve(lambda e: e.tensor_copy(out=gm[:].rearrange("p h n -> p (h n)"), in_=ps[7][:, 0:128]), reads=[PSK[7]], writes=["gm"])
                if own < 16:
                    P.dve((lambda own: lambda e: e.memset(gm[:, :, own:16], -1e30))(own), reads=["gm"], writes=["gm"])
                for h in range(8):
                    P.dve((lambda h: lambda e: e.max(out=m8[:, h, :], in_=gm[:, h, :]))(h), reads=["gm"], writes=["m8"])
                P.dve(lambda e: e.tensor_tensor(out=selm[:], in0=gm[:], in1=m8[:, :, 2:3].to_broadcast([128, 8, 16]), op=ALU.is_ge), reads=["gm", "m8"], writes=["selm"])
                P.dve(lambda e: e.tensor_scalar(out=selm[:], in0=selm[:], scalar1=-1.0, scalar2=-NEG, op0=ALU.add, op1=ALU.mult), reads=["selm"], writes=["selm"])
                P.dve((lambda t: lambda e: e.tensor_tensor(out=qa[:, t, :, 64:80], in0=selm[:], in1=lqq[:, t, :].unsqueeze(2).to_broadcast([128, 8, 16]),
                                                          op=ALU.subtract))(t), reads=["selm", "lqq", ("qa", t)], writes=[("qa", t)])
                P.dve((lambda t, own: lambda e: e.tensor_scalar(out=qa[:, t, :, 64 + own:65 + own], in0=lqq[:, t, :].unsqueeze(2), scalar1=-1.0, scalar2=None,
                                                               op0=ALU.mult))(t, own), reads=["lqq", ("qa", t)], writes=[("qa", t)])
                for half in range(2):
                    pb = 6 + half
                    for i in range(4):
                        h = half * 4 + i
                        P.pe((lambda h, i, pb, t: lambda e: e.transpose(out=ps[pb][0:80, i * 128:(i + 1) * 128], in_=qa[:, t, h, :], identity=ident[:]))(h, i, pb, t),
                             reads=[("qa", t), "ident"], writes=[PSK[pb]])
                    srcp = ps[pb][0:80, :].rearrange("p (a b) -> p a b", a=4)
                    dstp = qTa[:, half * 4:(half + 1) * 4, t * 128:(t + 1) * 128]
                    if half == 0:
                        P.act((lambda srcp, dstp: lambda e: e.copy(out=dstp, in_=srcp))(srcp, dstp), reads=[PSK[pb]], writes=["qTa"])
                    else:
                        P.dve((lambda srcp, dstp: lambda e: e.tensor_copy(out=dstp, in_=srcp))(srcp, dstp), reads=[PSK[pb]], writes=["qTa"])
            if STAGE < 4:
                return
            njt = 4 * g + 4
            L = njt * 128
            hist_k = [("KT_d", gg) for gg in range(g + 1)] + ["KTind"]
            hist_v = [("VA_d", gg) for gg in range(g + 1)] + ["VAones"]
            for c in range(4):
                sl = 0
                P.dma("sp", "ktb%d" % sl, (lambda c, sl: lambda e: e.dma_start(out=ktb[sl][0:80, :, 0:L], in_=KT_d[2 * c:2 * c + 2, :, 0:L].rearrange("h p k -> p h k")))(c, sl),
                      reads=hist_k, writes=["ktb%d" % sl])
                P.dma("sp", "vab%d" % sl, (lambda c, sl: lambda e: e.dma_start(out=vab[sl][:, 0:njt, :],
                                                                               in_=VA_d[0:L, c * 192:(c + 1) * 192].rearrange("(j p) f -> p j f", p=128)))(c, sl),
                      reads=hist_v, writes=["vab%d" % sl])
                for hh in range(2):
                    h = 2 * c + hh
                    ob = 4 + hh
                    def qk(j):
                        sb = 2 + (j % 2)
                        diag = j >= 4 * g
                        P.pe((lambda j, sb, sl, hh, h, diag: lambda e: e.matmul(ps[sb][:, :], lhsT=ktb[sl][0:80, hh, j * 128:(j + 1) * 128], rhs=qTa[0:80, h, :],
                                                                                start=True, stop=(not diag)))(j, sb, sl, hh, h, diag),
                             reads=["ktb%d" % sl, "qTa"], writes=[PSK[sb]])
                        if diag:
                            P.pe((lambda j, sb: lambda e: e.matmul(ps[sb][:, :], lhsT=identb[:], rhs=trib[:, j - 4 * g, :], start=False, stop=True))(j, sb),
                                 reads=["identb", "trib"], writes=[PSK[sb]])

                    def ex_pv(j):
                        sb = 2 + (j % 2)
                        pi = j % 3
                        P.act((lambda sb, pi: lambda e: e.activation(out=pT[pi][:, :], in_=ps[sb][:, :], func=AF.Exp))(sb, pi),
                              reads=[PSK[sb]], writes=["pT%d" % pi])
                        P.pe((lambda j, pi, sl, hh, ob: lambda e: e.matmul(ps[ob][:, :], lhsT=vab[sl][:, j, hh * 64:hh * 64 + 128], rhs=pT[pi][:, :],
                                                                           start=(j == 0), stop=(j == njt - 1)))(j, pi, sl, hh, ob),
                             reads=["vab%d" % sl, "pT%d" % pi], writes=[PSK[ob]])

                    qk(0)
                    for j in range(njt):
                        if j + 1 < njt:
                            qk(j + 1)
                        ex_pv(j)
                    if hh == 0:
                        P.dve((lambda ob: lambda e: e.reciprocal(out=rs[64:128, :], in_=ps[ob][64:128, :]))(ob), reads=[PSK[ob]], writes=["rs"])
                        P.dve((lambda ob, c: lambda e: e.tensor_tensor(out=oc[0:64, c, :], in0=ps[ob][0:64, :], in1=rs[64:128, :], op=ALU.mult))(ob, c),
                              reads=[PSK[ob], "rs"], writes=[("fB", c)])
                    else:
                        P.dve((lambda ob: lambda e: e.reciprocal(out=rs[0:64, :], in_=ps[ob][0:64, :]))(ob), reads=[PSK[ob]], writes=["rs"])
                        P.dve((lambda ob, c: lambda e: e.tensor_tensor(out=oc[64:128, c, :], in0=ps[ob][64:128, :], in1=rs[0:64, :], op=ALU.mult))(ob, c),
                              reads=[PSK[ob], "rs"], writes=[("fB", c)])
            if STAGE < 5:
                return
            w10, k10 = wload(*unit_in(l, 10))
            for c in range(4):
                b = proj_F(w10, k10, c, N)
                silu_mul(b, N, fA[:, c, 0:N], ("fA", c), oc[:, c, 0:N], ("fB", c))
            P.pool(lambda e: e.tensor_copy(out=small[:, 0:1], in_=small[:, 0:1]), reads=[("fA", c) for c in range(4)], writes=["fA"])
            branch_proj(l, 2, fA, "fA", N)
            if STAGE < 6:
                return
            w11, k11 = wload(*unit_in(l, 11))
            for c in range(4):
                b = proj_F(w11, k11, c, N)
                P.act((lambda b, c: lambda e: e.copy(out=fC[:, c, 0:N], in_=ps[b][:, 0:N]))(b, c), reads=[PSK[b]], writes=[("fC", c)])
            sc = 128.0 ** -0.5
            for h4 in range(4):
                ob = 4 + (h4 % 2)
                db = 6 + (h4 % 2)
                for mt in range(2):
                    P.pe((lambda mt, h4: lambda e: e.matmul(ps[2 + mt][:, :], lhsT=mkT[:, h4, mt * 128:(mt + 1) * 128], rhs=fC[:, h4, 0:512], start=True, stop=True))(mt, h4),
                         reads=[("fC", h4), "mkT"], writes=[PSK[2 + mt]])
                for mt in range(2):
                    P.act((lambda mt: lambda e: e.activation(out=pT[mt][:, :], in_=ps[2 + mt][:, :], func=AF.Exp, scale=sc))(mt),
                          reads=[PSK[2 + mt]], writes=["pT%d" % mt])
                    P.pe((lambda mt, h4, ob: lambda e: e.matmul(ps[ob][:, :], lhsT=mv_b[:, mt, h4 * 128:(h4 + 1) * 128], rhs=pT[mt][:, :], start=(mt == 0), stop=(mt == 1)))(mt, h4, ob),
                         reads=["mv_b", "pT%d" % mt], writes=[PSK[ob]])
                    P.pe((lambda mt, db: lambda e: e.matmul(ps[db][:, :], lhsT=ones_b[:, :], rhs=pT[mt][:, :], start=(mt == 0), stop=(mt == 1)))(mt, db),
                         reads=["ones_b", "pT%d" % mt], writes=[PSK[db]])
                P.dve((lambda db: lambda e: e.reciprocal(out=rs[:, :], in_=ps[db][:, :]))(db), reads=[PSK[db]], writes=["rs"])
                P.dve((lambda ob, h4: lambda e: e.tensor_tensor(out=oc[:, h4, :], in0=ps[ob][:, :], in1=rs[:, :], op=ALU.mult))(ob, h4),
                      reads=[PSK[ob], "rs"], writes=[("fB", h4)])
            w12, k12 = wload(*unit_in(l, 12))
            for c in range(4):
                b = proj_F(w12, k12, c, N)
                silu_mul(b, N, fA[:, c, 0:N], ("fA", c), oc[:, c, 0:N], ("fB", c))
            P.pool(lambda e: e.tensor_copy(out=small[:, 0:1], in_=small[:, 0:1]), reads=[("fA", c) for c in range(4)], writes=["fA"])
            branch_proj(l, 3, fA, "fA", N)
            hook = None
            if PREF and g + 1 < n_groups:
                def xnext(t):
                    return src[((g + 1) * 4 + t) * 128:((g + 1) * 4 + t + 1) * 128, :]

                def hook(t):
                    if t >= 0:
                        norm_post(tp, hT, "hT", t * tp)
                    if t + 1 < TT:
                        norm_pre(xnext(t + 1), tp, gpre_bc, "gpre_bc", xkeys)
            dense_back(l, N, TT, tp, xsrc, xdst, "o_x", xkeys, hook=hook)

        def sample_setup():
            P.dma("pool", "lc0", lambda e: e.dma_start(out=pt_i[:], in_=ptab[0:1, :].partition_broadcast(128)), writes=["pt_i"])
            P.dve(lambda e: e.tensor_copy(out=pt_f[:], in_=pt_i[:]), reads=["pt_i"], writes=["pt_f"])
            P.dve(lambda e: e.tensor_scalar(out=pt_f[:], in0=pt_f[:], scalar1=float(PAGE), scalar2=iota[:, 0:1], op0=ALU.mult, op1=ALU.add),
                  reads=["pt_f", "iota"], writes=["pt_f"])
            P.dve(lambda e: e.tensor_copy(out=idx_i[:], in_=pt_f[:]), reads=["pt_f"], writes=["idx_i"])
            if depth > 1:
                P.dve(lambda e: e.tensor_scalar(out=pt_f[:], in0=pt_f[:], scalar1=float(NPOOL * PAGE), scalar2=None, op0=ALU.add), reads=["pt_f"], writes=["pt_f"])
                P.dve(lambda e: e.tensor_copy(out=pt_i[:], in_=pt_f[:]), reads=["pt_f"], writes=["idx_i"])

        def ocs_to_oc():
            for c in range(4):
                P.pe((lambda c: lambda e: e.transpose(out=ps[6][:, c * 64:(c + 1) * 64], in_=ocs[:64, c * 128:(c + 1) * 128], identity=ident[:64, :64]))(c),
                     reads=["ocs", "ident"], writes=[PSK[6]])
            P.act(lambda e: e.copy(out=oc[:, :, 0:64], in_=ps[6][:, 0:256].rearrange("p (c q) -> p c q", c=4)), reads=[PSK[6]],
                  writes=[("fB", c) for c in range(4)])

        def scatter_acc(R, selc, bl, first):
            P.pe(lambda e: e.matmul(ps[4][:64, :], lhsT=selc[:R, bl, :], rhs=On[:R, :], start=True, stop=True),
                 reads=["ncst", "rsel", "rselm"], writes=[PSK[4]])
            if first:
                P.dve(lambda e: e.tensor_copy(out=ocs[:64, :], in_=ps[4][:64, :]), reads=[PSK[4]], writes=["ocs"])
            else:
                P.dve(lambda e: e.tensor_tensor(out=ocs[:64, :], in0=ocs[:64, :], in1=ps[4][:64, :], op=ALU.add), reads=[PSK[4], "ocs"], writes=["ocs"])

        def sample_group(l):
            N, TT, tp = 64, 1, 64
            src = xs if l == 0 else x1s
            dst = x1s if l < depth - 1 else ys
            xkeys = [("xsout", l - 1)] if l > 0 else []

            def xsrc(t):
                return src[0:64, :]

            def xdst(t):
                return dst[0:64, :]

            for c in range(4):
                for r in range(2):
                    P.dma("pool", "lc%d" % c, (lambda c, r: lambda e: e.dma_start(out=zp[:, c, r * 16:(r + 1) * 16],
                                                                                  in_=sconv[l, :, r, c * 128:(c + 1) * 128].rearrange("b p -> p b")))(c, r),
                          writes=[("zp", c)])
            dense_front(l, N, TT, tp, xsrc, True, 0, xkeys)
            w7, k7 = wload(*unit_in(l, 7))
            b = proj_T(w7, k7, 0, tp)
            rope(b, tp, rope_s, None, qa[:64, 0, :, :], 0.125, [("qa", 0)])
            P.act(lambda e: e.copy(out=vn_f[:64, :].rearrange("p (h d) -> p h d", h=8), in_=qa[:64, 0, :, 0:64]), reads=[("qa", 0)], writes=["vn_f"])
            for c in range(4):
                P.pe((lambda c: lambda e: e.transpose(out=ps[6][:, c * 64:(c + 1) * 64], in_=vn_f[:64, c * 128:(c + 1) * 128], identity=ident[:64, :64]))(c),
                     reads=["vn_f", "ident"], writes=[PSK[6]])
            P.act(lambda e: e.copy(out=qTs[:].rearrange("p c q -> p (c q)"), in_=ps[6][:, 0:256]), reads=[PSK[6]], writes=["qTs"])
            w8, k8 = wload(*unit_in(l, 8))
            b = proj_T(w8, k8, 0, tp)
            rope(b, tp, rope_s, None, kst[:64, :].rearrange("p (h d) -> p h d", h=8), 1.0, ["kst"])
            P.dma(XQ, "o_k", lambda e: e.dma_start(out=nks[l], in_=kst[:64, :]), reads=["kst"])
            for c in range(4):
                P.pe((lambda c: lambda e: e.transpose(out=ps[7][:, c * 64:(c + 1) * 64], in_=kst[:64, c * 128:(c + 1) * 128], identity=ident[:64, :64]))(c),
                     reads=["kst", "ident"], writes=[PSK[7]])
            P.act(lambda e: e.copy(out=kTn[:].rearrange("p c q -> p (c q)"), in_=ps[7][:, 0:256]), reads=[PSK[7]], writes=["kTn"])
            w9, k9 = wload(*unit_in(l, 9))
            b = proj_T(w9, k9, 0, tp)
            P.act((lambda b: lambda e: e.copy(out=vst[:64, :], in_=ps[b][:64, :]))(b), reads=[PSK[b]], writes=["vst"])
            P.dma(XQ, "o_v", lambda e: e.dma_start(out=nvs[l], in_=vst[:64, :]), reads=["vst"])
            qTs4 = qTs[:].rearrange("p c (t b) -> p c t b", b=16)
            gcnt = [0, 0]
            assert l < 2
            idx_l = idx_i if l == 0 else pt_i
            kv_i = [0, 0]
            ck_flat = cache_k.rearrange("l n w -> (l n) w")
            cv_flat = cache_v.rearrange("l n w -> (l n) w")

            def issue_k(upto):
                while kv_i[0] < min(upto, NSB * NPAGES):
                    n = kv_i[0]
                    kv_i[0] += 1
                    ap, key = kslots[n % 3]
                    P.dma("pool", "gk%d" % (n % 3), (lambda ap, n: lambda e: e.indirect_dma_start(
                        out=ap, out_offset=None, in_=ck_flat, in_offset=bass.IndirectOffsetOnAxis(ap=idx_l[:, n:n + 1], axis=0)))(ap, n),
                        reads=["idx_i"], writes=[key])

            def issue_v(upto):
                while kv_i[1] < min(upto, NSB * NPAGES):
                    n = kv_i[1]
                    kv_i[1] += 1
                    ap, key = vslots[n % 4]
                    P.dma("pool", "gv%d" % (n % 4), (lambda ap, n: lambda e: e.indirect_dma_start(
                        out=ap, out_offset=None, in_=cv_flat, in_offset=bass.IndirectOffsetOnAxis(ap=idx_l[:, n:n + 1], axis=0)))(ap, n),
                        reads=["idx_i"], writes=[key])
            for bl in range(NSB):
                qsl = qTs4[:, :, :, bl:bl + 1].rearrange("p c t o -> p c (t o)").unsqueeze(2).to_broadcast([128, 4, 8, 4])
                P.dve((lambda qsl: lambda e: e.tensor_tensor(out=Qblk_f[:].rearrange("p c (h t) -> p c h t", h=8), in0=qsl,
                                                            in1=qmask[:].unsqueeze(3).to_broadcast([128, 4, 8, 4]), op=ALU.mult))(qsl),
                      reads=["qTs", "qmask"], writes=["Qblk_f"])
                P.dve(lambda e: e.tensor_copy(out=Qblk[:], in_=Qblk_f[:]), reads=["Qblk_f"], writes=["Qblk"])
                if bl == 0:
                    issue_k(3)
                    issue_v(4)
                for j in range(NPAGES):
                    n = bl * NPAGES + j
                    ksl, kkey = kslots[n % 3]
                    kb, kbkey = kb16[n % 2]
                    if n % 2 == 0:
                        P.act((lambda kb, ksl: lambda e: e.copy(out=kb, in_=ksl))(kb, ksl), reads=[kkey], writes=[kbkey])
                    else:
                        P.dve((lambda kb, ksl: lambda e: e.tensor_copy(out=kb, in_=ksl))(kb, ksl), reads=[kkey], writes=[kbkey])
                    issue_k(n + 4)
                    P.pe((lambda kb, j: lambda e: e.matmul(ps[5][:8, :], lhsT=blkind_b[:, j, :], rhs=kb, start=(j == 0), stop=(j == NPAGES - 1)))(kb, j),
                         reads=[kbkey, "blkind_b"], writes=[PSK[5]])
                    pb = 6 + (j % 2)
                    for c in range(4):
                        P.pe((lambda kb, c, pb: lambda e: e.matmul(ps[pb][:, c * 128:(c + 1) * 128], lhsT=kb[:, c * 128:(c + 1) * 128], rhs=identb[:, :], start=True, stop=True))(kb, c, pb),
                             reads=[kbkey, "identb"], writes=[PSK[pb]])
                    srcp = ps[pb][:, :].rearrange("p (c k) -> p c k", c=4)
                    dstp = KTs[:, :, j * 128:(j + 1) * 128]
                    if j % 2 == 1:
                        P.act((lambda srcp, dstp: lambda e: e.copy(out=dstp, in_=srcp))(srcp, dstp), reads=[PSK[pb]], writes=["ktb0"])
                    else:
                        P.dve((lambda srcp, dstp: lambda e: e.tensor_copy(out=dstp, in_=srcp))(srcp, dstp), reads=[PSK[pb]], writes=["ktb0"])
                P.act(lambda e: e.copy(out=tmpf[:8, :], in_=ps[5][:8, :]), reads=[PSK[5]], writes=["tmpf"])
                for c in range(4):
                    P.pe((lambda c: lambda e: e.transpose(out=ps[5][:, c * 8:(c + 1) * 8], in_=tmpf[:8, c * 128:(c + 1) * 128], identity=ident[:8, :8]))(c),
                         reads=["tmpf", "ident"], writes=[PSK[5]])
                P.act(lambda e: e.copy(out=kmT[:].rearrange("p c n -> p (c n)"), in_=ps[5][:, 0:32]), reads=[PSK[5]], writes=["kmT"])
                for c in range(4):
                    P.pe((lambda c: lambda e: e.matmul(ps[4][:32, 0:8], lhsT=Qblk_f[:, c, :], rhs=kmT[:, c, :], start=(c == 0), stop=(c == 3)))(c),
                         reads=["Qblk_f", "kmT"], writes=[PSK[4]])
                P.dve(lambda e: e.tensor_copy(out=gms[:, :], in_=ps[4][:32, 0:8]), reads=[PSK[4]], writes=["gms"])
                P.dve(lambda e: e.max(out=m8s[:, :], in_=gms[:, :]), reads=["gms"], writes=["m8s"])
                P.dve(lambda e: e.tensor_tensor(out=selb[:, :], in0=gms[:, :], in1=m8s[:, 2:3].to_broadcast([32, 8]), op=ALU.is_ge), reads=["gms", "m8s"], writes=["selb"])
                P.dve(lambda e: e.tensor_scalar(out=selb[:, :], in0=selb[:, :], scalar1=-1.0, scalar2=-NEG, op0=ALU.add, op1=ALU.mult), reads=["selb"], writes=["selb"])
                for qd in range(4):
                    sb = 2 + (qd % 2)
                    for c in range(4):
                        P.pe((lambda c, sb, qd: lambda e: e.matmul(ps[sb][:32, :], lhsT=Qblk[:, c, :], rhs=KTs[:, c, qd * 512:(qd + 1) * 512], start=(c == 0), stop=(c == 3)))(c, sb, qd),
                             reads=["Qblk", "ktb0"], writes=[PSK[sb]])
                    pmt = Pm[qd % 2]
                    pkey = "rs" if qd % 2 == 0 else "cacc"
                    for hf in range(2):
                        n = qd * 2 + hf
                        P.act((lambda sb, hf, n, pmt: lambda e: e.activation(out=pmt[:32, hf * 256:(hf + 1) * 256], in_=ps[sb][:32, hf * 256:(hf + 1) * 256], func=AF.Exp,
                                                                             bias=selb[:, n:n + 1], accum_out=den[:, n:n + 1]))(sb, hf, n, pmt),
                              reads=[PSK[sb], "selb"], writes=[pkey, "den"])
                    for jj in range(4):
                        j = qd * 4 + jj
                        P.pe((lambda jj, j, pmt: lambda e: e.transpose(out=ps[0][:, j * 32:(j + 1) * 32], in_=pmt[:32, jj * 128:(jj + 1) * 128], identity=ident[:32, :32]))(jj, j, pmt),
                             reads=[pkey, "ident"], writes=[PSK[0]])
                P.dve(lambda e: e.tensor_copy(out=fT[:, :], in_=ps[0][:, :]), reads=[PSK[0]], writes=["fT"])
                for c in range(4):
                    P.pe((lambda c: lambda e: e.matmul(ps[2][:32, 0:64], lhsT=Qblk[:, c, :], rhs=kTn[:, c, :], start=(c == 0), stop=(c == 3)))(c),
                         reads=["Qblk", "kTn"], writes=[PSK[2]])
                P.dve((lambda bl: lambda e: e.tensor_tensor(out=Po[:, :], in0=ps[2][:32, 0:64], in1=ownb[:32, bl, :], op=ALU.add))(bl), reads=[PSK[2], "ownb"], writes=["Po"])
                P.act(lambda e: e.activation(out=Po[:, :], in_=Po[:, :], func=AF.Exp, accum_out=den[:, 8:9]), reads=["Po"], writes=["Po", "den"])
                P.pe(lambda e: e.transpose(out=ps[3][:64, 0:32], in_=Po[:32, :], identity=ident[:32, :32]), reads=["Po", "ident"], writes=[PSK[3]])
                P.act(lambda e: e.copy(out=PTo[:, :], in_=ps[3][:64, 0:32]), reads=[PSK[3]], writes=["PTo"])
                for j in range(NPAGES):
                    n = bl * NPAGES + j
                    vsl, vkey = vslots[n % 4]
                    vb, vbkey = vb16[n % 2]
                    if n % 2 == 1:
                        P.act((lambda vb, vsl: lambda e: e.copy(out=vb, in_=vsl))(vb, vsl), reads=[vkey], writes=[vbkey])
                    else:
                        P.dve((lambda vb, vsl: lambda e: e.tensor_copy(out=vb, in_=vsl))(vb, vsl), reads=[vkey], writes=[vbkey])
                    issue_v(n + 5)
                    P.pe((lambda vb, j: lambda e: e.matmul(ps[1][:32, :], lhsT=PTb[:, j, :], rhs=vb, start=(j == 0), stop=False))(vb, j),
                         reads=[vbkey, "fT"], writes=[PSK[1]])
                P.pe(lambda e: e.matmul(ps[1][:32, :], lhsT=PTo[:64, :], rhs=vst[:64, :], start=False, stop=True), reads=["PTo", "vst"], writes=[PSK[1]])
                P.dve(lambda e: e.tensor_reduce(out=den[:, 9:10], in_=den[:, 0:9], axis=AX.X, op=ALU.add), reads=["den"], writes=["den"])
                P.dve(lambda e: e.reciprocal(out=den[:, 10:11], in_=den[:, 9:10]), reads=["den"], writes=["den"])
                P.dve(lambda e: e.scalar_tensor_tensor(out=On[:32, :], in0=ps[1][:32, :], scalar=den[:, 10:11], in1=mdiag[:32, :], op0=ALU.mult, op1=ALU.mult),
                      reads=[PSK[1], "den", "mdiag"], writes=["ncst"])
                scatter_acc(32, rsel, bl, bl == 0)
            ocs_to_oc()
            w10, k10 = wload(*unit_in(l, 10))
            for c in range(4):
                b = proj_F(w10, k10, c, N)
                silu_mul(b, N, fA[:, c, 0:N], ("fA", c), oc[:, c, 0:N], ("fB", c))
            P.pool(lambda e: e.tensor_copy(out=small[:, 0:1], in_=small[:, 0:1]), reads=[("fA", c) for c in range(4)], writes=["fA"])
            branch_proj(l, 2, fA, "fA", N)
            w11, k11 = wload(*unit_in(l, 11))
            for c in range(4):
                b = proj_F(w11, k11, c, N)
                P.act((lambda b, c: lambda e: e.copy(out=fC[:, c, 0:N], in_=ps[b][:, 0:N]))(b, c), reads=[PSK[b]], writes=[("fC", c)])
            P.pool(lambda e: e.tensor_copy(out=small[:, 0:1], in_=small[:, 0:1]), reads=[("fC", c) for c in range(4)], writes=["fC"])
            sc = 128.0 ** -0.5
            fC4 = fC[:, :, 0:64].rearrange("p c (t b) -> p c t b", b=16)
            Qm = Qblk[:, :, 0:16]
            for bl in range(NSB):
                qsl = fC4[:, :, :, bl:bl + 1].rearrange("p c t o -> p c (t o)").unsqueeze(2).to_broadcast([128, 4, 4, 4])
                P.dve((lambda qsl: lambda e: e.tensor_tensor(out=Qm.rearrange("p c (h t) -> p c h t", h=4), in0=qsl,
                                                            in1=mmask[:].unsqueeze(3).to_broadcast([128, 4, 4, 4]), op=ALU.mult))(qsl),
                      reads=["fC", "mmask"], writes=["Qblk"])
                for mt in range(2):
                    P.dma("sp", "mk%d" % mt, (lambda mt, bl: lambda e: e.dma_start(out=kring[:, mt, :], in_=cmk[l, bl, mt * 128:(mt + 1) * 128, :]))(mt, bl), writes=["kring%d" % mt])
                    P.dma("sp", "mv%d" % mt, (lambda mt, bl: lambda e: e.dma_start(out=vring[:, mt, :], in_=cmv[l, bl, mt * 128:(mt + 1) * 128, :]))(mt, bl), writes=["vring%d" % mt])
                    pb = 6 + mt
                    for c in range(4):
                        P.pe((lambda mt, c, pb: lambda e: e.transpose(out=ps[pb][:, c * 128:(c + 1) * 128], in_=kring[:, mt, c * 128:(c + 1) * 128], identity=ident[:]))(mt, c, pb),
                             reads=["kring%d" % mt, "ident"], writes=[PSK[pb]])
                    srcp = ps[pb][:, :].rearrange("p (c k) -> p c k", c=4)
                    dstp = mkT[:, :, mt * 128:(mt + 1) * 128]
                    P.act((lambda srcp, dstp: lambda e: e.copy(out=dstp, in_=srcp))(srcp, dstp), reads=[PSK[pb]], writes=["mkT"])
                for c in range(4):
                    P.pe((lambda c: lambda e: e.matmul(ps[2][:16, 0:256], lhsT=Qm[:, c, :], rhs=mkT[:, c, :], start=(c == 0), stop=(c == 3)))(c),
                         reads=["Qblk", "mkT"], writes=[PSK[2]])
                P.dve(lambda e: e.tensor_reduce(out=den[:16, 11:12], in_=ps[2][:16, 0:256], axis=AX.X, op=ALU.max), reads=[PSK[2]], writes=["den"])
                P.dve(lambda e: e.tensor_scalar(out=den[:16, 12:13], in0=den[:16, 11:12], scalar1=-sc, scalar2=None, op0=ALU.mult), reads=["den"], writes=["den"])
                P.act(lambda e: e.activation(out=rs[:16, 0:256], in_=ps[2][:16, 0:256], func=AF.Exp, scale=sc, bias=den[:16, 12:13], accum_out=den[:16, 13:14]),
                      reads=[PSK[2], "den"], writes=["rs", "den"])
                P.dve(lambda e: e.reciprocal(out=den[:16, 14:15], in_=den[:16, 13:14]), reads=["den"], writes=["den"])
                for mt in range(2):
                    P.pe((lambda mt: lambda e: e.transpose(out=ps[0][:, mt * 16:(mt + 1) * 16], in_=rs[:16, mt * 128:(mt + 1) * 128], identity=ident[:16, :16]))(mt),
                         reads=["rs", "ident"], writes=[PSK[0]])
                P.dve(lambda e: e.tensor_copy(out=vn_f[:, 0:32], in_=ps[0][:, 0:32]), reads=[PSK[0]], writes=["vn_f"])
                for mt in range(2):
                    P.pe((lambda mt: lambda e: e.matmul(ps[1][:16, :], lhsT=vn_f[:, mt * 16:(mt + 1) * 16], rhs=vring[:, mt, :], start=(mt == 0), stop=(mt == 1)))(mt),
                         reads=["vn_f", "vring%d" % mt], writes=[PSK[1]])
                P.dve(lambda e: e.scalar_tensor_tensor(out=On[:16, :], in0=ps[1][:16, :], scalar=den[:16, 14:15], in1=mdiagm[:16, :], op0=ALU.mult, op1=ALU.mult),
                      reads=[PSK[1], "den", "mdiagm"], writes=["ncst"])
                scatter_acc(16, rselm, bl, bl == 0)
            ocs_to_oc()
            w12, k12 = wload(*unit_in(l, 12))
            for c in range(4):
                b = proj_F(w12, k12, c, N)
                silu_mul(b, N, fA[:, c, 0:N], ("fA", c), oc[:, c, 0:N], ("fB", c))
            P.pool(lambda e: e.tensor_copy(out=small[:, 0:1], in_=small[:, 0:1]), reads=[("fA", c) for c in range(4)], writes=["fA"])
            branch_proj(l, 3, fA, "fA", N)
            dense_back(l, N, TT, tp, xsrc, xdst, "o_xs", xkeys, okey="xsout")

        if do_sample:
            sample_setup()
        for l in range(depth):
            layer_consts(l)
            if do_prompt:
                for c in range(4):
                    P.dve((lambda c: lambda e: e.memset(zp[:, c, 0:32], 0.0))(c), writes=[("zp", c)])
                P.dve(lambda e: e.memset(KM[:], 0.0), writes=["KM"])
                P.dve(lambda e: e.memset(vaug[:], 1.0), writes=["vaug"])
                mem_kv_prompt(l)
                P.dma("pool", "lc1", lambda e: e.dma_start(out=gpost_bc[:], in_=g_post[l:l + 1, :].partition_broadcast(128)), writes=["gpost_bc"])
                for g in range(n_groups):
                    prompt_group(l, g)
            if do_sample:
                sample_group(l)
        P.emit()
    return nc, hc


_CACHE = {}


def kernel(**inp):
    if "nc" not in _CACHE:
        _CACHE["nc"] = build()
    nc, hc = _CACHE["nc"]
    f = lambda a: np.ascontiguousarray(np.asarray(a, dtype=np.float32))
    ck = f(inp["cache_k"]).reshape(DEPTH, NPOOL * PAGE, W)
    cv = f(inp["cache_v"]).reshape(DEPTH, NPOOL * PAGE, W)
    shared = {
        "cache_k": ck, "cache_v": cv,
        "g_pre": f(inp["g_pre"]), "g_post": f(inp["g_post"]), "w_in": f(inp["w_in"]), "ln_v_gain": f(inp["ln_v_gain"]),
        "w_spatial": f(inp["w_spatial"]), "b_spatial": f(inp["b_spatial"]), "conv_w": f(inp["conv_w"]), "g_mem": f(inp["g_mem"]),
        "w_mem_kv": f(inp["w_mem_kv"]), "w_merge": f(inp["w_merge"]), "b_merge": f(inp["b_merge"]),
        "w_branch": f(inp["w_branch"]).reshape(DEPTH, 4 * W, D), "w_out": f(inp["w_out"]),
    }
    for k, v in hc.items():
        shared["c_" + k] = v
    xpr = f(inp["x_prompt"])
    xsm = f(inp["x_sample"])
    in_maps = []
    for c in range(8):
        b0 = c * NSB
        m = dict(shared)
        m["xp"] = xpr[c % 4]
        m["xs"] = np.ascontiguousarray(xsm[b0:b0 + NSB].transpose(1, 0, 2).reshape(NS, D))
        m["cmk"] = f(inp["cache_mem_k"])[:, b0:b0 + NSB].reshape(DEPTH, NSB, 256, W)
        m["cmv"] = f(inp["cache_mem_v"])[:, b0:b0 + NSB].reshape(DEPTH, NSB, 256, W)
        m["sconv"] = f(inp["state_conv"])[:, b0:b0 + NSB]
        m["ptab"] = np.ascontiguousarray(np.asarray(inp["page_table"], dtype=np.int32)[b0:b0 + NSB]).reshape(1, NSB * NPAGES)
        m["memp"] = f(inp["mem_prompt"])[c % 4]
        in_maps.append(m)
    res = run_bass_kernel_spmd(nc, in_maps, core_ids=list(range(8)))
    R = res.results
    y_prompt = np.stack([R[c]["yp"] for c in range(4)])
    tm = lambda a: a.reshape(-1, 4, NSB, a.shape[-1])
    y_sample = np.concatenate([R[c]["ys"].reshape(4, NSB, D).transpose(1, 0, 2) for c in range(8)], axis=0)
    nkp = np.stack([R[c]["nkp"] for c in range(4)], axis=1).reshape(DEPTH, 4, SEQ, 8, 64)
    nvp = np.stack([R[c]["nvp"] for c in range(4)], axis=1).reshape(DEPTH, 4, SEQ, 8, 64)
    ncp = np.stack([R[c]["ncp"] for c in range(4)], axis=1)
    nmk = np.stack([R[c]["nmk"] for c in range(4)], axis=1).reshape(DEPTH, 4, 256, 4, 128)
    nmv = np.stack([R[c]["nmv"] for c in range(4)], axis=1).reshape(DEPTH, 4, 256, 4, 128)
    nks = np.concatenate([R[c]["nks"].reshape(DEPTH, 4, NSB, W).transpose(0, 2, 1, 3) for c in range(8)], axis=1).reshape(DEPTH, 128, 4, 8, 64)
    nvs = np.concatenate([R[c]["nvs"].reshape(DEPTH, 4, NSB, W).transpose(0, 2, 1, 3) for c in range(8)], axis=1).reshape(DEPTH, 128, 4, 8, 64)
    ncs = np.concatenate([R[c]["ncs"] for c in range(8)], axis=1)
    nvn = np.concatenate([R[c]["nvn"].reshape(DEPTH, 4, NSB, W).transpose(0, 2, 1, 3) for c in range(8)], axis=1)
    out = (y_prompt, y_sample, nkp, nvp, ncp, nmk, nmv, nks, nvs, ncs, nvn)
    return tuple(np.ascontiguousarray(o, dtype=np.float32) for o in out)
```

```python
import contextlib
import os
STAGE = int(os.environ.get('KSTAGE', '9'))
SUB = int(os.environ.get('KSUB', '99'))
FIN = int(os.environ.get('KFIN', '99'))
KC = int(os.environ.get('KC', '99'))
KR = int(os.environ.get('KR', '99'))
PSRR = int(os.environ.get('PSRR', '1'))
NDB = int(os.environ.get('NDB', '4'))
PREF = int(os.environ.get('PREF', '1'))
XQ = os.environ.get('XQ', 'sp')
import numpy as np
import concourse.bass as bass
import concourse.mybir as mybir
from concourse.bass_utils import run_bass_kernel_spmd

F32 = mybir.dt.float32
BF16 = mybir.dt.bfloat16
I32 = mybir.dt.int32
ALU = mybir.AluOpType
AF = mybir.ActivationFunctionType
AX = mybir.AxisListType

ENGS = ["pe", "act", "dve", "pool", "sp"]
NEG = -30000.0
EPS = 1e-6


class Op:
    __slots__ = ("eng", "fn", "waits", "needs_inc", "val", "dma_slot", "dma_val")

    def __init__(self, eng, fn, dma_slot=None):
        self.eng = eng
        self.fn = fn
        self.waits = []
        self.needs_inc = False
        self.val = None
        self.dma_slot = dma_slot
        self.dma_val = None


import types


def _freeze(fn):
    if fn.__closure__ is None:
        return fn
    cells = []
    for c in fn.__closure__:
        try:
            cells.append(types.CellType(c.cell_contents))
        except ValueError:
            cells.append(c)
    return types.FunctionType(fn.__code__, fn.__globals__, fn.__name__, fn.__defaults__, tuple(cells))


class Prog:
    def __init__(self, nc, same_engine_sync=True):
        self.nc = nc
        self.ops = {e: [] for e in ENGS}
        self.last_w = {}
        self.readers = {}
        self.same_engine_sync = same_engine_sync
        self.dma_slots = {}

    def op(self, eng, fn, reads=(), writes=(), dma=None):
        o = Op(eng, _freeze(fn), dma_slot=dma)
        deps = []
        for r in reads:
            w = self.last_w.get(r)
            if w is not None:
                deps.append((w, "raw"))
            if PSRR and isinstance(r, str) and r.startswith("ps") and r[2:].isdigit():
                lastrd = {}
                for rd in self.readers.get(r, ()):
                    if rd.eng != eng:
                        lastrd[rd.eng] = rd
                for rd in lastrd.values():
                    deps.append((rd, "rar"))
        for wkey in writes:
            w = self.last_w.get(wkey)
            if w is not None:
                deps.append((w, "waw"))
            lastrd = {}
            for rd in self.readers.get(wkey, ()):
                if rd.dma_slot is not None:
                    deps.append((rd, "war"))
                else:
                    lastrd[rd.eng] = rd
            for rd in lastrd.values():
                deps.append((rd, "war"))
        seen = set()
        for d, kind in deps:
            if d is o or id(d) in seen:
                continue
            seen.add(id(d))
            if d.eng == o.eng and d.dma_slot is None and o.dma_slot is None:
                if o.eng == "pe" or not self.same_engine_sync:
                    continue
            o.waits.append(d)
            if d.dma_slot is None:
                d.needs_inc = True
        if dma is not None:
            st = self.dma_slots.setdefault(dma, [0, None])
            if st[1] is not None:
                o.waits.append(st[1])
            st[0] += 16
            o.dma_val = st[0]
            st[1] = o
        for r in reads:
            self.readers.setdefault(r, []).append(o)
        for wkey in writes:
            self.last_w[wkey] = o
            self.readers[wkey] = []
        self.ops[eng].append(o)
        return o

    def pe(self, fn, reads=(), writes=()):
        return self.op("pe", fn, reads, writes)

    def act(self, fn, reads=(), writes=()):
        return self.op("act", fn, reads, writes)

    def dve(self, fn, reads=(), writes=()):
        return self.op("dve", fn, reads, writes)

    def pool(self, fn, reads=(), writes=()):
        return self.op("pool", fn, reads, writes)

    def dma(self, eng, slot, fn, reads=(), writes=()):
        return self.op(eng, fn, reads, writes, dma=slot)

    def emit(self):
        nc = self.nc
        for e in ENGS:
            c = 0
            for o in self.ops[e]:
                if o.dma_slot is None and o.needs_inc:
                    c += 1
                    o.val = c
        slots = sorted(self.dma_slots.keys(), key=str)
        with contextlib.ExitStack() as es:
            esem = {e: es.enter_context(nc.semaphore("s_" + e)) for e in ENGS}
            dsem = {s: es.enter_context(nc.semaphore("d_%d" % i)) for i, s in enumerate(slots)}
            es.enter_context(nc.allow_non_contiguous_dma(reason="small strided parameter / layout DMAs"))
            block = es.enter_context(nc.Block())

            def run(ename, eng):
                waited = {}
                for o in self.ops[ename]:
                    for d in o.waits:
                        if d.dma_slot is not None:
                            sem, v = dsem[d.dma_slot], d.dma_val
                        else:
                            sem, v = esem[d.eng], d.val
                        k = id(sem)
                        if waited.get(k, 0) >= v:
                            continue
                        waited[k] = v
                        eng.wait_ge(sem, v)
                    ins = o.fn(eng)
                    if o.dma_slot is not None:
                        ins.then_inc(dsem[o.dma_slot], 16)
                    elif o.needs_inc:
                        ins.then_inc(esem[ename], 1)
                if ename == "sp":
                    for s in slots:
                        st = self.dma_slots[s]
                        if waited.get(id(dsem[s]), 0) < st[0]:
                            eng.wait_ge(dsem[s], st[0])

            @block.tensor
            def _(eng):
                run("pe", eng)

            @block.scalar
            def _(eng):
                run("act", eng)

            @block.vector
            def _(eng):
                run("dve", eng)

            @block.gpsimd
            def _(eng):
                run("pool", eng)

            @block.sync
            def _(eng):
                run("sp", eng)


D = 1024
SEQ = 4096
DEPTH = 2
NSB = 16
NS = 64
W = 512
NPARTS = 13
NPOOL = 2560
PAGE = 128
NPAGES = 16


def host_consts():
    c = {}
    c["ident"] = np.eye(128, dtype=np.float32)
    k = np.arange(128)[:, None, None]
    o = np.arange(4)[None, :, None]
    q = np.arange(512)[None, None, :]
    c["tri"] = np.where(q >= o * 128 + k, 0.0, NEG).astype(np.float32)
    n = np.arange(16)[:, None]
    key = np.arange(SEQ)[None, :]
    c["kind"] = (key // 256 == n).astype(np.float32)
    half = 8
    inv = np.power(np.float32(500000.0), -np.arange(half, dtype=np.float32) * np.float32(2.0 / 16)).astype(np.float32)
    pos = np.arange(SEQ, dtype=np.float32)
    ang = (pos[:, None] * inv[None, :]).astype(np.float32)
    cs = np.cos(ang).astype(np.float32).reshape(32, 128, 8).transpose(1, 0, 2)
    sn = np.sin(ang).astype(np.float32).reshape(32, 128, 8).transpose(1, 0, 2)
    c["rope_p"] = np.ascontiguousarray(np.stack([cs, sn], axis=2))
    pos_s = (2048 + np.arange(4, dtype=np.float32))
    ang_s = (pos_s[:, None] * inv[None, :]).astype(np.float32)
    cs_s = np.repeat(np.cos(ang_s).astype(np.float32), 16, axis=0)
    sn_s = np.repeat(np.sin(ang_s).astype(np.float32), 16, axis=0)
    rs = np.zeros((128, 2, 8), np.float32)
    rs[:64, 0] = cs_s
    rs[:64, 1] = sn_s
    c["rope_s"] = rs
    p = np.arange(64)
    dm = np.zeros((128, 4, 16), np.float32)
    for t in range(4):
        for blp in range(16):
            dm[:64, t, blp] = ((p % 16) == blp) & ((p // 16) <= t)
    c["dmask"] = dm
    r = np.arange(32)
    md = np.zeros((128, 512), np.float32)
    md[:32] = ((r // 4)[:, None] == (np.arange(512) // 64)[None, :])
    c["mdiag"] = md
    rsel = np.zeros((128, 16, 64), np.float32)
    for bl in range(16):
        for rr in range(32):
            rsel[rr, bl, (rr % 4) * 16 + bl] = 1.0
    c["rsel"] = rsel
    cown = np.zeros((128, 4), np.float32)
    cown[:32] = np.where(np.arange(4)[None, :] <= (r % 4)[:, None], 0.0, NEG)
    c["cown"] = cown
    selT = np.zeros((128, 16, 4), np.float32)
    for bl in range(16):
        for t in range(4):
            selT[t * 16 + bl, bl, t] = 1.0
    c["selT"] = selT
    r16 = np.arange(16)
    mdm = np.zeros((128, 512), np.float32)
    mdm[:16] = ((r16 // 4)[:, None] == (np.arange(512) // 128)[None, :])
    c["mdiagm"] = mdm
    rselm = np.zeros((128, 16, 64), np.float32)
    for bl in range(16):
        for rr in range(16):
            rselm[rr, bl, (rr % 4) * 16 + bl] = 1.0
    c["rselm"] = rselm
    iot = np.zeros((128, 1), np.float32)
    iot[:, 0] = np.arange(128)
    c["iota"] = iot
    pp = np.arange(128)
    qm = np.zeros((128, 4, 8), np.float32)
    for cc in range(4):
        for h in range(8):
            qm[:, cc, h] = (h == 2 * cc + pp // 64)
    c["qmask"] = qm
    mm = np.zeros((128, 4, 4), np.float32)
    for cc in range(4):
        mm[:, cc, cc] = 1.0
    c["mmask"] = mm
    bi = np.zeros((128, 16, 8), np.float32)
    for j in range(16):
        bi[:, j, j // 2] = 1.0 / 256
    c["blkind"] = bi
    ob = np.full((128, 16, 64), NEG, np.float32)
    for rr in range(32):
        t = rr % 4
        for bl in range(16):
            for t2 in range(t + 1):
                ob[rr, bl, t2 * 16 + bl] = 0.0
    c["ownb"] = ob
    return c


def build(do_prompt=True, do_sample=True, n_groups=8, depth=DEPTH):
    nc = bass.Bass("TRN2", target_bir_lowering=False)
    P = Prog(nc, same_engine_sync=bool(int(os.environ.get('KSES', '1'))))

    def din(name, shape, dt=F32):
        return nc.dram_tensor(name, list(shape), dt, kind="ExternalInput").ap()

    def dout(name, shape, dt=F32):
        return nc.dram_tensor(name, list(shape), dt, kind="ExternalOutput").ap()

    def dscr(name, shape, dt):
        return nc.dram_tensor(name, list(shape), dt).ap()

    xp = din("xp", [SEQ, D])
    xs = din("xs", [NS, D])
    cache_k = din("cache_k", [DEPTH, NPOOL * PAGE, W])
    cache_v = din("cache_v", [DEPTH, NPOOL * PAGE, W])
    cmk = din("cmk", [DEPTH, NSB, 256, W])
    cmv = din("cmv", [DEPTH, NSB, 256, W])
    sconv = din("sconv", [DEPTH, NSB, 2, W])
    ptab = din("ptab", [1, NSB * NPAGES], I32)
    memp = din("memp", [256, D])
    g_pre = din("g_pre", [DEPTH, D])
    g_post = din("g_post", [DEPTH, D])
    w_in = din("w_in", [DEPTH, D, NPARTS * W])
    ln_v_gain = din("ln_v_gain", [DEPTH, W])
    w_spatial = din("w_spatial", [DEPTH, 4, 128, 128])
    b_spatial = din("b_spatial", [DEPTH, 4, 128])
    conv_w = din("conv_w", [DEPTH, 3, W])
    g_mem = din("g_mem", [DEPTH, D])
    w_mem_kv = din("w_mem_kv", [DEPTH, D, D])
    w_merge = din("w_merge", [DEPTH, D, 4 * D])
    b_merge = din("b_merge", [DEPTH, 4 * D])
    w_branch = din("w_branch", [DEPTH, 4 * W, D])
    w_out = din("w_out", [DEPTH, D, D])
    hc = host_consts()
    cin = {k: din("c_" + k, v.shape) for k, v in hc.items()}

    yp = dout("yp", [SEQ, D])
    ys = dout("ys", [NS, D])
    nkp = dout("nkp", [DEPTH, SEQ, W])
    nvp = dout("nvp", [DEPTH, SEQ, W])
    ncp = dout("ncp", [DEPTH, 2, W])
    nmk = dout("nmk", [DEPTH, 256, W])
    nmv = dout("nmv", [DEPTH, 256, W])
    nks = dout("nks", [DEPTH, NS, W])
    nvs = dout("nvs", [DEPTH, NS, W])
    ncs = dout("ncs", [DEPTH, NSB, 2, W])
    nvn = dout("nvn", [DEPTH, NS, W])

    w_in_b = dscr("w_in_b", [DEPTH, NPARTS, 128, 8 * W], BF16)
    w_merge_b = dscr("w_merge_b", [DEPTH, 8, 128, 8 * W], BF16)
    w_branch_b = dscr("w_branch_b", [DEPTH, 4, 128, 4 * D], BF16)
    w_out_b = dscr("w_out_b", [DEPTH, 2, 128, 8 * W], BF16)
    w_mem_b = dscr("w_mem_b", [DEPTH, 2, 128, 8 * W], BF16)
    x1p = dscr("x1p", [SEQ, D], F32)
    x1s = dscr("x1s", [NS, D], F32)
    KT_d = dscr("KT_d", [8, 80, SEQ], BF16)
    VA_d = dscr("VA_d", [SEQ, 4 * 192], BF16)

    es = contextlib.ExitStack()
    with es:
        def T(name, shape, dt):
            return es.enter_context(nc.sbuf_tensor(name, list(shape), dt))

        ps = [es.enter_context(nc.psum_tensor("ps%d" % i, [128, 512], F32)) for i in range(8)]
        PSK = ["ps%d" % i for i in range(8)]

        ident = T("ident", [128, 128], F32)
        identb = T("identb", [128, 128], BF16)
        trib = T("trib", [128, 4, 512], BF16)
        ones_f = T("ones_f", [128, 128], F32)
        ones_b = T("ones_b", [128, 128], BF16)
        rope_p = T("rope_p", [128, 32, 2, 8], F32)
        rope_s = T("rope_s", [128, 2, 8], F32)
        dmask = T("dmask", [128, 4, 16], F32)
        mdiag = T("mdiag", [128, 512], F32)
        rsel = T("rsel", [128, 16, 64], F32)
        cown = T("cown", [128, 4], F32)
        selT = T("selT", [128, 16, 4], F32)
        mdiagm = T("mdiagm", [128, 512], F32)
        rselm = T("rselm", [128, 16, 64], F32)
        iota = T("iota", [128, 1], F32)
        epsc = T("epsc", [128, 1], F32)
        qmask = T("qmask", [128, 4, 8], F32)
        mmask = T("mmask", [128, 4, 4], F32)
        blkind = T("blkind", [128, 16, 8], F32)
        ownb = T("ownb", [128, 16, 64], F32)

        cq = [0]

        def cload(dst, src, key):
            cq[0] += 1
            P.dma("pool", "c%d" % (cq[0] % 4), lambda e: e.dma_start(out=dst, in_=src), writes=[key])

        cload(ident[:], cin["ident"], "ident")
        cload(rope_p[:], cin["rope_p"], "rope_p")
        cload(rope_s[:], cin["rope_s"], "rope_s")
        cload(dmask[:], cin["dmask"], "dmask")
        cload(mdiag[:], cin["mdiag"], "mdiag")
        cload(rsel[:], cin["rsel"], "rsel")
        cload(cown[:], cin["cown"], "cown")
        cload(selT[:], cin["selT"], "selT")
        cload(mdiagm[:], cin["mdiagm"], "mdiagm")
        cload(rselm[:], cin["rselm"], "rselm")
        cload(iota[:], cin["iota"], "iota")
        cload(qmask[:], cin["qmask"], "qmask")
        cload(mmask[:], cin["mmask"], "mmask")
        cload(blkind[:], cin["blkind"], "blkind")
        cload(ownb[:], cin["ownb"], "ownb")
        P.dma("pool", "c0", lambda e: e.dma_start(out=trib[:], in_=cin["tri"]), writes=["trib"])
        P.dve(lambda e: e.tensor_copy(out=identb[:], in_=ident[:]), reads=["ident"], writes=["identb"])
        P.dve(lambda e: e.memset(ones_f[:], 1.0), writes=["ones_f"])
        P.dve(lambda e: e.memset(ones_b[:], 1.0), writes=["ones_b"])
        P.dve(lambda e: e.memset(epsc[:], EPS), writes=["epsc"])
        for h in range(8):
            for q4 in range(4):
                P.dma("pool", "c1", (lambda h, q4: lambda e: e.dma_start(out=KT_d[h, 64:80, q4 * 1024:(q4 + 1) * 1024], in_=cin["kind"][:, q4 * 1024:(q4 + 1) * 1024]))(h, q4), writes=["KTind"])
        KT_ALL = ["ktb0"] + [("ktbq", q) for q in range(4)]
        VA_ALL = ["vab0"] + [("vabq", q) for q in range(4)]

        def _va_ones():
          P.dve(lambda e: e.memset(vab[0][:], 1.0), writes=VA_ALL)
          for c in range(4):
            P.dma("pool", "c2", (lambda c: lambda e: e.dma_start(
                out=VA_d.rearrange("(j p) f -> p j f", p=128)[:, :, c * 192 + 64:c * 192 + 128], in_=vab[0][:, :, 0:64]))(c),
                reads=VA_ALL, writes=["VAones"])

        wq = [0]

        def wconv(dst, src, key):
            wq[0] += 1
            P.dma("pool", "wc%d" % (wq[0] % 4), lambda e: e.dma_start(out=dst, in_=src), writes=[key])

        def kpc(ap2d):
            return ap2d.rearrange("(k p) c -> p k c", p=128)

        def conv_in(l, j):
            wconv(w_in_b[l, j].rearrange("p (k c) -> p k c", k=8), kpc(w_in[l, :, j * W:(j + 1) * W]), ("w_in_b", l, j))

        def conv_br(l, n):
            for hf in range(2):
                wconv(w_merge_b[l, n * 2 + hf].rearrange("p (k c) -> p k c", k=8), kpc(w_merge[l, :, n * D + hf * W:n * D + (hf + 1) * W]), ("w_merge_b", l, n, hf))
            wconv(w_branch_b[l, n].rearrange("p (k c) -> p k c", k=4), kpc(w_branch[l, n * W:(n + 1) * W, :]), ("w_branch_b", l, n))

        for l in range(depth):
            for hf in range(2):
                wconv(w_mem_b[l, hf].rearrange("p (k c) -> p k c", k=8), kpc(w_mem_kv[l, :, hf * W:(hf + 1) * W]), ("w_mem_b", l, hf))
            for j in (0, 1, 2):
                conv_in(l, j)
            conv_br(l, 0)
            for j in (3, 4, 5, 6):
                conv_in(l, j)
            conv_br(l, 1)
            for j in (7, 8, 9, 10):
                conv_in(l, j)
            conv_br(l, 2)
            for j in (11, 12):
                conv_in(l, j)
            conv_br(l, 3)
            for hf in range(2):
                wconv(w_out_b[l, hf].rearrange("p (k c) -> p k c", k=8), kpc(w_out[l, :, hf * W:(hf + 1) * W]), ("w_out_b", l, hf))

        NR = 3
        ring = [T("wr%d" % i, [128, 8, 512], BF16) for i in range(NR)]
        wcount = [0]

        def wload(src_ap_fn, srckey):
            i = wcount[0] % NR
            wcount[0] += 1
            key = "wr%d" % i
            P.dma("sp", "w%d" % i, lambda e: e.dma_start(out=src_ap_fn[0](ring[i]), in_=src_ap_fn[1]), reads=[srckey], writes=[key])
            return ring[i], key

        flat = (lambda r: r[:].rearrange("p k c -> p (k c)"))

        def unit_in(l, j):
            return (flat, w_in_b[l, j]), ("w_in_b", l, j)

        def unit_merge(l, n, hf):
            return (flat, w_merge_b[l, n * 2 + hf]), ("w_merge_b", l, n, hf)

        def unit_branch(l, n):
            return (flat, w_branch_b[l, n]), ("w_branch_b", l, n)

        def unit_out(l, hf):
            return (flat, w_out_b[l, hf]), ("w_out_b", l, hf)

        def unit_mem(l, hf):
            return (flat, w_mem_b[l, hf]), ("w_mem_b", l, hf)

        xt = T("xt", [128, D], F32)
        junk = T("junk", [128, D], BF16)
        st = T("st", [128, 8], F32)
        gpre_bc = T("gpre_bc", [128, D], F32)
        gpost_bc = T("gpost_bc", [128, D], F32)
        gln_bc = T("gln_bc", [128, W], F32)
        bm_col = T("bm_col", [128, 32], F32)
        cw_col = T("cw_col", [128, 4, 3], F32)
        hT = T("hT", [128, 8, 512], BF16)
        yacc = T("yacc", [128, 8, 512], BF16)
        fA = T("fA", [128, 4, 512], BF16)
        fB = T("fB", [128, 4, 512], BF16)
        fC = T("fC", [128, 4, 512], BF16)
        fT = T("fT", [128, 512], BF16)
        gT = T("gT", [128, 512], BF16)
        tmpf = T("tmpf", [128, 512], F32)
        zp = T("zp", [128, 4, 32 + 512], F32)
        cacc = T("cacc", [128, 512], F32)
        vn_b = T("vn_b", [128, 4, 512], BF16)
        vn_f = T("vn_f", [128, 512], F32)
        bnst = T("bnst", [128, 8], F32)
        wsT = T("wsT", [128, 4, 128], BF16)
        ws_nat = T("ws_nat", [128, 4, 128], F32)
        bsp_row = T("bsp_row", [1, 4, 128], BF16)
        bsp_f = T("bsp_f", [1, 4, 128], F32)
        ws64 = T("ws64", [128, 4, 64], BF16)
        w4bc = T("w4bc", [128, 4, 4], F32)
        bsp64 = T("bsp64", [1, 4, 64], BF16)
        qa = T("qa", [128, 4, 8, 80], F32)
        kst = T("kst", [128, 512], F32)
        vst = T("vst", [128, 512], F32)
        lqq = T("lqq", [128, 4, 8], F32)
        ktst = T("ktst", [64, 8, 128], BF16)
        vaug = T("vaug", [128, 4, 192], BF16)
        KM = T("KM", [128, 4, 128], BF16)
        qT4 = T("qT4", [128, 4, 128], BF16)
        gm = T("gm", [128, 8, 16], F32)
        m8 = T("m8", [128, 8, 8], F32)
        selm = T("selm", [128, 8, 16], F32)
        qTa = T("qTa", [80, 8, 512], BF16)
        ktb = [T("ktb0", [128, 2, SEQ], BF16)] * 2
        vab = [T("vab0", [128, 32, 192], BF16)] * 2
        pT = [T("pT%d" % i, [128, 512], BF16) for i in range(3)]
        rs = T("rs", [128, 512], F32)
        oc = fB
        memT = fA[:].rearrange("p c (a n) -> p (c a) n", a=2)
        mkT = T("mkT", [128, 4, 256], BF16)
        mv_b = T("mv_b", [128, 2, 512], BF16)
        pm = T("pm", [128, 256], F32)
        pmT = T("pmT", [128, 2, 128], BF16)
        small = T("small", [128, 16], F32)
        ncst = T("ncst", [32, 512], F32)
        ksum = T("ksum", [128, 4], F32)
        kring = T("kring", [128, 2, 512], F32)
        vring = T("vring", [128, 2, 512], F32)
        ocs = T("ocs", [64, 512], F32)
        pt_i = T("pt_i", [128, NSB * NPAGES], I32)
        pt_f = T("pt_f", [128, NSB * NPAGES], F32)
        idx_i = T("idx_i", [128, NSB * NPAGES], I32)
        qTs = T("qTs", [128, 4, 64], F32)
        kTn = T("kTn", [128, 4, 64], BF16)
        Qblk_f = T("Qblk_f", [128, 4, 32], F32)
        Qblk = T("Qblk", [128, 4, 32], BF16)
        kmT = T("kmT", [128, 4, 8], F32)
        gms = T("gms", [32, 8], F32)
        m8s = T("m8s", [32, 8], F32)
        selb = T("selb", [32, 8], F32)
        den = T("den", [32, 16], F32)
        PTo = T("PTo", [64, 32], F32)
        Po = T("Po", [32, 64], F32)
        KTs = ktb[0][:].rearrange("p a (b k) -> p (a b) k", b=2)
        blkind_b = T("blkind_b", [128, 16, 8], BF16)
        P.dve(lambda e: e.tensor_copy(out=blkind_b[:], in_=blkind[:]), reads=["blkind"], writes=["blkind_b"])
        PTb = fT[:].rearrange("p (j r) -> p j r", j=16)
        kslots = [(kring[:, 0, :], "kring0"), (kring[:, 1, :], "kring1"), (qa[:, 1, :, :].rearrange("p h d -> p (h d)")[:, 0:512], ("qa", 1))]
        vslots = [(vring[:, 0, :], "vring0"), (vring[:, 1, :], "vring1"), (qa[:, 2, :, :].rearrange("p h d -> p (h d)")[:, 0:512], ("qa", 2)),
                  (qa[:, 3, :, :].rearrange("p h d -> p (h d)")[:, 0:512], ("qa", 3))]
        kb16 = [(pT[0][:, :], "pT0"), (pT[1][:, :], "pT1")]
        vb16 = [(pT[2][:, :], "pT2"), (junk[:, 0:512], "junk")]
        Pm = [rs, cacc]
        PT = vn_f[:].rearrange("p (j r) -> p j r", j=16)
        On = ncst
        P.pool(lambda e: e.memset(small[:], 0.0), writes=["small"])
        _va_ones()

        def rstd_from_sumsq(col_in, col_out, np_, n_elem, keys_r, keys_w):
            P.act(lambda e: e.activation(out=st[:np_, col_out:col_out + 1], in_=st[:np_, col_in:col_in + 1], func=AF.Ln,
                                         scale=1.0 / n_elem, bias=epsc[:np_, 0:1]), reads=keys_r + ["epsc"], writes=keys_w)
            P.act(lambda e: e.activation(out=st[:np_, col_out:col_out + 1], in_=st[:np_, col_out:col_out + 1], func=AF.Exp,
                                         scale=-0.5), reads=keys_w, writes=keys_w)

        def norm_rows_to_T(src_rows_ap, np_, gbc, gkey, dstT, dst_key, col0, xkeys=()):
            norm_pre(src_rows_ap, np_, gbc, gkey, xkeys)
            norm_post(np_, dstT, dst_key, col0)

        def norm_pre(src_rows_ap, np_, gbc, gkey, xkeys=()):
            P.dma(XQ, "xin", lambda e: e.dma_start(out=xt[:np_, :], in_=src_rows_ap), reads=list(xkeys), writes=["xt"])
            P.act(lambda e: e.activation(out=junk[:np_, :], in_=xt[:np_, :], func=AF.Square, accum_out=st[:np_, 0:1]),
                  reads=["xt"], writes=["junk", "st0"])
            rstd_from_sumsq(0, 1, np_, D, ["st0"], ["st1"])
            P.dve(lambda e: e.scalar_tensor_tensor(out=xt[:np_, :], in0=xt[:np_, :], scalar=st[:np_, 1:2], in1=gbc[:np_, :],
                                                   op0=ALU.mult, op1=ALU.mult), reads=["xt", "st1", gkey], writes=["xt"])

        def norm_post(np_, dstT, dst_key, col0):
            for half in range(2):
                pb = 6 + half
                for i in range(4):
                    k = half * 4 + i
                    P.pe((lambda k, i, pb: lambda e: e.transpose(out=ps[pb][:, i * 128:i * 128 + np_], in_=xt[:np_, k * 128:(k + 1) * 128],
                                                                 identity=ident[:np_, :np_]))(k, i, pb),
                         reads=["xt", "ident"], writes=[PSK[pb]])
                src = ps[pb][:].rearrange("p (a b) -> p a b", a=4)[:, :, 0:np_]
                dst = dstT[:, half * 4:(half + 1) * 4, col0:col0 + np_]
                if half == 0:
                    P.act((lambda src, dst: lambda e: e.copy(out=dst, in_=src))(src, dst), reads=[PSK[pb]], writes=[dst_key])
                else:
                    P.dve((lambda src, dst: lambda e: e.tensor_copy(out=dst, in_=src))(src, dst), reads=[PSK[pb]], writes=[dst_key])

        dps = [0]

        def dense_bank():
            dps[0] = (dps[0] + 1) % NDB
            return dps[0]

        def proj_F(wt, wkey, c, N, src=None, srckey="hT", nk=8, cols=None):
            b = dense_bank()
            s = hT if src is None else src
            for k in range(nk):
                lw = wt[:, k, c * 128:(c + 1) * 128] if cols is None else cols(k)
                P.pe((lambda k, lw: lambda e: e.matmul(ps[b][:, 0:N], lhsT=lw, rhs=s[:, k, 0:N], start=(k == 0), stop=(k == nk - 1)))(k, lw),
                     reads=[wkey, srckey], writes=[PSK[b]])
            return b

        def proj_T(wt, wkey, t, tp, src=None, srckey="hT"):
            b = dense_bank()
            s = hT if src is None else src
            for k in range(8):
                P.pe((lambda k: lambda e: e.matmul(ps[b][:tp, 0:512], lhsT=s[:, k, t * tp:(t + 1) * tp], rhs=wt[:, k, :],
                                                   start=(k == 0), stop=(k == 7)))(k),
                     reads=[wkey, srckey], writes=[PSK[b]])
            return b

        def rope(b, tp, tab, jt, dst3, scale, keys_w):
            src = ps[b][:tp, :].rearrange("p (h d) -> p h d", h=8)
            if jt is None:
                cs = tab[:tp, 0, :].unsqueeze(1).to_broadcast([tp, 8, 8])
                sn = tab[:tp, 1, :].unsqueeze(1).to_broadcast([tp, 8, 8])
            else:
                cs = tab[:tp, jt, 0, :].unsqueeze(1).to_broadcast([tp, 8, 8])
                sn = tab[:tp, jt, 1, :].unsqueeze(1).to_broadcast([tp, 8, 8])
            t1 = tmpf[:tp, 0:64].rearrange("p (h d) -> p h d", h=8)
            t2 = tmpf[:tp, 64:128].rearrange("p (h d) -> p h d", h=8)
            x1 = src[:, :, 0:8]
            x2 = src[:, :, 8:16]
            rk = [PSK[b], "rope"]
            if KR < 1:
                return
            P.act(lambda e: e.mul(out=dst3[:, :, 16:64], in_=src[:, :, 16:64], mul=scale), reads=[PSK[b]], writes=keys_w)
            if KR < 2:
                return
            P.dve(lambda e: e.tensor_tensor(out=t1, in0=x1, in1=cs, op=ALU.mult), reads=rk, writes=["tmpf"])
            P.dve(lambda e: e.tensor_tensor(out=t2, in0=x2, in1=sn, op=ALU.mult), reads=rk, writes=["tmpf"])
            P.dve(lambda e: e.tensor_tensor(out=dst3[:, :, 0:8], in0=t1, in1=t2, op=ALU.subtract), reads=["tmpf"], writes=keys_w)
            if KR < 3:
                return
            P.dve(lambda e: e.tensor_tensor(out=t1, in0=x2, in1=cs, op=ALU.mult), reads=rk, writes=["tmpf"])
            P.dve(lambda e: e.tensor_tensor(out=t2, in0=x1, in1=sn, op=ALU.mult), reads=rk, writes=["tmpf"])
            P.dve(lambda e: e.tensor_tensor(out=dst3[:, :, 8:16], in0=t1, in1=t2, op=ALU.add), reads=["tmpf"], writes=keys_w)
            if scale != 1.0:
                P.dve(lambda e: e.tensor_scalar(out=dst3[:, :, 0:16], in0=dst3[:, :, 0:16], scalar1=scale, scalar2=None, op0=ALU.mult),
                      reads=keys_w, writes=keys_w)

        def silu_mul(b, N, dst, dkey, other, okey):
            P.act(lambda e: e.activation(out=fT[:, 0:N], in_=ps[b][:, 0:N], func=AF.Silu), reads=[PSK[b]], writes=["fT"])
            P.pool(lambda e: e.tensor_tensor(out=dst, in0=fT[:, 0:N], in1=other, op=ALU.mult), reads=["fT", okey], writes=[dkey])

        def branch_proj(l, n, brT, brkey, N):
            wm0, km0 = wload(*unit_merge(l, n, 0))
            wm1, km1 = wload(*unit_merge(l, n, 1))
            wb, kb = wload(*unit_branch(l, n))
            wbv = wb[:].rearrange("p (a b) c -> p a (b c)", a=4)
            for ocn in range(8):
                wm, km = (wm0, km0) if ocn < 4 else (wm1, km1)
                bg = proj_F(wm, km, ocn % 4, N)
                P.act((lambda bg, ocn: lambda e: e.activation(out=gT[:, 0:N], in_=ps[bg][:, 0:N], func=AF.Sigmoid,
                                                              bias=bm_col[:, n * 8 + ocn:n * 8 + ocn + 1]))(bg, ocn),
                      reads=[PSK[bg], "bm_col"], writes=["gT"])
                bp = proj_F(wb, kb, ocn, N, src=brT, srckey=brkey, nk=4, cols=(lambda ocn: lambda k: wbv[:, k, ocn * 128:(ocn + 1) * 128])(ocn))
                if n == 0:
                    P.dve((lambda bp, ocn: lambda e: e.tensor_tensor(out=yacc[:, ocn, 0:N], in0=ps[bp][:, 0:N], in1=gT[:, 0:N], op=ALU.mult))(bp, ocn),
                          reads=[PSK[bp], "gT"], writes=[("yacc", ocn)])
                else:
                    P.dve((lambda bp: lambda e: e.tensor_tensor(out=fT[:, 0:N], in0=ps[bp][:, 0:N], in1=gT[:, 0:N], op=ALU.mult))(bp),
                          reads=[PSK[bp], "gT"], writes=["fT"])
                    P.pool((lambda ocn: lambda e: e.tensor_tensor(out=yacc[:, ocn, 0:N], in0=yacc[:, ocn, 0:N], in1=fT[:, 0:N], op=ALU.add))(ocn),
                           reads=["fT", ("yacc", ocn)], writes=[("yacc", ocn)])

        def layer_consts(l):
            P.dma("pool", "lc0", lambda e: e.dma_start(out=gpre_bc[:], in_=g_pre[l:l + 1, :].partition_broadcast(128)), writes=["gpre_bc"])
            P.dma("pool", "lc1", lambda e: e.dma_start(out=gpost_bc[:], in_=g_post[l:l + 1, :].partition_broadcast(128)), writes=["gpost_bc"])
            P.dma("pool", "lc2", lambda e: e.dma_start(out=gln_bc[:], in_=ln_v_gain[l:l + 1, :].partition_broadcast(128)), writes=["gln_bc"])
            with nc.allow_non_contiguous_dma(reason="small per-layer vectors"):
                P.dma("pool", "lc3", lambda e: e.dma_start(out=bm_col[:], in_=b_merge[l].rearrange("(a p) -> p a", p=128)), writes=["bm_col"])
                for j3 in range(3):
                    P.dma("pool", "lc0", (lambda j3: lambda e: e.dma_start(out=cw_col[:, :, j3], in_=conv_w[l, j3].rearrange("(c p) -> p c", p=128)))(j3), writes=["cw_col"])
            P.dma("pool", "lc1", lambda e: e.dma_start(out=ws_nat[:], in_=w_spatial[l].rearrange("g t s -> t g s")), writes=["ws_nat"])
            for g4 in range(4):
                P.pe((lambda g4: lambda e: e.transpose(out=ps[6][:, g4 * 128:(g4 + 1) * 128], in_=ws_nat[:, g4, :], identity=ident[:]))(g4),
                     reads=["ws_nat", "ident"], writes=[PSK[6]])
            P.act(lambda e: e.copy(out=tmpf[:, :], in_=ps[6][:, :]), reads=[PSK[6]], writes=["tmpf"])
            P.pool(lambda e: e.affine_select(out=tmpf[:].rearrange("p (g t) -> p g t", g=4), in_=tmpf[:].rearrange("p (g t) -> p g t", g=4),
                                             pattern=[[0, 4], [1, 128]], compare_op=ALU.is_ge, fill=0.0, base=0, channel_multiplier=-1),
                   reads=["tmpf"], writes=["tmpf"])
            P.dve(lambda e: e.tensor_copy(out=wsT[:].rearrange("p g t -> p (g t)"), in_=tmpf[:, :]), reads=["tmpf"], writes=["wsT"])
            P.dma("pool", "lc2", lambda e: e.dma_start(out=bsp_f[:], in_=b_spatial[l:l + 1, :, :]), writes=["bsp_f"])
            P.dve(lambda e: e.tensor_copy(out=bsp_row[:], in_=bsp_f[:]), reads=["bsp_f"], writes=["bsp_row"])
            with nc.allow_non_contiguous_dma(reason="tiny 4x4 spatial block"):
                for s in range(4):
                    for g4 in range(4):
                        P.dma("pool", "lc3", (lambda s, g4: lambda e: e.dma_start(
                            out=w4bc[s * 16:(s + 1) * 16, g4, :],
                            in_=w_spatial[l, g4, 0:4, s:s + 1].rearrange("t o -> o t").partition_broadcast(16)))(s, g4), writes=["w4bc"])
            for g4 in range(4):
                for t in range(4):
                    P.dve((lambda g4, t: lambda e: e.tensor_scalar(out=ws64[:64, g4, t * 16:(t + 1) * 16], in0=dmask[:64, t, :],
                                                                    scalar1=w4bc[:64, g4, t:t + 1], scalar2=None, op0=ALU.mult))(g4, t),
                          reads=["w4bc", "dmask"], writes=["ws64"])
            P.dve(lambda e: e.tensor_copy(out=bsp64[:].rearrange("o g (t b) -> o g t b", b=16),
                                          in_=bsp_f[0:1, :, 0:4].unsqueeze(3).to_broadcast([1, 4, 4, 16])), reads=["bsp_f"], writes=["bsp64"])

        def dense_front(l, N, TT, tp, xsrc, is_sample, g, xkeys=(), skip_norm=False):
            for t in range(TT):
                if not skip_norm:
                    norm_rows_to_T(xsrc(t), tp, gpre_bc, "gpre_bc", hT, "hT", t * tp, xkeys)
            if SUB < 1:
                return
            w0, k0 = wload(*unit_in(l, 0))
            if os.environ.get('KW1'):
                return
            w1, k1 = wload(*unit_in(l, 1))
            w2, k2 = wload(*unit_in(l, 2))
            for c in range(4):
                b = proj_F(w0, k0, c, N)
                P.act((lambda b, c: lambda e: e.copy(out=fA[:, c, 0:N], in_=ps[b][:, 0:N]))(b, c), reads=[PSK[b]], writes=[("fA", c)])
            if FIN < 1:
                return
            for t in range(TT):
                b = proj_T(w1, k1, t, tp)
                if FIN < 2:
                    continue
                P.act((lambda b: lambda e: e.activation(out=junk[:tp, 0:512], in_=ps[b][:tp, :], func=AF.Identity, accum_out=st[:tp, 2:3]))(b),
                      reads=[PSK[b]], writes=["junk", "st2"])
                P.dve(lambda e: e.tensor_scalar(out=st[:tp, 3:4], in0=st[:tp, 2:3], scalar1=-1.0 / 512, scalar2=None, op0=ALU.mult), reads=["st2"], writes=["st3"])
                P.act((lambda b: lambda e: e.activation(out=junk[:tp, 0:512], in_=ps[b][:tp, :], func=AF.Square, bias=st[:tp, 3:4], accum_out=st[:tp, 4:5]))(b),
                      reads=[PSK[b], "st3"], writes=["junk", "st4"])
                P.act(lambda e: e.activation(out=st[:tp, 4:5], in_=st[:tp, 4:5], func=AF.Ln, scale=1.0 / 512, bias=epsc[:tp, 0:1]), reads=["st4", "epsc"], writes=["st4"])
                P.act(lambda e: e.activation(out=st[:tp, 4:5], in_=st[:tp, 4:5], func=AF.Exp, scale=-0.5), reads=["st4"], writes=["st4"])
                P.dve(lambda e: e.tensor_tensor(out=st[:tp, 2:3], in0=st[:tp, 3:4], in1=st[:tp, 4:5], op=ALU.mult), reads=["st3", "st4"], writes=["st2"])
                P.act((lambda b: lambda e: e.activation(out=vn_f[:tp, :], in_=ps[b][:tp, :], func=AF.Identity, scale=st[:tp, 4:5], bias=st[:tp, 2:3]))(b),
                      reads=[PSK[b], "st2", "st4"], writes=["vn_f"])
                P.dve(lambda e: e.tensor_tensor(out=vn_f[:tp, :], in0=vn_f[:tp, :], in1=gln_bc[:tp, :], op=ALU.mult), reads=["vn_f", "gln_bc"], writes=["vn_f"])
                P.act((lambda t: lambda e: e.copy(out=vn_b[:tp, t, :], in_=vn_f[:tp, :]))(t), reads=["vn_f"], writes=["vn_b"])
                if is_sample:
                    P.dma(XQ, "o_vn", lambda e: e.dma_start(out=nvn[l], in_=vn_f[:tp, :]), reads=["vn_f"])
            if SUB < 2:
                return
            for g4 in range(4):
                b = dense_bank()
                for t in range(TT):
                    rhs_w = ws64[:tp, g4, :] if is_sample else wsT[:, g4, :]
                    rhs_b = bsp64[0:1, g4, :] if is_sample else bsp_row[0:1, g4, :]
                    P.pe((lambda t, g4, rhs_w: lambda e: e.matmul(ps[b][:, t * tp:(t + 1) * tp], lhsT=vn_b[:tp, t, g4 * 128:(g4 + 1) * 128], rhs=rhs_w,
                                                                  start=True, stop=False))(t, g4, rhs_w),
                         reads=["vn_b", "wsT", "ws64"], writes=[PSK[b]])
                    P.pe((lambda t, rhs_b: lambda e: e.matmul(ps[b][:, t * tp:(t + 1) * tp], lhsT=ones_b[0:1, :], rhs=rhs_b, start=False, stop=True))(t, rhs_b),
                         reads=["ones_b", "bsp_row", "bsp64"], writes=[PSK[b]])
                P.dve((lambda b, g4: lambda e: e.tensor_tensor(out=fA[:, g4, 0:N], in0=ps[b][:, 0:N], in1=fA[:, g4, 0:N], op=ALU.mult))(b, g4),
                      reads=[PSK[b], ("fA", g4)], writes=[("fA", g4)])
            if SUB < 3:
                return
            for c in range(4):
                b = proj_F(w2, k2, c, N)
                silu_mul(b, N, fA[:, c, 0:N], ("fA", c), fA[:, c, 0:N], ("fA", c))
            fAk = [("fA", c) for c in range(4)]
            P.pool(lambda e: e.tensor_copy(out=small[:, 0:1], in_=small[:, 0:1]), reads=fAk, writes=["fA"])
            if SUB < 4:
                return
            branch_proj(l, 0, fA, "fA", N)
            if SUB < 5:
                return
            S0 = 32 if is_sample else 2
            sh = 16 if is_sample else 1
            w3, k3 = wload(*unit_in(l, 3))
            for c in range(4):
                b = proj_F(w3, k3, c, N)
                P.act((lambda b, c: lambda e: e.copy(out=fB[:, c, 0:N], in_=ps[b][:, 0:N]))(b, c), reads=[PSK[b]], writes=[("fB", c)])
            w4, k4 = wload(*unit_in(l, 4))
            for c in range(4):
                b = proj_F(w4, k4, c, N)
                P.act((lambda b, c: lambda e: e.copy(out=fC[:, c, 0:N], in_=ps[b][:, 0:N]))(b, c), reads=[PSK[b]], writes=[("fC", c)])
            w5, k5 = wload(*unit_in(l, 5))
            for c in range(4):
                b = proj_F(w5, k5, c, N)
                P.dve((lambda b, c: lambda e: e.tensor_tensor(out=zp[:, c, S0:S0 + N], in0=ps[b][:, 0:N], in1=fC[:, c, 0:N], op=ALU.mult))(b, c),
                      reads=[PSK[b], ("fC", c)], writes=[("zp", c)])
                P.dve((lambda c: lambda e: e.tensor_scalar(out=cacc[:, 0:N], in0=zp[:, c, 0:N], scalar1=cw_col[:, c, 0:1], scalar2=None, op0=ALU.mult))(c),
                      reads=[("zp", c), "cw_col"], writes=["cacc"])
                P.dve((lambda c: lambda e: e.scalar_tensor_tensor(out=cacc[:, 0:N], in0=zp[:, c, sh:sh + N], scalar=cw_col[:, c, 1:2], in1=cacc[:, 0:N],
                                                                   op0=ALU.mult, op1=ALU.add))(c), reads=[("zp", c), "cw_col", "cacc"], writes=["cacc"])
                P.dve((lambda c: lambda e: e.scalar_tensor_tensor(out=cacc[:, 0:N], in0=zp[:, c, 2 * sh:2 * sh + N], scalar=cw_col[:, c, 2:3], in1=cacc[:, 0:N],
                                                                   op0=ALU.mult, op1=ALU.add))(c), reads=[("zp", c), "cw_col", "cacc"], writes=["cacc"])
                P.dve((lambda c: lambda e: e.tensor_tensor(out=fB[:, c, 0:N], in0=cacc[:, 0:N], in1=fB[:, c, 0:N], op=ALU.mult))(c),
                      reads=["cacc", ("fB", c)], writes=[("fB", c)])
            if SUB < 6:
                return
            last = is_sample or (g == n_groups - 1)
            if last:
                ncol = 32 if is_sample else 2
                for c in range(4):
                    P.pe((lambda c: lambda e: e.transpose(out=ps[7][:ncol, c * 128:(c + 1) * 128], in_=zp[:, c, S0 + N - ncol:S0 + N], identity=ident[:]))(c),
                         reads=[("zp", c), "ident"], writes=[PSK[7]])
                P.act(lambda e: e.copy(out=ncst[:ncol, :], in_=ps[7][:ncol, :]), reads=[PSK[7]], writes=["ncst"])
                if is_sample:
                    for r in range(2):
                        P.dma(XQ, "o_nc", (lambda r: lambda e: e.dma_start(out=ncs[l, :, r, :], in_=ncst[r * 16:(r + 1) * 16, :]))(r), reads=["ncst"])
                else:
                    P.dma(XQ, "o_nc", lambda e: e.dma_start(out=ncp[l], in_=ncst[0:2, :]), reads=["ncst"])
            if not is_sample:
                for c in range(4):
                    P.act((lambda c: lambda e: e.copy(out=zp[:, c, 0:2], in_=zp[:, c, N:N + 2]))(c), reads=[("zp", c)], writes=[("zp", c)])
            w6, k6 = wload(*unit_in(l, 6))
            for c in range(4):
                b = proj_F(w6, k6, c, N)
                silu_mul(b, N, fB[:, c, 0:N], ("fB", c), fB[:, c, 0:N], ("fB", c))
            P.pool(lambda e: e.tensor_copy(out=small[:, 0:1], in_=small[:, 0:1]), reads=[("fB", c) for c in range(4)], writes=["fB"])
            branch_proj(l, 1, fB, "fB", N)

        def dense_back(l, N, TT, tp, xsrc, dst_rows, dslot, xkeys=(), okey="xout", hook=None):
            wo0, ko0 = wload(*unit_out(l, 0))
            wo1, ko1 = wload(*unit_out(l, 1))
            ykeys = [("yacc", i) for i in range(8)]
            xh = [(vn_f, "vn_f"), (cacc, "cacc")]
            if hook is not None:
                hook(-1)
            for t in range(TT):
                bs = []
                for hf, (wo, ko) in enumerate(((wo0, ko0), (wo1, ko1))):
                    b = 2 + hf
                    for k in range(8):
                        P.pe((lambda k, b, wo: lambda e: e.matmul(ps[b][:tp, :], lhsT=yacc[:, k, t * tp:(t + 1) * tp], rhs=wo[:, k, :],
                                                                  start=(k == 0), stop=(k == 7)))(k, b, wo),
                             reads=[ko] + ykeys, writes=[PSK[b]])
                    P.act((lambda b, hf: lambda e: e.activation(out=junk[:tp, 0:512], in_=ps[b][:tp, :], func=AF.Square,
                                                                accum_out=st[:tp, 5 + hf:6 + hf]))(b, hf),
                          reads=[PSK[b]], writes=["junk", "st56"])
                    bs.append(b)
                P.dve(lambda e: e.tensor_tensor(out=st[:tp, 5:6], in0=st[:tp, 5:6], in1=st[:tp, 6:7], op=ALU.add), reads=["st56"], writes=["st56"])
                rstd_from_sumsq(5, 7, tp, D, ["st56"], ["st7"])
                for hf in range(2):
                    xb, xbk = xh[hf]
                    P.dma(XQ, "xin%d" % (2 + hf), (lambda t, hf, xb: lambda e: e.dma_start(out=xb[:tp, :], in_=xsrc(t)[:, hf * 512:(hf + 1) * 512]))(t, hf, xb),
                          reads=list(xkeys), writes=[xbk])
                for hf in range(2):
                    b = bs[hf]
                    xb, xbk = xh[hf]
                    P.dve((lambda b, hf: lambda e: e.scalar_tensor_tensor(out=tmpf[:tp, :], in0=ps[b][:tp, :], scalar=st[:tp, 7:8],
                                                                          in1=gpost_bc[:tp, hf * 512:(hf + 1) * 512], op0=ALU.mult, op1=ALU.mult))(b, hf),
                          reads=[PSK[b], "st7", "gpost_bc"], writes=["tmpf"])
                    P.pool((lambda xb: lambda e: e.tensor_tensor(out=xb[:tp, :], in0=xb[:tp, :], in1=tmpf[:tp, :], op=ALU.add))(xb), reads=["tmpf", xbk], writes=[xbk])
                    P.dma(XQ, dslot, (lambda t, hf, xb: lambda e: e.dma_start(out=dst_rows(t)[:, hf * 512:(hf + 1) * 512], in_=xb[:tp, :]))(t, hf, xb),
                          reads=[xbk], writes=[(okey, l)])
                if hook is not None:
                    hook(t)

        def mem_kv_prompt(l):
            P.dma("pool", "lc0", lambda e: e.dma_start(out=gpost_bc[:], in_=g_mem[l:l + 1, :].partition_broadcast(128)), writes=["gpost_bc"])
            for t in range(2):
                norm_rows_to_T(memp[t * 128:(t + 1) * 128, :], 128, gpost_bc, "gpost_bc", memT, "fA", t * 128)
            wk, kk = wload(*unit_mem(l, 0))
            wv, kv = wload(*unit_mem(l, 1))
            for t in range(2):
                b = proj_T(wk, kk, t, 128, src=memT, srckey="fA")
                P.act((lambda b: lambda e: e.copy(out=kst[:, :], in_=ps[b][:, :]))(b), reads=[PSK[b]], writes=["kst"])
                P.dma(XQ, "o_mk", (lambda t: lambda e: e.dma_start(out=nmk[l, t * 128:(t + 1) * 128, :], in_=kst[:, :]))(t), reads=["kst"])
                b = proj_T(wv, kv, t, 128, src=memT, srckey="fA")
                P.act((lambda b: lambda e: e.copy(out=vst[:, :], in_=ps[b][:, :]))(b), reads=[PSK[b]], writes=["vst"])
                P.dve((lambda t: lambda e: e.tensor_copy(out=mv_b[:, t, :], in_=vst[:, :]))(t), reads=["vst"], writes=["mv_b"])
                P.dma(XQ, "o_mv", (lambda t: lambda e: e.dma_start(out=nmv[l, t * 128:(t + 1) * 128, :], in_=vst[:, :]))(t), reads=["vst"])
            for h4 in range(4):
                b = proj_F(wk, kk, h4, 256, src=memT, srckey="fA")
                P.act((lambda b, h4: lambda e: e.copy(out=mkT[:, h4, :], in_=ps[b][:, 0:256]))(b, h4), reads=[PSK[b]], writes=["mkT"])

        def prompt_group(l, g):
            N, TT, tp = 512, 4, 128
            src = xp if l == 0 else x1p
            dst = x1p if l < depth - 1 else yp

            def xsrc(t):
                return src[(g * 4 + t) * 128:(g * 4 + t + 1) * 128, :]

            def xdst(t):
                return dst[(g * 4 + t) * 128:(g * 4 + t + 1) * 128, :]

            if STAGE < 1:
                return
            xkeys = [("xout", l - 1)] if l > 0 else []
            dense_front(l, N, TT, tp, xsrc, False, g, xkeys, skip_norm=(PREF and g > 0))
            if STAGE < 2:
                return
            w7, k7 = wload(*unit_in(l, 7))
            for t in range(TT):
                b = proj_T(w7, k7, t, tp)
                rope(b, tp, rope_p, g * 4 + t, qa[:, t, :, :], 0.125, [("qa", t)])
            if KC < 1:
                return
            w8, k8 = wload(*unit_in(l, 8))
            for t in range(TT):
                jt = g * 4 + t
                b = proj_T(w8, k8, t, tp)
                rope(b, tp, rope_p, jt, kst[:, :].rearrange("p (h d) -> p h d", h=8), 1.0, ["kst"])
                P.dma(XQ, "o_k", (lambda jt: lambda e: e.dma_start(out=nkp[l, jt * 128:(jt + 1) * 128, :], in_=kst[:, :]))(jt), reads=["kst"])
                if KC < 2:
                    continue
                P.dve((lambda t: lambda e: e.tensor_tensor(out=tmpf[:, :].rearrange("p (h d) -> p h d", h=8), in0=qa[:, t, :, 0:64],
                                                          in1=kst[:, :].rearrange("p (h d) -> p h d", h=8), op=ALU.mult))(t),
                      reads=[("qa", t), "kst"], writes=["tmpf"])
                P.dve((lambda t: lambda e: e.tensor_reduce(out=lqq[:, t, :], in_=tmpf[:, :].rearrange("p (h d) -> p h d", h=8), axis=AX.X, op=ALU.add))(t),
                      reads=["tmpf"], writes=["lqq"])
                if KC < 3:
                    continue
                for half in range(2):
                    pb = 6 + half
                    for i in range(4):
                        h = half * 4 + i
                        P.pe((lambda h, i, pb: lambda e: e.transpose(out=ps[pb][0:64, i * 128:(i + 1) * 128], in_=kst[:, h * 64:(h + 1) * 64], identity=ident[:]))(h, i, pb),
                             reads=["kst", "ident"], writes=[PSK[pb]])
                    srcp = ps[pb][0:64, :].rearrange("p (a b) -> p a b", a=4)
                    if half == 0:
                        P.act((lambda srcp: lambda e: e.copy(out=ktst[:, 0:4, :], in_=srcp))(srcp), reads=[PSK[pb]], writes=["ktst"])
                    else:
                        P.dve((lambda srcp: lambda e: e.tensor_copy(out=ktst[:, 4:8, :], in_=srcp))(srcp), reads=[PSK[pb]], writes=["ktst"])
                P.dma(XQ, "kt_w", (lambda jt: lambda e: e.dma_start(out=KT_d[:, 0:64, jt * 128:(jt + 1) * 128].rearrange("h p k -> p h k"), in_=ktst[:, :, :]))(jt),
                      reads=["ktst", "KTind"], writes=[("KT_d", g)])
                if KC < 4:
                    continue
                for c in range(4):
                    P.pe((lambda c: lambda e: e.matmul(ps[5][:, c:c + 1], lhsT=kst[:, c * 128:(c + 1) * 128], rhs=ones_f[:, 0:1],
                                                      start=True, stop=True))(c),
                         reads=["kst", "ones_f"], writes=[PSK[5]])
                if jt % 2 == 0:
                    P.dve(lambda e: e.tensor_scalar(out=ksum[:, 0:4], in0=ps[5][:, 0:4], scalar1=1.0 / 256, scalar2=None, op0=ALU.mult),
                          reads=[PSK[5]], writes=["ksum"])
                else:
                    nb = jt // 2
                    KMf = KM[:].rearrange("p c n -> p (c n)")
                    for c in range(4):
                        P.dve((lambda c, nb: lambda e: e.scalar_tensor_tensor(out=KMf[0:64, c * 128 + (2 * c) * 16 + nb:c * 128 + (2 * c) * 16 + nb + 1], in0=ps[5][0:64, c:c + 1],
                                                                               scalar=1.0 / 256, in1=ksum[0:64, c:c + 1], op0=ALU.mult, op1=ALU.add))(c, nb),
                              reads=[PSK[5], "ksum"], writes=["KM"])
                        P.dve((lambda c, nb: lambda e: e.scalar_tensor_tensor(out=KMf[64:128, c * 128 + (2 * c + 1) * 16 + nb:c * 128 + (2 * c + 1) * 16 + nb + 1],
                                                                               in0=ps[5][64:128, c:c + 1], scalar=1.0 / 256, in1=ksum[64:128, c:c + 1], op0=ALU.mult, op1=ALU.add))(c, nb),
                              reads=[PSK[5], "ksum"], writes=["KM"])
            if KC < 5:
                return
            w9, k9 = wload(*unit_in(l, 9))
            for t in range(TT):
                jt = g * 4 + t
                b = proj_T(w9, k9, t, tp)
                P.act((lambda b: lambda e: e.copy(out=vst[:, :], in_=ps[b][:, :]))(b), reads=[PSK[b]], writes=["vst"])
                if KC < 6:
                    continue
                P.dma(XQ, "o_v", (lambda jt: lambda e: e.dma_start(out=nvp[l, jt * 128:(jt + 1) * 128, :], in_=vst[:, :]))(jt), reads=["vst"])
                P.dve((lambda b: lambda e: e.tensor_copy(out=vaug[:].rearrange("p c (s d) -> p c s d", s=3)[:, :, 0:3:2, :],
                                                        in_=ps[b][:, :].rearrange("p (c s d) -> p c s d", c=4, s=2)))(b), reads=[PSK[b]], writes=["vaug"])
                P.dma(XQ, "va_w", (lambda jt: lambda e: e.dma_start(out=VA_d[jt * 128:(jt + 1) * 128, :], in_=vaug[:].rearrange("p c f -> p (c f)")))(jt),
                      reads=["vaug", "VAones"], writes=[("VA_d", g)])
            if STAGE < 3:
                return
            for t in range(TT):
                jt = g * 4 + t
                own = jt // 2
                P.act((lambda t: lambda e: e.copy(out=vn_f[:, :].rearrange("p (h d) -> p h d", h=8), in_=qa[:, t, :, 0:64]))(t), reads=[("qa", t)], writes=["vn_f"])
                for c in range(4):
                    P.pe((lambda c, t: lambda e: e.transpose(out=ps[6][:, c * 128:(c + 1) * 128], in_=vn_f[:, c * 128:(c + 1) * 128], identity=ident[:]))(c, t),
                         reads=["vn_f", "ident"], writes=[PSK[6]])
                P.act(lambda e: e.copy(out=qT4[:].rearrange("p c q -> p (c q)"), in_=ps[6][:, :]), reads=[PSK[6]], writes=["qT4"])
                for c in range(4):
                    P.pe((lambda c: lambda e: e.matmul(ps[7][:, 0:128], lhsT=qT4[:, c, :], rhs=KM[:, c, :], start=(c == 0), stop=(c == 3)))(c),
                         reads=["qT4", "KM"], writes=[PSK[7]])
                P.dve(lambda e: e.tensor_copy(out=gm[:].rearrange("p h n -> p (h n)"), in_=ps[7][:, 0:128]), reads=[PSK[7]], writes=["gm"])
                if own < 16:
                    P.dve((lambda own: lambda e: e.memset(gm[:, :, own:16], -1e30))(own), reads=["gm"], writes=["gm"])
                for h in range(8):
                    P.dve((lambda h: lambda e: e.max(out=m8[:, h, :], in_=gm[:, h, :]))(h), reads=["gm"], writes=["m8"])
                P.dve(lambda e: e.tensor_tensor(out=selm[:], in0=gm[:], in1=m8[:, :, 2:3].to_broadcast([128, 8, 16]), op=ALU.is_ge), reads=["gm", "m8"], writes=["selm"])
                P.dve(lambda e: e.tensor_scalar(out=selm[:], in0=selm[:], scalar1=-1.0, scalar2=-NEG, op0=ALU.add, op1=ALU.mult), reads=["selm"], writes=["selm"])
                P.dve((lambda t: lambda e: e.tensor_tensor(out=qa[:, t, :, 64:80], in0=selm[:], in1=lqq[:, t, :].unsqueeze(2).to_broadcast([128, 8, 16]),
                                                          op=ALU.subtract))(t), reads=["selm", "lqq", ("qa", t)], writes=[("qa", t)])
                P.dve((lambda t, own: lambda e: e.tensor_scalar(out=qa[:, t, :, 64 + own:65 + own], in0=lqq[:, t, :].unsqueeze(2), scalar1=-1.0, scalar2=None,
                                                               op0=ALU.mult))(t, own), reads=["lqq", ("qa", t)], writes=[("qa", t)])
                for half in range(2):
                    pb = 6 + half
                    for i in range(4):
                        h = half * 4 + i
                        P.pe((lambda h, i, pb, t: lambda e: e.transpose(out=ps[pb][0:80, i * 128:(i + 1) * 128], in_=qa[:, t, h, :], identity=ident[:]))(h, i, pb, t),
                             reads=[("qa", t), "ident"], writes=[PSK[pb]])
                    srcp = ps[pb][0:80, :].rearrange("p (a b) -> p a b", a=4)
                    dstp = qTa[:, half * 4:(half + 1) * 4, t * 128:(t + 1) * 128]
                    if half == 0:
                        P.act((lambda srcp, dstp: lambda e: e.copy(out=dstp, in_=srcp))(srcp, dstp), reads=[PSK[pb]], writes=["qTa"])
                    else:
                        P.dve((lambda srcp, dstp: lambda e: e.tensor_copy(out=dstp, in_=srcp))(srcp, dstp), reads=[PSK[pb]], writes=["qTa"])
            if STAGE < 4:
                return
            njt = 4 * g + 4
            L = njt * 128
            hist_k = [("KT_d", gg) for gg in range(g + 1)] + ["KTind"]
            hist_v = [("VA_d", gg) for gg in range(g + 1)] + ["VAones"]
            for c in range(4):
                sl = 0
                tq = njt // 4
                for q in range(4):
                    j0, j1 = q * tq, (q + 1) * tq
                    P.dma("sp", "ktbq%d" % q, (lambda c, sl, j0, j1: lambda e: e.dma_start(
                        out=ktb[sl][0:80, :, j0 * 128:j1 * 128], in_=KT_d[2 * c:2 * c + 2, :, j0 * 128:j1 * 128].rearrange("h p k -> p h k")))(c, sl, j0, j1),
                        reads=hist_k, writes=[("ktbq", q)])
                    P.dma("sp", "vabq%d" % q, (lambda c, sl, j0, j1: lambda e: e.dma_start(
                        out=vab[sl][:, j0:j1, :], in_=VA_d[j0 * 128:j1 * 128, c * 192:(c + 1) * 192].rearrange("(j p) f -> p j f", p=128)))(c, sl, j0, j1),
                        reads=hist_v, writes=[("vabq", q)])
                for hh in range(2):
                    h = 2 * c + hh
                    ob = 4 + hh
                    def qk(j):
                        sb = 2 + (j % 2)
                        diag = j >= 4 * g
                        P.pe((lambda j, sb, sl, hh, h, diag: lambda e: e.matmul(ps[sb][:, :], lhsT=ktb[sl][0:80, hh, j * 128:(j + 1) * 128], rhs=qTa[0:80, h, :],
                                                                                start=True, stop=(not diag)))(j, sb, sl, hh, h, diag),
                             reads=[("ktbq", j // tq), "qTa"], writes=[PSK[sb]])
                        if diag:
                            P.pe((lambda j, sb: lambda e: e.matmul(ps[sb][:, :], lhsT=identb[:], rhs=trib[:, j - 4 * g, :], start=False, stop=True))(j, sb),
                                 reads=["identb", "trib"], writes=[PSK[sb]])

                    def ex_pv(j):
                        sb = 2 + (j % 2)
                        pi = j % 3
                        P.act((lambda sb, pi: lambda e: e.activation(out=pT[pi][:, :], in_=ps[sb][:, :], func=AF.Exp))(sb, pi),
                              reads=[PSK[sb]], writes=["pT%d" % pi])
                        P.pe((lambda j, pi, sl, hh, ob: lambda e: e.matmul(ps[ob][:, :], lhsT=vab[sl][:, j, hh * 64:hh * 64 + 128], rhs=pT[pi][:, :],
                                                                           start=(j == 0), stop=(j == njt - 1)))(j, pi, sl, hh, ob),
                             reads=[("vabq", j // tq), "pT%d" % pi], writes=[PSK[ob]])

                    qk(0)
                    for j in range(njt):
                        if j + 1 < njt:
                            qk(j + 1)
                        ex_pv(j)
                    if hh == 0:
                        P.dve((lambda ob: lambda e: e.reciprocal(out=rs[64:128, :], in_=ps[ob][64:128, :]))(ob), reads=[PSK[ob]], writes=["rs"])
                        P.dve((lambda ob, c: lambda e: e.tensor_tensor(out=oc[0:64, c, :], in0=ps[ob][0:64, :], in1=rs[64:128, :], op=ALU.mult))(ob, c),
                              reads=[PSK[ob], "rs"], writes=[("fB", c)])
                    else:
                        P.dve((lambda ob: lambda e: e.reciprocal(out=rs[0:64, :], in_=ps[ob][0:64, :]))(ob), reads=[PSK[ob]], writes=["rs"])
                        P.dve((lambda ob, c: lambda e: e.tensor_tensor(out=oc[64:128, c, :], in0=ps[ob][64:128, :], in1=rs[0:64, :], op=ALU.mult))(ob, c),
                              reads=[PSK[ob], "rs"], writes=[("fB", c)])
            if STAGE < 5:
                return
            w10, k10 = wload(*unit_in(l, 10))
            for c in range(4):
                b = proj_F(w10, k10, c, N)
                silu_mul(b, N, fA[:, c, 0:N], ("fA", c), oc[:, c, 0:N], ("fB", c))
            P.pool(lambda e: e.tensor_copy(out=small[:, 0:1], in_=small[:, 0:1]), reads=[("fA", c) for c in range(4)], writes=["fA"])
            branch_proj(l, 2, fA, "fA", N)
            if STAGE < 6:
                return
            w11, k11 = wload(*unit_in(l, 11))
            for c in range(4):
                b = proj_F(w11, k11, c, N)
                P.act((lambda b, c: lambda e: e.copy(out=fC[:, c, 0:N], in_=ps[b][:, 0:N]))(b, c), reads=[PSK[b]], writes=[("fC", c)])
            sc = 128.0 ** -0.5
            for h4 in range(4):
                ob = 4 + (h4 % 2)
                db = 6 + (h4 % 2)
                for mt in range(2):
                    P.pe((lambda mt, h4: lambda e: e.matmul(ps[2 + mt][:, :], lhsT=mkT[:, h4, mt * 128:(mt + 1) * 128], rhs=fC[:, h4, 0:512], start=True, stop=True))(mt, h4),
                         reads=[("fC", h4), "mkT"], writes=[PSK[2 + mt]])
                for mt in range(2):
                    P.act((lambda mt: lambda e: e.activation(out=pT[mt][:, :], in_=ps[2 + mt][:, :], func=AF.Exp, scale=sc))(mt),
                          reads=[PSK[2 + mt]], writes=["pT%d" % mt])
                    P.pe((lambda mt, h4, ob: lambda e: e.matmul(ps[ob][:, :], lhsT=mv_b[:, mt, h4 * 128:(h4 + 1) * 128], rhs=pT[mt][:, :], start=(mt == 0), stop=(mt == 1)))(mt, h4, ob),
                         reads=["mv_b", "pT%d" % mt], writes=[PSK[ob]])
                    P.pe((lambda mt, db: lambda e: e.matmul(ps[db][:, :], lhsT=ones_b[:, :], rhs=pT[mt][:, :], start=(mt == 0), stop=(mt == 1)))(mt, db),
                         reads=["ones_b", "pT%d" % mt], writes=[PSK[db]])
                P.dve((lambda db: lambda e: e.reciprocal(out=rs[:, :], in_=ps[db][:, :]))(db), reads=[PSK[db]], writes=["rs"])
                P.dve((lambda ob, h4: lambda e: e.tensor_tensor(out=oc[:, h4, :], in0=ps[ob][:, :], in1=rs[:, :], op=ALU.mult))(ob, h4),
                      reads=[PSK[ob], "rs"], writes=[("fB", h4)])
            w12, k12 = wload(*unit_in(l, 12))
            for c in range(4):
                b = proj_F(w12, k12, c, N)
                silu_mul(b, N, fA[:, c, 0:N], ("fA", c), oc[:, c, 0:N], ("fB", c))
            P.pool(lambda e: e.tensor_copy(out=small[:, 0:1], in_=small[:, 0:1]), reads=[("fA", c) for c in range(4)], writes=["fA"])
            branch_proj(l, 3, fA, "fA", N)
            hook = None
            if PREF and g + 1 < n_groups:
                def xnext(t):
                    return src[((g + 1) * 4 + t) * 128:((g + 1) * 4 + t + 1) * 128, :]

                def hook(t):
                    if t >= 0:
                        norm_post(tp, hT, "hT", t * tp)
                    if t + 1 < TT:
                        norm_pre(xnext(t + 1), tp, gpre_bc, "gpre_bc", xkeys)
            dense_back(l, N, TT, tp, xsrc, xdst, "o_x", xkeys, hook=hook)

        def sample_setup():
            P.dma("pool", "lc0", lambda e: e.dma_start(out=pt_i[:], in_=ptab[0:1, :].partition_broadcast(128)), writes=["pt_i"])
            P.dve(lambda e: e.tensor_copy(out=pt_f[:], in_=pt_i[:]), reads=["pt_i"], writes=["pt_f"])
            P.dve(lambda e: e.tensor_scalar(out=pt_f[:], in0=pt_f[:], scalar1=float(PAGE), scalar2=iota[:, 0:1], op0=ALU.mult, op1=ALU.add),
                  reads=["pt_f", "iota"], writes=["pt_f"])
            P.dve(lambda e: e.tensor_copy(out=idx_i[:], in_=pt_f[:]), reads=["pt_f"], writes=["idx_i"])
            if depth > 1:
                P.dve(lambda e: e.tensor_scalar(out=pt_f[:], in0=pt_f[:], scalar1=float(NPOOL * PAGE), scalar2=None, op0=ALU.add), reads=["pt_f"], writes=["pt_f"])
                P.dve(lambda e: e.tensor_copy(out=pt_i[:], in_=pt_f[:]), reads=["pt_f"], writes=["idx_i"])

        def ocs_to_oc():
            for c in range(4):
                P.pe((lambda c: lambda e: e.transpose(out=ps[6][:, c * 64:(c + 1) * 64], in_=ocs[:64, c * 128:(c + 1) * 128], identity=ident[:64, :64]))(c),
                     reads=["ocs", "ident"], writes=[PSK[6]])
            P.act(lambda e: e.copy(out=oc[:, :, 0:64], in_=ps[6][:, 0:256].rearrange("p (c q) -> p c q", c=4)), reads=[PSK[6]],
                  writes=[("fB", c) for c in range(4)])

        def scatter_acc(R, selc, bl, first):
            P.pe(lambda e: e.matmul(ps[4][:64, :], lhsT=selc[:R, bl, :], rhs=On[:R, :], start=True, stop=True),
                 reads=["ncst", "rsel", "rselm"], writes=[PSK[4]])
            if first:
                P.dve(lambda e: e.tensor_copy(out=ocs[:64, :], in_=ps[4][:64, :]), reads=[PSK[4]], writes=["ocs"])
            else:
                P.dve(lambda e: e.tensor_tensor(out=ocs[:64, :], in0=ocs[:64, :], in1=ps[4][:64, :], op=ALU.add), reads=[PSK[4], "ocs"], writes=["ocs"])

        def sample_group(l):
            N, TT, tp = 64, 1, 64
            src = xs if l == 0 else x1s
            dst = x1s if l < depth - 1 else ys
            xkeys = [("xsout", l - 1)] if l > 0 else []

            def xsrc(t):
                return src[0:64, :]

            def xdst(t):
                return dst[0:64, :]

            for c in range(4):
                for r in range(2):
                    P.dma("pool", "lc%d" % c, (lambda c, r: lambda e: e.dma_start(out=zp[:, c, r * 16:(r + 1) * 16],
                                                                                  in_=sconv[l, :, r, c * 128:(c + 1) * 128].rearrange("b p -> p b")))(c, r),
                          writes=[("zp", c)])
            dense_front(l, N, TT, tp, xsrc, True, 0, xkeys)
            w7, k7 = wload(*unit_in(l, 7))
            b = proj_T(w7, k7, 0, tp)
            rope(b, tp, rope_s, None, qa[:64, 0, :, :], 0.125, [("qa", 0)])
            P.act(lambda e: e.copy(out=vn_f[:64, :].rearrange("p (h d) -> p h d", h=8), in_=qa[:64, 0, :, 0:64]), reads=[("qa", 0)], writes=["vn_f"])
            for c in range(4):
                P.pe((lambda c: lambda e: e.transpose(out=ps[6][:, c * 64:(c + 1) * 64], in_=vn_f[:64, c * 128:(c + 1) * 128], identity=ident[:64, :64]))(c),
                     reads=["vn_f", "ident"], writes=[PSK[6]])
            P.act(lambda e: e.copy(out=qTs[:].rearrange("p c q -> p (c q)"), in_=ps[6][:, 0:256]), reads=[PSK[6]], writes=["qTs"])
            w8, k8 = wload(*unit_in(l, 8))
            b = proj_T(w8, k8, 0, tp)
            rope(b, tp, rope_s, None, kst[:64, :].rearrange("p (h d) -> p h d", h=8), 1.0, ["kst"])
            P.dma(XQ, "o_k", lambda e: e.dma_start(out=nks[l], in_=kst[:64, :]), reads=["kst"])
            for c in range(4):
                P.pe((lambda c: lambda e: e.transpose(out=ps[7][:, c * 64:(c + 1) * 64], in_=kst[:64, c * 128:(c + 1) * 128], identity=ident[:64, :64]))(c),
                     reads=["kst", "ident"], writes=[PSK[7]])
            P.act(lambda e: e.copy(out=kTn[:].rearrange("p c q -> p (c q)"), in_=ps[7][:, 0:256]), reads=[PSK[7]], writes=["kTn"])
            w9, k9 = wload(*unit_in(l, 9))
            b = proj_T(w9, k9, 0, tp)
            P.act((lambda b: lambda e: e.copy(out=vst[:64, :], in_=ps[b][:64, :]))(b), reads=[PSK[b]], writes=["vst"])
            P.dma(XQ, "o_v", lambda e: e.dma_start(out=nvs[l], in_=vst[:64, :]), reads=["vst"])
            qTs4 = qTs[:].rearrange("p c (t b) -> p c t b", b=16)
            gcnt = [0, 0]
            assert l < 2
            idx_l = idx_i if l == 0 else pt_i
            kv_i = [0, 0]
            ck_flat = cache_k.rearrange("l n w -> (l n) w")
            cv_flat = cache_v.rearrange("l n w -> (l n) w")

            def issue_k(upto):
                while kv_i[0] < min(upto, NSB * NPAGES):
                    n = kv_i[0]
                    kv_i[0] += 1
                    ap, key = kslots[n % 3]
                    P.dma("pool", "gk%d" % (n % 3), (lambda ap, n: lambda e: e.indirect_dma_start(
                        out=ap, out_offset=None, in_=ck_flat, in_offset=bass.IndirectOffsetOnAxis(ap=idx_l[:, n:n + 1], axis=0)))(ap, n),
                        reads=["idx_i"], writes=[key])

            def issue_v(upto):
                while kv_i[1] < min(upto, NSB * NPAGES):
                    n = kv_i[1]
                    kv_i[1] += 1
                    ap, key = vslots[n % 4]
                    P.dma("pool", "gv%d" % (n % 4), (lambda ap, n: lambda e: e.indirect_dma_start(
                        out=ap, out_offset=None, in_=cv_flat, in_offset=bass.IndirectOffsetOnAxis(ap=idx_l[:, n:n + 1], axis=0)))(ap, n),
                        reads=["idx_i"], writes=[key])
            for bl in range(NSB):
                qsl = qTs4[:, :, :, bl:bl + 1].rearrange("p c t o -> p c (t o)").unsqueeze(2).to_broadcast([128, 4, 8, 4])
                P.dve((lambda qsl: lambda e: e.tensor_tensor(out=Qblk_f[:].rearrange("p c (h t) -> p c h t", h=8), in0=qsl,
                                                            in1=qmask[:].unsqueeze(3).to_broadcast([128, 4, 8, 4]), op=ALU.mult))(qsl),
                      reads=["qTs", "qmask"], writes=["Qblk_f"])
                P.dve(lambda e: e.tensor_copy(out=Qblk[:], in_=Qblk_f[:]), reads=["Qblk_f"], writes=["Qblk"])
                if bl == 0:
                    issue_k(3)
                    issue_v(4)
                for j in range(NPAGES):
                    n = bl * NPAGES + j
                    ksl, kkey = kslots[n % 3]
                    kb, kbkey = kb16[n % 2]
                    if n % 2 == 0:
                        P.act((lambda kb, ksl: lambda e: e.copy(out=kb, in_=ksl))(kb, ksl), reads=[kkey], writes=[kbkey])
                    else:
                        P.dve((lambda kb, ksl: lambda e: e.tensor_copy(out=kb, in_=ksl))(kb, ksl), reads=[kkey], writes=[kbkey])
                    issue_k(n + 4)
                    P.pe((lambda kb, j: lambda e: e.matmul(ps[5][:8, :], lhsT=blkind_b[:, j, :], rhs=kb, start=(j == 0), stop=(j == NPAGES - 1)))(kb, j),
                         reads=[kbkey, "blkind_b"], writes=[PSK[5]])
                    pb = 6 + (j % 2)
                    for c in range(4):
                        P.pe((lambda kb, c, pb: lambda e: e.matmul(ps[pb][:, c * 128:(c + 1) * 128], lhsT=kb[:, c * 128:(c + 1) * 128], rhs=identb[:, :], start=True, stop=True))(kb, c, pb),
                             reads=[kbkey, "identb"], writes=[PSK[pb]])
                    srcp = ps[pb][:, :].rearrange("p (c k) -> p c k", c=4)
                    dstp = KTs[:, :, j * 128:(j + 1) * 128]
                    if j % 2 == 1:
                        P.act((lambda srcp, dstp: lambda e: e.copy(out=dstp, in_=srcp))(srcp, dstp), reads=[PSK[pb]], writes=KT_ALL)
                    else:
                        P.dve((lambda srcp, dstp: lambda e: e.tensor_copy(out=dstp, in_=srcp))(srcp, dstp), reads=[PSK[pb]], writes=KT_ALL)
                P.act(lambda e: e.copy(out=tmpf[:8, :], in_=ps[5][:8, :]), reads=[PSK[5]], writes=["tmpf"])
                for c in range(4):
                    P.pe((lambda c: lambda e: e.transpose(out=ps[5][:, c * 8:(c + 1) * 8], in_=tmpf[:8, c * 128:(c + 1) * 128], identity=ident[:8, :8]))(c),
                         reads=["tmpf", "ident"], writes=[PSK[5]])
                P.act(lambda e: e.copy(out=kmT[:].rearrange("p c n -> p (c n)"), in_=ps[5][:, 0:32]), reads=[PSK[5]], writes=["kmT"])
                for c in range(4):
                    P.pe((lambda c: lambda e: e.matmul(ps[4][:32, 0:8], lhsT=Qblk_f[:, c, :], rhs=kmT[:, c, :], start=(c == 0), stop=(c == 3)))(c),
                         reads=["Qblk_f", "kmT"], writes=[PSK[4]])
                P.dve(lambda e: e.tensor_copy(out=gms[:, :], in_=ps[4][:32, 0:8]), reads=[PSK[4]], writes=["gms"])
                P.dve(lambda e: e.max(out=m8s[:, :], in_=gms[:, :]), reads=["gms"], writes=["m8s"])
                P.dve(lambda e: e.tensor_tensor(out=selb[:, :], in0=gms[:, :], in1=m8s[:, 2:3].to_broadcast([32, 8]), op=ALU.is_ge), reads=["gms", "m8s"], writes=["selb"])
                P.dve(lambda e: e.tensor_scalar(out=selb[:, :], in0=selb[:, :], scalar1=-1.0, scalar2=-NEG, op0=ALU.add, op1=ALU.mult), reads=["selb"], writes=["selb"])
                for qd in range(4):
                    sb = 2 + (qd % 2)
                    for c in range(4):
                        P.pe((lambda c, sb, qd: lambda e: e.matmul(ps[sb][:32, :], lhsT=Qblk[:, c, :], rhs=KTs[:, c, qd * 512:(qd + 1) * 512], start=(c == 0), stop=(c == 3)))(c, sb, qd),
                             reads=["Qblk"] + KT_ALL, writes=[PSK[sb]])
                    pmt = Pm[qd % 2]
                    pkey = "rs" if qd % 2 == 0 else "cacc"
                    for hf in range(2):
                        n = qd * 2 + hf
                        P.act((lambda sb, hf, n, pmt: lambda e: e.activation(out=pmt[:32, hf * 256:(hf + 1) * 256], in_=ps[sb][:32, hf * 256:(hf + 1) * 256], func=AF.Exp,
                                                                             bias=selb[:, n:n + 1], accum_out=den[:, n:n + 1]))(sb, hf, n, pmt),
                              reads=[PSK[sb], "selb"], writes=[pkey, "den"])
                    for jj in range(4):
                        j = qd * 4 + jj
                        P.pe((lambda jj, j, pmt: lambda e: e.transpose(out=ps[0][:, j * 32:(j + 1) * 32], in_=pmt[:32, jj * 128:(jj + 1) * 128], identity=ident[:32, :32]))(jj, j, pmt),
                             reads=[pkey, "ident"], writes=[PSK[0]])
                P.dve(lambda e: e.tensor_copy(out=fT[:, :], in_=ps[0][:, :]), reads=[PSK[0]], writes=["fT"])
                for c in range(4):
                    P.pe((lambda c: lambda e: e.matmul(ps[2][:32, 0:64], lhsT=Qblk[:, c, :], rhs=kTn[:, c, :], start=(c == 0), stop=(c == 3)))(c),
                         reads=["Qblk", "kTn"], writes=[PSK[2]])
                P.dve((lambda bl: lambda e: e.tensor_tensor(out=Po[:, :], in0=ps[2][:32, 0:64], in1=ownb[:32, bl, :], op=ALU.add))(bl), reads=[PSK[2], "ownb"], writes=["Po"])
                P.act(lambda e: e.activation(out=Po[:, :], in_=Po[:, :], func=AF.Exp, accum_out=den[:, 8:9]), reads=["Po"], writes=["Po", "den"])
                P.pe(lambda e: e.transpose(out=ps[3][:64, 0:32], in_=Po[:32, :], identity=ident[:32, :32]), reads=["Po", "ident"], writes=[PSK[3]])
                P.act(lambda e: e.copy(out=PTo[:, :], in_=ps[3][:64, 0:32]), reads=[PSK[3]], writes=["PTo"])
                for j in range(NPAGES):
                    n = bl * NPAGES + j
                    vsl, vkey = vslots[n % 4]
                    vb, vbkey = vb16[n % 2]
                    if n % 2 == 1:
                        P.act((lambda vb, vsl: lambda e: e.copy(out=vb, in_=vsl))(vb, vsl), reads=[vkey], writes=[vbkey])
                    else:
                        P.dve((lambda vb, vsl: lambda e: e.tensor_copy(out=vb, in_=vsl))(vb, vsl), reads=[vkey], writes=[vbkey])
                    issue_v(n + 5)
                    P.pe((lambda vb, j: lambda e: e.matmul(ps[1][:32, :], lhsT=PTb[:, j, :], rhs=vb, start=(j == 0), stop=False))(vb, j),
                         reads=[vbkey, "fT"], writes=[PSK[1]])
                P.pe(lambda e: e.matmul(ps[1][:32, :], lhsT=PTo[:64, :], rhs=vst[:64, :], start=False, stop=True), reads=["PTo", "vst"], writes=[PSK[1]])
                P.dve(lambda e: e.tensor_reduce(out=den[:, 9:10], in_=den[:, 0:9], axis=AX.X, op=ALU.add), reads=["den"], writes=["den"])
                P.dve(lambda e: e.reciprocal(out=den[:, 10:11], in_=den[:, 9:10]), reads=["den"], writes=["den"])
                P.dve(lambda e: e.scalar_tensor_tensor(out=On[:32, :], in0=ps[1][:32, :], scalar=den[:, 10:11], in1=mdiag[:32, :], op0=ALU.mult, op1=ALU.mult),
                      reads=[PSK[1], "den", "mdiag"], writes=["ncst"])
                scatter_acc(32, rsel, bl, bl == 0)
            ocs_to_oc()
            w10, k10 = wload(*unit_in(l, 10))
            for c in range(4):
                b = proj_F(w10, k10, c, N)
                silu_mul(b, N, fA[:, c, 0:N], ("fA", c), oc[:, c, 0:N], ("fB", c))
            P.pool(lambda e: e.tensor_copy(out=small[:, 0:1], in_=small[:, 0:1]), reads=[("fA", c) for c in range(4)], writes=["fA"])
            branch_proj(l, 2, fA, "fA", N)
            w11, k11 = wload(*unit_in(l, 11))
            for c in range(4):
                b = proj_F(w11, k11, c, N)
                P.act((lambda b, c: lambda e: e.copy(out=fC[:, c, 0:N], in_=ps[b][:, 0:N]))(b, c), reads=[PSK[b]], writes=[("fC", c)])
            P.pool(lambda e: e.tensor_copy(out=small[:, 0:1], in_=small[:, 0:1]), reads=[("fC", c) for c in range(4)], writes=["fC"])
            sc = 128.0 ** -0.5
            fC4 = fC[:, :, 0:64].rearrange("p c (t b) -> p c t b", b=16)
            Qm = Qblk[:, :, 0:16]
            for bl in range(NSB):
                qsl = fC4[:, :, :, bl:bl + 1].rearrange("p c t o -> p c (t o)").unsqueeze(2).to_broadcast([128, 4, 4, 4])
                P.dve((lambda qsl: lambda e: e.tensor_tensor(out=Qm.rearrange("p c (h t) -> p c h t", h=4), in0=qsl,
                                                            in1=mmask[:].unsqueeze(3).to_broadcast([128, 4, 4, 4]), op=ALU.mult))(qsl),
                      reads=["fC", "mmask"], writes=["Qblk"])
                for mt in range(2):
                    P.dma("sp", "mk%d" % mt, (lambda mt, bl: lambda e: e.dma_start(out=kring[:, mt, :], in_=cmk[l, bl, mt * 128:(mt + 1) * 128, :]))(mt, bl), writes=["kring%d" % mt])
                    P.dma("sp", "mv%d" % mt, (lambda mt, bl: lambda e: e.dma_start(out=vring[:, mt, :], in_=cmv[l, bl, mt * 128:(mt + 1) * 128, :]))(mt, bl), writes=["vring%d" % mt])
                    pb = 6 + mt
                    for c in range(4):
                        P.pe((lambda mt, c, pb: lambda e: e.transpose(out=ps[pb][:, c * 128:(c + 1) * 128], in_=kring[:, mt, c * 128:(c + 1) * 128], identity=ident[:]))(mt, c, pb),
                             reads=["kring%d" % mt, "ident"], writes=[PSK[pb]])
                    srcp = ps[pb][:, :].rearrange("p (c k) -> p c k", c=4)
                    dstp = mkT[:, :, mt * 128:(mt + 1) * 128]
                    P.act((lambda srcp, dstp: lambda e: e.copy(out=dstp, in_=srcp))(srcp, dstp), reads=[PSK[pb]], writes=["mkT"])
                for c in range(4):
                    P.pe((lambda c: lambda e: e.matmul(ps[2][:16, 0:256], lhsT=Qm[:, c, :], rhs=mkT[:, c, :], start=(c == 0), stop=(c == 3)))(c),
                         reads=["Qblk", "mkT"], writes=[PSK[2]])
                P.dve(lambda e: e.tensor_reduce(out=den[:16, 11:12], in_=ps[2][:16, 0:256], axis=AX.X, op=ALU.max), reads=[PSK[2]], writes=["den"])
                P.dve(lambda e: e.tensor_scalar(out=den[:16, 12:13], in0=den[:16, 11:12], scalar1=-sc, scalar2=None, op0=ALU.mult), reads=["den"], writes=["den"])
                P.act(lambda e: e.activation(out=rs[:16, 0:256], in_=ps[2][:16, 0:256], func=AF.Exp, scale=sc, bias=den[:16, 12:13], accum_out=den[:16, 13:14]),
                      reads=[PSK[2], "den"], writes=["rs", "den"])
                P.dve(lambda e: e.reciprocal(out=den[:16, 14:15], in_=den[:16, 13:14]), reads=["den"], writes=["den"])
                for mt in range(2):
                    P.pe((lambda mt: lambda e: e.transpose(out=ps[0][:, mt * 16:(mt + 1) * 16], in_=rs[:16, mt * 128:(mt + 1) * 128], identity=ident[:16, :16]))(mt),
                         reads=["rs", "ident"], writes=[PSK[0]])
                P.dve(lambda e: e.tensor_copy(out=vn_f[:, 0:32], in_=ps[0][:, 0:32]), reads=[PSK[0]], writes=["vn_f"])
                for mt in range(2):
                    P.pe((lambda mt: lambda e: e.matmul(ps[1][:16, :], lhsT=vn_f[:, mt * 16:(mt + 1) * 16], rhs=vring[:, mt, :], start=(mt == 0), stop=(mt == 1)))(mt),
                         reads=["vn_f", "vring%d" % mt], writes=[PSK[1]])
                P.dve(lambda e: e.scalar_tensor_tensor(out=On[:16, :], in0=ps[1][:16, :], scalar=den[:16, 14:15], in1=mdiagm[:16, :], op0=ALU.mult, op1=ALU.mult),
                      reads=[PSK[1], "den", "mdiagm"], writes=["ncst"])
                scatter_acc(16, rselm, bl, bl == 0)
            ocs_to_oc()
            w12, k12 = wload(*unit_in(l, 12))
            for c in range(4):
                b = proj_F(w12, k12, c, N)
                silu_mul(b, N, fA[:, c, 0:N], ("fA", c), oc[:, c, 0:N], ("fB", c))
            P.pool(lambda e: e.tensor_copy(out=small[:, 0:1], in_=small[:, 0:1]), reads=[("fA", c) for c in range(4)], writes=["fA"])
            branch_proj(l, 3, fA, "fA", N)
            dense_back(l, N, TT, tp, xsrc, xdst, "o_xs", xkeys, okey="xsout")

        if do_sample:
            sample_setup()
        for l in range(depth):
            layer_consts(l)
            if do_prompt:
                for c in range(4):
                    P.dve((lambda c: lambda e: e.memset(zp[:, c, 0:32], 0.0))(c), writes=[("zp", c)])
                P.dve(lambda e: e.memset(KM[:], 0.0), writes=["KM"])
                P.dve(lambda e: e.memset(vaug[:], 1.0), writes=["vaug"])
                mem_kv_prompt(l)
                P.dma("pool", "lc1", lambda e: e.dma_start(out=gpost_bc[:], in_=g_post[l:l + 1, :].partition_broadcast(128)), writes=["gpost_bc"])
                for g in range(n_groups):
                    prompt_group(l, g)
            if do_sample:
                sample_group(l)
        P.emit()
    return nc, hc


_CACHE = {}


def kernel(**inp):
    if "nc" not in _CACHE:
        _CACHE["nc"] = build()
    nc, hc = _CACHE["nc"]
    f = lambda a: np.ascontiguousarray(np.asarray(a, dtype=np.float32))
    ck = f(inp["cache_k"]).reshape(DEPTH, NPOOL * PAGE, W)
    cv = f(inp["cache_v"]).reshape(DEPTH, NPOOL * PAGE, W)
    shared = {
        "cache_k": ck, "cache_v": cv,
        "g_pre": f(inp["g_pre"]), "g_post": f(inp["g_post"]), "w_in": f(inp["w_in"]), "ln_v_gain": f(inp["ln_v_gain"]),
        "w_spatial": f(inp["w_spatial"]), "b_spatial": f(inp["b_spatial"]), "conv_w": f(inp["conv_w"]), "g_mem": f(inp["g_mem"]),
        "w_mem_kv": f(inp["w_mem_kv"]), "w_merge": f(inp["w_merge"]), "b_merge": f(inp["b_merge"]),
        "w_branch": f(inp["w_branch"]).reshape(DEPTH, 4 * W, D), "w_out": f(inp["w_out"]),
    }
    for k, v in hc.items():
        shared["c_" + k] = v
    xpr = f(inp["x_prompt"])
    xsm = f(inp["x_sample"])
    in_maps = []
    for c in range(8):
        b0 = c * NSB
        m = dict(shared)
        m["xp"] = xpr[c % 4]
        m["xs"] = np.ascontiguousarray(xsm[b0:b0 + NSB].transpose(1, 0, 2).reshape(NS, D))
        m["cmk"] = f(inp["cache_mem_k"])[:, b0:b0 + NSB].reshape(DEPTH, NSB, 256, W)
        m["cmv"] = f(inp["cache_mem_v"])[:, b0:b0 + NSB].reshape(DEPTH, NSB, 256, W)
        m["sconv"] = f(inp["state_conv"])[:, b0:b0 + NSB]
        m["ptab"] = np.ascontiguousarray(np.asarray(inp["page_table"], dtype=np.int32)[b0:b0 + NSB]).reshape(1, NSB * NPAGES)
        m["memp"] = f(inp["mem_prompt"])[c % 4]
        in_maps.append(m)
    res = run_bass_kernel_spmd(nc, in_maps, core_ids=list(range(8)))
    R = res.results
    y_prompt = np.stack([R[c]["yp"] for c in range(4)])
    tm = lambda a: a.reshape(-1, 4, NSB, a.shape[-1])
    y_sample = np.concatenate([R[c]["ys"].reshape(4, NSB, D).transpose(1, 0, 2) for c in range(8)], axis=0)
    nkp = np.stack([R[c]["nkp"] for c in range(4)], axis=1).reshape(DEPTH, 4, SEQ, 8, 64)
    nvp = np.stack([R[c]["nvp"] for c in range(4)], axis=1).reshape(DEPTH, 4, SEQ, 8, 64)
    ncp = np.stack([R[c]["ncp"] for c in range(4)], axis=1)
    nmk = np.stack([R[c]["nmk"] for c in range(4)], axis=1).reshape(DEPTH, 4, 256, 4, 128)
    nmv = np.stack([R[c]["nmv"] for c in range(4)], axis=1).reshape(DEPTH, 4, 256, 4, 128)
    nks = np.concatenate([R[c]["nks"].reshape(DEPTH, 4, NSB, W).transpose(0, 2, 1, 3) for c in range(8)], axis=1).reshape(DEPTH, 128, 4, 8, 64)
    nvs = np.concatenate([R[c]["nvs"].reshape(DEPTH, 4, NSB, W).transpose(0, 2, 1, 3) for c in range(8)], axis=1).reshape(DEPTH, 128, 4, 8, 64)
    ncs = np.concatenate([R[c]["ncs"] for c in range(8)], axis=1)
    nvn = np.concatenate([R[c]["nvn"].reshape(DEPTH, 4, NSB, W).transpose(0, 2, 1, 3) for c in range(8)], axis=1)
    out = (y_prompt, y_sample, nkp, nvp, ncp, nmk, nmv, nks, nvs, ncs, nvn)
    return tuple(np.ascontiguousarray(o, dtype=np.float32) for o in out)
```
